# Optimizing a Trainium2 kernel written in Bass

```python
import jax
import jax.numpy as jnp
from jax import lax
import numpy as np

D_MODEL = 1024
BATCH = 2
SEQ = 8192
DEPTH = 1
DEC_BATCH = 16
DEC_SEQ = 4096
PAST_LEN = 128

ATTN_HEADS = 8
HEAD_DIM = 64
ATTN_WIDTH = ATTN_HEADS * HEAD_DIM
DILATED_PATTERNS = ((128, 1), (512, 4), (2048, 16))
ATTN_BLOCK = 128
SGU_WIDTH = D_MODEL - ATTN_WIDTH
SGU_GROUPS = 4
SGU_GROUP_DIM = SGU_WIDTH // SGU_GROUPS
SGU_CHUNK = 128
MIX_WIDTH = ATTN_WIDTH + SGU_WIDTH
IN_WIDTH = 3 * ATTN_WIDTH + 2 * SGU_WIDTH
D_FF = ((8 * D_MODEL // 3 + 255) // 256) * 256
FFN_RESID = 0.5
N_SUBLAYERS = 3
EPS = 1e-6
NEG = -1e30

kernel_name = 'hybrid_dilated_attn_sgu_encoder'


def _rmsnorm(x, g):
    xf = x.astype(jnp.float32)
    y = xf * lax.rsqrt(jnp.mean(xf * xf, axis=-1, keepdims=True) + EPS)
    return (y * g.astype(jnp.float32)).astype(x.dtype)


def _modulate(h, shift, scale):
    return h * (1 + scale[:, None, :]) + shift[:, None, :]


def _swiglu(h, w_gate, w_up, w_down):
    return (jax.nn.silu(h @ w_gate) * (h @ w_up)) @ w_down


def _alibi_slopes():
    return jnp.exp2(-8.0 * jnp.arange(1, ATTN_HEADS + 1, dtype=jnp.float32) / ATTN_HEADS)


def _dilated_window_attention(q, k, v, window, dil):
    b, s, h, e = q.shape
    n = window // (2 * dil)
    l = s // dil
    nb = -(-l // ATTN_BLOCK)
    lp = nb * ATTN_BLOCK
    kw = ATTN_BLOCK + 2 * n

    def by_residue(t):
        return t.reshape(b, l, dil, h, e).transpose(0, 2, 3, 1, 4)

    qs = jnp.pad(by_residue(q), ((0, 0), (0, 0), (0, 0), (0, lp - l), (0, 0)))
    kpad = ((0, 0), (0, 0), (0, 0), (n, lp - l + n), (0, 0))
    ks = jnp.pad(by_residue(k), kpad)
    vs = jnp.pad(by_residue(v), kpad)
    qb = qs.reshape(b, dil, h, nb, ATTN_BLOCK, e)
    kidx = jnp.arange(nb)[:, None] * ATTN_BLOCK + jnp.arange(kw)[None, :]
    kb = ks[:, :, :, kidx]
    vb = vs[:, :, :, kidx]
    scores = jnp.einsum('bdhnqe,bdhnke->bdhnqk', qb, kb, preferred_element_type=jnp.float32)
    rel = jnp.arange(kw)[None, :] - n - jnp.arange(ATTN_BLOCK)[:, None]
    kpos = jnp.arange(nb)[:, None, None] * ATTN_BLOCK + jnp.arange(kw)[None, None, :] - n
    valid = (jnp.abs(rel) <= n)[None] & (kpos >= 0) & (kpos < l)
    dist = (dil * jnp.abs(rel)).astype(jnp.float32)
    bias = -_alibi_slopes()[:, None, None] * dist[None]
    scores = jnp.where(valid, scores + bias[:, None], NEG)
    m = jnp.max(scores, axis=-1, keepdims=True)
    p = jnp.exp(scores - m)
    denom = jnp.sum(p, axis=-1)
    o = jnp.einsum('bdhnqk,bdhnke->bdhnqe', p, vb.astype(jnp.float32)) / denom[..., None]
    lse = m[..., 0] + jnp.log(denom)
    o = o.reshape(b, dil, h, lp, e)[:, :, :, :l].transpose(0, 3, 1, 2, 4).reshape(b, s, h, e)
    lse = lse.reshape(b, dil, h, lp)[..., :l].transpose(0, 3, 1, 2).reshape(b, s, h)
    return o, lse


def _spatial_gating(u, v, norm_g, w_s, b_s):
    b, s, _ = u.shape
    v = _rmsnorm(v, norm_g)
    vc = v.reshape(b, s // SGU_CHUNK, SGU_CHUNK, SGU_GROUPS, SGU_GROUP_DIM)
    mixed = jnp.einsum('gts,bnsgc->bntgc', w_s, vc) + b_s.T[:, :, None]
    return u * mixed.reshape(b, s, SGU_WIDTH)


def _token_mixing(h, w_in, sgu_norm, sgu_w, sgu_b, w_out):
    b, s, _ = h.shape
    z = h @ w_in
    q, k, v, zu, zv = jnp.split(
        z, [ATTN_WIDTH, 2 * ATTN_WIDTH, 3 * ATTN_WIDTH, 3 * ATTN_WIDTH + SGU_WIDTH], axis=-1)
    q = q.reshape(b, s, ATTN_HEADS, HEAD_DIM) * (HEAD_DIM ** -0.5)
    k = k.reshape(b, s, ATTN_HEADS, HEAD_DIM)
    v = v.reshape(b, s, ATTN_HEADS, HEAD_DIM)
    outs = []
    lses = []
    for window, dil in DILATED_PATTERNS:
        o, lse = _dilated_window_attention(q, k, v, window, dil)
        outs.append(o)
        lses.append(lse)
    weights = jax.nn.softmax(jnp.stack(lses), axis=0)
    attn = jnp.einsum('pbsh,pbshe->bshe', weights, jnp.stack(outs))
    attn = attn.reshape(b, s, ATTN_WIDTH).astype(h.dtype)
    gated = _spatial_gating(jax.nn.gelu(zu), jax.nn.gelu(zv), sgu_norm, sgu_w, sgu_b)
    return jnp.concatenate([attn, gated], axis=-1) @ w_out


def _encode(x, c, ada_w, ada_b, ffn1_norm, ffn1_w_gate, ffn1_w_up, ffn1_w_down,
            mix_norm, w_in, sgu_norm, sgu_w, sgu_b, w_out,
            ffn2_norm, ffn2_w_gate, ffn2_w_up, ffn2_w_down, final_norm):
    b = x.shape[0]
    for i in range(DEPTH):
        mod = (jax.nn.silu(c) @ ada_w[i] + ada_b[i]).reshape(b, N_SUBLAYERS, 3, D_MODEL)
        h = _modulate(_rmsnorm(x, ffn1_norm[i]), mod[:, 0, 0], mod[:, 0, 1])
        x = x + FFN_RESID * mod[:, 0, 2][:, None, :] * _swiglu(h, ffn1_w_gate[i], ffn1_w_up[i], ffn1_w_down[i])
        h = _modulate(_rmsnorm(x, mix_norm[i]), mod[:, 1, 0], mod[:, 1, 1])
        x = x + mod[:, 1, 2][:, None, :] * _token_mixing(h, w_in[i], sgu_norm[i], sgu_w[i], sgu_b[i], w_out[i])
        h = _modulate(_rmsnorm(x, ffn2_norm[i]), mod[:, 2, 0], mod[:, 2, 1])
        x = x + FFN_RESID * mod[:, 2, 2][:, None, :] * _swiglu(h, ffn2_w_gate[i], ffn2_w_up[i], ffn2_w_down[i])
    return _rmsnorm(x, final_norm)


def setup_inputs(seed: int = 0) -> dict:
    key = jax.random.key(seed)
    ks = jax.random.split(key, 24)

    def nrm(k, shape, scale):
        return jax.random.normal(k, shape, jnp.float32) * scale

    def gain(k, shape):
        return 1.0 + nrm(k, shape, 0.05)

    dinv = D_MODEL ** -0.5
    return {
        'x_prompt': nrm(ks[0], (BATCH, SEQ, D_MODEL), 1.0),
        'x_sample': nrm(ks[1], (DEC_BATCH, DEC_SEQ, D_MODEL), 1.0),
        'c_prompt': nrm(ks[2], (BATCH, D_MODEL), 1.0),
        'c_sample': nrm(ks[3], (DEC_BATCH, D_MODEL), 1.0),
        'ada_w': nrm(ks[4], (DEPTH, D_MODEL, N_SUBLAYERS * 3 * D_MODEL), 0.5 * dinv),
        'ada_b': nrm(ks[5], (DEPTH, N_SUBLAYERS * 3 * D_MODEL), 0.01),
        'ffn1_norm': gain(ks[6], (DEPTH, D_MODEL)),
        'ffn1_w_gate': nrm(ks[7], (DEPTH, D_MODEL, D_FF), dinv),
        'ffn1_w_up': nrm(ks[8], (DEPTH, D_MODEL, D_FF), dinv),
        'ffn1_w_down': nrm(ks[9], (DEPTH, D_FF, D_MODEL), D_FF ** -0.5),
        'mix_norm': gain(ks[10], (DEPTH, D_MODEL)),
        'w_in': nrm(ks[11], (DEPTH, D_MODEL, IN_WIDTH), dinv),
        'sgu_norm': gain(ks[12], (DEPTH, SGU_WIDTH)),
        'sgu_w': nrm(ks[13], (DEPTH, SGU_GROUPS, SGU_CHUNK, SGU_CHUNK), SGU_CHUNK ** -0.5),
        'sgu_b': gain(ks[14], (DEPTH, SGU_GROUPS, SGU_CHUNK)),
        'w_out': nrm(ks[15], (DEPTH, MIX_WIDTH, D_MODEL), MIX_WIDTH ** -0.5),
        'ffn2_norm': gain(ks[16], (DEPTH, D_MODEL)),
        'ffn2_w_gate': nrm(ks[17], (DEPTH, D_MODEL, D_FF), dinv),
        'ffn2_w_up': nrm(ks[18], (DEPTH, D_MODEL, D_FF), dinv),
        'ffn2_w_down': nrm(ks[19], (DEPTH, D_FF, D_MODEL), D_FF ** -0.5),
        'final_norm': gain(ks[20], (D_MODEL,)),
    }


def reference(x_prompt, x_sample, c_prompt, c_sample, ada_w, ada_b,
              ffn1_norm, ffn1_w_gate, ffn1_w_up, ffn1_w_down,
              mix_norm, w_in, sgu_norm, sgu_w, sgu_b, w_out,
              ffn2_norm, ffn2_w_gate, ffn2_w_up, ffn2_w_down, final_norm):
    y_prompt = _encode(x_prompt, c_prompt, ada_w, ada_b, ffn1_norm, ffn1_w_gate, ffn1_w_up, ffn1_w_down,
                       mix_norm, w_in, sgu_norm, sgu_w, sgu_b, w_out,
                       ffn2_norm, ffn2_w_gate, ffn2_w_up, ffn2_w_down, final_norm)
    y_sample = _encode(x_sample, c_sample, ada_w, ada_b, ffn1_norm, ffn1_w_gate, ffn1_w_up, ffn1_w_down,
                       mix_norm, w_in, sgu_norm, sgu_w, sgu_b, w_out,
                       ffn2_norm, ffn2_w_gate, ffn2_w_up, ffn2_w_down, final_norm)
    return (y_prompt, y_sample)
```

```python
import numpy as np
from contextlib import ExitStack
import concourse.bass as bass
import concourse.mybir as mybir
from concourse.bass_utils import run_bass_kernel_spmd

F32 = mybir.dt.float32
BF16 = mybir.dt.bfloat16
AF = mybir.ActivationFunctionType
ALU = mybir.AluOpType

D = 1024
KC = 8
DFF = 2816
FC = 22
S = 4096
NSEQ = 3
NTOK = NSEQ * S
T = 512
TB = 4
NQTOK = 2 * S + 2048
EPS = 1e-6
HEADS = 8
PATTERNS = ((128, 1), (512, 4), (2048, 16))
N_CORES = 8
ARENA_F32 = 53200

ENGINES = ("tensor", "vector", "scalar", "gpsimd", "sync")


class Sem:
    def __init__(self, h):
        self.h = h
        self.n = 0


class Ctx:
    def __init__(self, nc, es):
        self.nc = nc
        self.es = es
        self.q = {e: [] for e in ENGINES}
        self.nsem = 0
        self.arena = es.enter_context(nc.sbuf_tensor("arena", [128, ARENA_F32], F32))
        self.top = 0
        self.psum = [es.enter_context(nc.psum_tensor("ps%d" % i, [128, 512], F32)) for i in range(8)]

    def sem(self, name):
        h = self.es.enter_context(self.nc.semaphore(name))
        self.nsem += 1
        return Sem(h)

    def alloc(self, cols, dtype=F32):
        if dtype == BF16:
            ncol32 = (cols + 1) // 2
        else:
            ncol32 = cols
        a = self.top
        self.top += ncol32
        assert self.top <= ARENA_F32, ("SBUF arena overflow", self.top)
        ap = self.arena[:, a:a + ncol32]
        if dtype == BF16:
            ap = ap.bitcast(BF16)
        return ap

    def op(self, eng, fn, sig=None, k=None):
        if sig is None:
            self.q[eng].append(fn)
            return None
        if k is None:
            k = 16 if eng in ("sync", "gpsimd_dma") else 1
        sig.n += k
        h = sig.h
        self.q[eng].append(lambda e, fn=fn, h=h, k=k: fn(e).then_inc(h, k))
        return sig.n

    def dma(self, eng, out, in_, sig):
        sig.n += 16
        h = sig.h
        self.q[eng].append(lambda e, out=out, in_=in_, h=h: e.dma_start(out=out, in_=in_).then_inc(h, 16))
        return sig.n

    def wait(self, eng, sem, val):
        if val is None or val <= 0:
            return
        h = sem.h
        self.q[eng].append(lambda e, h=h, val=val: e.wait_ge(h, val))

    def emit_all(self):
        nc = self.nc
        with nc.Block() as blk:
            for name in ENGINES:
                ops = self.q[name]

                def body(e, ops=ops):
                    for f in ops:
                        f(e)
                getattr(blk, name)(body)


def v3(ap, a):
    return ap.rearrange("p (a b) -> p a b", a=a)


def build_program(debug=None):
    nc = bass.Bass("TRN2", target_bir_lowering=False)
    dt = nc.dram_tensor

    def din(name, shape, dtype=F32):
        return dt(name, list(shape), dtype, kind="ExternalInput").ap()

    x_d = din("x", [NTOK, D])
    cT_d = din("cT", [128, KC * NSEQ])
    pvec_d = din("pvec", [128, 104])
    valid_d = din("valid", [128, NTOK // 128])
    ident_d = din("ident", [128, 128])
    absrel_d = din("absrel", [128, 256])
    band_d = din("band", [128, 256])
    adaw_d = din("ada_w", [D, 9 * D])
    wg_d = [din("wg1", [D, DFF]), din("wg2", [D, DFF])]
    wu_d = [din("wu1", [D, DFF]), din("wu2", [D, DFF])]
    wd_d = [din("wd1", [DFF, D]), din("wd2", [DFF, D])]
    win_d = din("w_in", [D, 2560])
    wsT_d = din("wsT", [128, 512])
    sgub_d = din("sgu_b", [1, 512])
    sgun_d = din("sgu_norm", [1, 512])
    wout_d = din("w_out", [D, D])
    y_d = dt("y", [NQTOK, D], F32, kind="ExternalOutput").ap()

    skind = "ExternalOutput" if debug else "Internal"
    x1T_d = dt("x1T", [KC, 128, NTOK], F32, kind=skind).ap()
    h2T_d = dt("h2T", [KC, 128, NTOK], BF16, kind=skind).ap()
    qT_d = dt("qT", [4, 128, NTOK], BF16, kind=skind).ap()
    kT_d = dt("kT", [4, 128, NTOK], BF16, kind=skind).ap()
    V_d = dt("V", [4, NTOK, 192], BF16, kind=skind).ap()
    gT_d = dt("gT", [4, 128, NTOK], BF16, kind=skind).ap()
    aT_d = dt("aT", [4, 128, NTOK], BF16, kind=skind).ap()
    x2T_d = dt("x2T", [KC, 128, NTOK], F32, kind=skind).ap()
    h3T_d = dt("h3T", [KC, 128, NTOK], BF16, kind=skind).ap()

    stop_after = debug.get("stop_after", 99) if debug else 99

    with ExitStack() as es:
        cx = Ctx(nc, es)
        PS = cx.psum

        ident = cx.alloc(128)
        ones_bf = cx.alloc(128, BF16)
        pvec = cx.alloc(104)
        valid = cx.alloc(NTOK // 128)
        modT = cx.alloc(72 * 3)
        Amod = cx.alloc(3 * 8 * 3)
        Gmod = cx.alloc(3 * 8 * 3)
        Emask = cx.alloc(24 * 256, BF16)
        persist_top = cx.top

        modT3 = v3(modT, 72)
        Amod4 = Amod.rearrange("p (l k s) -> p l k s", l=3, k=8)
        Gmod4 = Gmod.rearrange("p (l k s) -> p l k s", l=3, k=8)
        Emask3 = v3(Emask, 24)

        def A_(l, kc, s):
            return Amod4[:, l, kc, s:s + 1]

        def B_(l, kc, s):
            return modT3[:, l * 24 + kc, s:s + 1]

        def G_(l, kc, s):
            return Gmod4[:, l, kc, s:s + 1]

        s_out = cx.sem("s_out")

        def barrier():
            for e in ENGINES:
                cx.wait(e, s_out, s_out.n)

        s_ld = cx.sem("p0_ld")
        s_c = cx.sem("p0_c")
        s_aw = cx.sem("p0_aw")
        s_awf = cx.sem("p0_awf")
        s_v = cx.sem("p0_v")
        s_a = cx.sem("p0_a")

        cT = cx.alloc(24)
        scT = cx.alloc(24)
        absrel = cx.alloc(256)
        band = cx.alloc(256)
        etmp = [cx.alloc(256), cx.alloc(256)]
        tmp24 = cx.alloc(24)
        awbuf = [cx.alloc(8 * 1024), cx.alloc(8 * 1024)]

        for dst, src in ((ident, ident_d), (pvec, pvec_d), (valid, valid_d), (cT, cT_d),
                         (absrel, absrel_d), (band, band_d)):
            cx.dma("sync", dst, src, s_ld)
        n_ld = s_ld.n
        cx.wait("scalar", s_ld, n_ld)
        cx.wait("vector", s_ld, n_ld)
        cx.op("vector", lambda e: e.memset(ones_bf, 1.0))
        v_sc = cx.op("scalar", lambda e: e.activation(out=scT, in_=cT, func=AF.Silu), s_c)
        cx.wait("tensor", s_c, v_sc)
        adaw_v = adaw_d.rearrange("(kc p) f -> p kc f", p=128)
        for ch in range(9):
            b = ch % 2
            if ch >= 2:
                cx.wait("sync", s_awf, ch - 1)
            vld = cx.dma("sync", v3(awbuf[b], 8), adaw_v[:, :, ch * 1024:(ch + 1) * 1024], s_aw)
            cx.wait("tensor", s_aw, vld)
            for fl in range(8):
                col = (ch * 8 + fl) * 3
                for kc in range(KC):
                    fn = (lambda e, b=b, fl=fl, kc=kc, col=col: e.matmul(
                        PS[0][:, col:col + 3], v3(awbuf[b], 8)[:, kc, fl * 128:(fl + 1) * 128],
                        scT[:, kc * 3:(kc + 1) * 3], start=(kc == 0), stop=(kc == KC - 1)))
                    if fl == 7 and kc == KC - 1:
                        cx.op("tensor", fn, s_awf)
                    else:
                        cx.op("tensor", fn)
        cx.wait("vector", s_awf, 9)
        psm3 = PS[0][:, 0:216].rearrange("p (a b) -> p a b", b=3)
        for s in range(3):
            vm = cx.op("vector", lambda e, s=s: e.tensor_tensor(modT3[:, :, s], psm3[:, :, s], pvec[:, 0:72], ALU.add), s_v)
        cx.wait("vector", s_v, vm)
        for l in range(3):
            gl = pvec[:, 72 + 8 * l:80 + 8 * l]
            for s in range(3):
                v1 = cx.op("vector", lambda e, l=l, s=s: e.tensor_scalar(
                    tmp24[:, 0:8], modT3[:, l * 24 + 8:l * 24 + 16, s], 1.0, None, ALU.add), s_v)
                cx.wait("vector", s_v, v1)
                v2 = cx.op("vector", lambda e, l=l, s=s, gl=gl: e.tensor_tensor(
                    Amod4[:, l, :, s], tmp24[:, 0:8], gl, ALU.mult), s_v)
                cx.wait("vector", s_v, v2)
                rw = 1.0 if l == 1 else 0.5
                cx.op("vector", lambda e, l=l, s=s, rw=rw: e.tensor_scalar(
                    Gmod4[:, l, :, s], modT3[:, l * 24 + 16:l * 24 + 24, s], rw, None, ALU.mult), s_v)
        i = 0
        for h in range(HEADS):
            slope = 2.0 ** (-8.0 * (h + 1) / HEADS)
            for pi, (_, dil) in enumerate(PATTERNS):
                b = i % 2
                if i >= 2:
                    cx.wait("scalar", s_v, vmask[b])
                va = cx.op("scalar", lambda e, b=b, sc=-slope * dil: e.activation(
                    out=etmp[b], in_=absrel, func=AF.Exp, scale=sc), s_a)
                cx.wait("vector", s_a, va)
                vv = cx.op("vector", lambda e, b=b, idx=h * 3 + pi: e.tensor_tensor(
                    Emask3[:, idx, :], etmp[b], band, ALU.mult), s_v)
                if i == 0:
                    vmask = [0, 0]
                vmask[b] = vv
                i += 1
        v_p0 = s_v.n
        for e in ENGINES:
            cx.wait(e, s_v, v_p0)
            cx.wait(e, s_a, s_a.n)

        def ffn_phase(mode):
            cx.top = persist_top
            pre = "f%d_" % mode
            s_w = cx.sem(pre + "w")
            s_xin = cx.sem(pre + "xin")
            s_trp = cx.sem(pre + "trp")
            s_cv = cx.sem(pre + "cv")
            s_ca = cx.sem(pre + "ca")
            s_st = cx.sem(pre + "st")
            s_sq = cx.sem(pre + "sq")
            s_rs = cx.sem(pre + "rs")
            s_tA = cx.sem(pre + "tA")
            s_hT = cx.sem(pre + "hT")
            s_g = cx.sem(pre + "g")
            s_u = cx.sem(pre + "u")
            s_sg = cx.sem(pre + "sg")
            s_hid = cx.sem(pre + "hid")
            s_dn = cx.sem(pre + "dn")
            s_res = cx.sem(pre + "res")
            s_yt = cx.sem(pre + "yt")
            s_ye = cx.sem(pre + "ye")

            Wg = v3(cx.alloc(KC * DFF, BF16), KC)
            Wu = v3(cx.alloc(KC * DFF, BF16), KC)
            Wd = v3(cx.alloc(FC * D, BF16), FC)
            xT = v3(cx.alloc(KC * T), KC)
            hT = v3(cx.alloc(KC * T, BF16), KC)
            hid_raw32 = cx.alloc(FC * T // 2)
            hid = v3(hid_raw32.bitcast(BF16), FC)
            xtok = hid_raw32[:, 0:TB * D].rearrange("p (a b) -> p a b", a=TB)
            xsq32 = cx.alloc(KC * T // 2)
            xsq = v3(xsq32.bitcast(BF16), KC)
            rstd = cx.alloc(T)
            sqt = cx.alloc(T)
            tmpA = [xsq32[:, 0:T], xsq32[:, T:2 * T]]
            sg = [xsq32[:, 2 * T:3 * T], xsq32[:, 3 * T:4 * T]]

            wgv = wg_d[mode].rearrange("(kc p) f -> p kc f", p=128)
            wuv = wu_d[mode].rearrange("(kc p) f -> p kc f", p=128)
            wdv = wd_d[mode].rearrange("(j p) d -> p j d", p=128)
            for kc in range(KC):
                cx.dma("gpsimd", Wg[:, kc, :], wgv[:, kc, :], s_w)
                cx.dma("gpsimd", Wu[:, kc, :], wuv[:, kc, :], s_w)
            for j0 in range(0, FC, 2):
                cx.dma("gpsimd", Wd[:, j0:j0 + 2, :], wdv[:, j0:j0 + 2, :], s_w)
            cx.wait("tensor", s_w, s_w.n)

            l_in = 0 if mode == 0 else 2

            if mode == 0:
                tiles = [(s, ti) for s in range(NSEQ) for ti in range(S // T)]
            else:
                tiles = [(s, ti) for s in range(NSEQ) for ti in range(S // T) if s < 2 or 2 <= ti < 6]

            sg_hist = []
            hid_hist = []
            dstage = debug.get("stage", 9) if debug else 9
            if debug and "ntiles" in debug:
                tiles = tiles[:debug["ntiles"]]
            bank_i = [0]
            tr_hist = []
            gu_i = [0]
            out_row = 0
            last_store = {"x": 0, "h": 0}
            prev_dn = 0

            def ring_acquire(consumers):
                i = bank_i[0]
                if i >= 2:
                    for sem, val in tr_hist[i - 2]:
                        cx.wait("tensor", sem, val)
                bank_i[0] += 1
                tr_hist.append(None)
                return PS[i % 2], i

            def norm_to_hT(l, s, square_src_ready):
                cx.wait("tensor", s_sq, s_sq.n)
                for kc in range(KC):
                    cx.wait("tensor", s_ca, square_src_ready[kc])
                    fn = lambda e, kc=kc: e.matmul(PS[2][:, :], ones_bf, xsq[:, kc, :], start=(kc == 0), stop=(kc == KC - 1))
                    if kc == KC - 1:
                        vst = cx.op("tensor", fn, s_st)
                    else:
                        cx.op("tensor", fn)
                cx.wait("scalar", s_st, vst)
                vsq = cx.op("scalar", lambda e: e.activation(out=sqt, in_=PS[2][:, :], func=AF.Sqrt, bias=EPS_AP[0], scale=1.0 / D), s_sq)
                cx.wait("vector", s_sq, vsq)
                vrs = cx.op("vector", lambda e: e.reciprocal(rstd, sqt), s_rs)
                cx.wait("vector", s_rs, vrs)
                vh = []
                for kc in range(KC):
                    b = kc % 2
                    if len(hT_hist) >= 2:
                        cx.wait("vector", s_hT, hT_hist[-2])
                    if l is None:
                        vt = cx.op("vector", lambda e, kc=kc: e.scalar_tensor_tensor(
                            xT[:, kc, :], xT[:, kc, :], pvec[:, 96 + kc:97 + kc], rstd, ALU.mult, ALU.mult), s_hT)
                        hT_hist.append(vt)
                        vh.append(vt)
                    else:
                        vt = cx.op("vector", lambda e, kc=kc, b=b: e.scalar_tensor_tensor(
                            tmpA[b], xT[:, kc, :], A_(l, kc, s), rstd, ALU.mult, ALU.mult), s_tA)
                        cx.wait("scalar", s_tA, vt)
                        va = cx.op("scalar", lambda e, kc=kc, b=b: e.activation(
                            out=hT[:, kc, :], in_=tmpA[b], func=AF.Identity, bias=B_(l, kc, s), scale=1.0), s_hT)
                        hT_hist.append(va)
                        vh.append(va)
                return vh

            hT_hist = []
            EPS_AP = [None]
            eps_t = cx.alloc(1)
            cx.op("vector", lambda e: e.memset(eps_t, EPS))
            EPS_AP[0] = eps_t[:, 0:1]

            for (s, ti) in tiles:
                g0 = s * S + ti * T
                if mode == 0:
                    cx.wait("sync", s_dn, prev_dn)
                    vx = cx.dma("sync", xtok, x_d[g0:g0 + T, :].rearrange("(tb p) d -> p tb d", p=128), s_xin)
                    cx.wait("tensor", s_xin, vx)
                    cx.wait("vector", s_out, last_store["x"])
                    cx.wait("scalar", s_out, last_store["h"])
                    ca_vals = []
                    for kc in range(KC if dstage >= 1 else 0):
                        bank, bi = ring_acquire(2)
                        for tb in range(TB):
                            fn = lambda e, bank=bank, tb=tb, kc=kc: e.transpose(
                                bank[:, tb * 128:(tb + 1) * 128], xtok[:, tb, kc * 128:(kc + 1) * 128], ident)
                            if tb == TB - 1:
                                vtr = cx.op("tensor", fn, s_trp)
                            else:
                                cx.op("tensor", fn)
                        cx.wait("vector", s_trp, vtr)
                        vcv = cx.op("vector", lambda e, bank=bank, kc=kc: e.tensor_copy(xT[:, kc, :], bank[:, :]), s_cv)
                        cx.wait("scalar", s_cv, vcv)
                        vca = cx.op("scalar", lambda e, kc=kc: e.activation(
                            out=xsq[:, kc, :], in_=xT[:, kc, :], func=AF.Square), s_ca)
                        tr_hist[bi] = [(s_cv, vcv)]
                        ca_vals.append(vca)
                    if dstage < 2:
                        continue
                    vh = norm_to_hT(0, s, ca_vals)
                    if dstage < 3:
                        continue
                else:
                    cx.wait("sync", s_out, max(last_store["x"], last_store["h"]))
                    cx.wait("sync", s_res, s_res.n)
                    cx.wait("sync", s_yt, s_yt.n)
                    cx.wait("sync", s_u, s_u.n)
                    vx1 = cx.dma("sync", xT, x2T_d[:, :, g0:g0 + T].rearrange("k p n -> p k n"), s_xin)
                    vx2 = cx.dma("sync", hT, h3T_d[:, :, g0:g0 + T].rearrange("k p n -> p k n"), s_xin)
                    cx.wait("tensor", s_xin, vx2)
                    cx.wait("vector", s_xin, vx2)
                    vh = [0] * KC

                for j in range(FC):
                    gi = gu_i[0]
                    gu_i[0] += 1
                    pg = PS[3 + gi % 2]
                    pu = PS[5 + gi % 2]
                    if gi >= 2:
                        cx.wait("tensor", s_sg, sg_hist[gi - 2])
                    for kc in range(KC):
                        if j == 0:
                            cx.wait("tensor", s_hT, vh[kc])
                        fn = lambda e, pg=pg, kc=kc, j=j: e.matmul(
                            pg[:, :], Wg[:, kc, j * 128:(j + 1) * 128], hT[:, kc, :], start=(kc == 0), stop=(kc == KC - 1))
                        if kc == KC - 1:
                            vg = cx.op("tensor", fn, s_g)
                        else:
                            cx.op("tensor", fn)
                    if gi >= 2:
                        cx.wait("tensor", s_hid, hid_hist[gi - 2])
                    for kc in range(KC):
                        fn = lambda e, pu=pu, kc=kc, j=j: e.matmul(
                            pu[:, :], Wu[:, kc, j * 128:(j + 1) * 128], hT[:, kc, :], start=(kc == 0), stop=(kc == KC - 1))
                        if kc == KC - 1:
                            vu = cx.op("tensor", fn, s_u)
                        else:
                            cx.op("tensor", fn)
                    b = gi % 2
                    cx.wait("scalar", s_g, vg)
                    if gi >= 2:
                        cx.wait("scalar", s_hid, hid_hist[gi - 2])
                    vsg = cx.op("scalar", lambda e, pg=pg, b=b: e.activation(out=sg[b], in_=pg[:, :], func=AF.Silu), s_sg)
                    sg_hist.append(vsg)
                    cx.wait("vector", s_sg, vsg)
                    cx.wait("vector", s_u, vu)
                    vhid = cx.op("vector", lambda e, pu=pu, b=b, j=j: e.tensor_tensor(hid[:, j, :], pu[:, :], sg[b], ALU.mult), s_hid)
                    hid_hist.append(vhid)

                if dstage < 4:
                    continue
                res_vals = []
                for dc in range(KC):
                    bank, bi = ring_acquire(1)
                    for j in range(FC):
                        if dc == 0:
                            cx.wait("tensor", s_hid, hid_hist[len(hid_hist) - FC + j])
                        fn = lambda e, bank=bank, j=j, dc=dc: e.matmul(
                            bank[:, :], Wd[:, j, dc * 128:(dc + 1) * 128], hid[:, j, :], start=(j == 0), stop=(j == FC - 1))
                        if j == FC - 1:
                            vdn = cx.op("tensor", fn, s_dn)
                        else:
                            cx.op("tensor", fn)
                    cx.wait("vector", s_dn, vdn)
                    vres = cx.op("vector", lambda e, bank=bank, dc=dc, s=s: e.scalar_tensor_tensor(
                        xT[:, dc, :], bank[:, :], G_(l_in, dc, s), xT[:, dc, :], ALU.mult, ALU.add), s_res)
                    tr_hist[bi] = [(s_res, vres)]
                    res_vals.append(vres)
                prev_dn = vdn

                if dstage < 5:
                    continue
                ca_vals = []
                for kc in range(KC):
                    cx.wait("scalar", s_res, res_vals[kc])
                    vca = cx.op("scalar", lambda e, kc=kc: e.activation(out=xsq[:, kc, :], in_=xT[:, kc, :], func=AF.Square), s_ca)
                    ca_vals.append(vca)
                if mode == 0:
                    cx.wait("gpsimd", s_res, res_vals[-1])
                    last_store["x"] = cx.dma("gpsimd", x1T_d[:, :, g0:g0 + T].rearrange("k p n -> p k n"), xT, s_out)
                    vh2 = norm_to_hT(1, s, ca_vals)
                    cx.wait("gpsimd", s_hT, vh2[-1])
                    last_store["h"] = cx.dma("gpsimd", h2T_d[:, :, g0:g0 + T].rearrange("k p n -> p k n"), hT, s_out)
                else:
                    vy = norm_to_hT(None, s, ca_vals)
                    ytok = xtok
                    for tb in range(TB):
                        for half in range(2):
                            bank, bi = ring_acquire(1)
                            for k4 in range(4):
                                kc = half * 4 + k4
                                if tb == 0:
                                    cx.wait("tensor", s_hT, vy[kc])
                                fn = lambda e, bank=bank, k4=k4, kc=kc, tb=tb: e.transpose(
                                    bank[:, k4 * 128:(k4 + 1) * 128], xT[:, kc, tb * 128:(tb + 1) * 128], ident)
                                if k4 == 3:
                                    vtr = cx.op("tensor", fn, s_yt)
                                else:
                                    cx.op("tensor", fn)
                            cx.wait("vector", s_yt, vtr)
                            vye = cx.op("vector", lambda e, bank=bank, tb=tb, half=half: e.tensor_copy(
                                ytok[:, tb, half * 512:(half + 1) * 512], bank[:, :]), s_ye)
                            tr_hist[bi] = [(s_ye, vye)]
                    cx.wait("gpsimd", s_ye, vye)
                    last_store["x"] = cx.dma("gpsimd", y_d[out_row:out_row + T, :].rearrange("(tb p) d -> p tb d", p=128), ytok, s_out)
                    last_store["h"] = last_store["x"]
                    out_row += T
                    cx.wait("vector", s_out, last_store["x"])
            barrier()

        if debug:
            dbg_d = dt("dbg", [128, persist_top], F32, kind="ExternalOutput").ap()
            cx.dma("gpsimd", dbg_d, cx.arena[:, 0:persist_top], s_out)
            barrier()

        class Ring:
            def __init__(self, banks):
                self.banks = banks
                self.hist = []

            def acquire(self):
                i = len(self.hist)
                n = len(self.banks)
                if i >= n:
                    for sem, val in self.hist[i - n]:
                        cx.wait("tensor", sem, val)
                self.hist.append([])
                return self.banks[i % n], i

            def release(self, i, sem, val):
                self.hist[i].append((sem, val))

        def mix_in_phase():
            cx.top = persist_top
            s_w = cx.sem("m_w")
            s_in = cx.sem("m_in")
            s_pe = cx.sem("m_pe")
            s_a = cx.sem("m_a")
            s_v = cx.sem("m_v")
            Win = v3(cx.alloc(KC * 2560, BF16), KC)
            wsT = v3(cx.alloc(512, BF16), 4)
            bs_row = cx.alloc(512, BF16)
            row32 = cx.alloc(512)
            ones32 = cx.alloc(256)
            sgn_bc = cx.alloc(512)
            eps_t = cx.alloc(1)
            h2 = [v3(cx.alloc(KC * T, BF16), KC), v3(cx.alloc(KC * T, BF16), KC)]
            qk_sb = v3(cx.alloc(8 * T, BF16), 8)
            V_sb = cx.alloc(TB * 4 * 192, BF16).rearrange("p (t h c) -> p t h c", t=TB, h=4)
            vn = v3(cx.alloc(TB * 512, BF16), TB)
            gv = [cx.alloc(512), cx.alloc(512)]
            sqv = cx.alloc(512)
            uT = v3(cx.alloc(4 * T), 4)
            gT_sb = v3(cx.alloc(4 * T, BF16), 4)
            ss = cx.alloc(8)
            sq1 = cx.alloc(8)
            rs = cx.alloc(8)

            winv = win_d.rearrange("(kc p) f -> p kc f", p=128)
            for kc in range(KC):
                cx.dma("gpsimd", Win[:, kc, :], winv[:, kc, :], s_w)
            cx.dma("gpsimd", wsT, wsT_d.rearrange("p (g t) -> p g t", g=4), s_w)
            cx.dma("gpsimd", bs_row[0:1, :], sgub_d, s_w)
            cx.dma("gpsimd", row32[0:1, :], sgun_d, s_w)
            cx.wait("vector", s_w, s_w.n)
            cx.wait("tensor", s_w, s_w.n)
            cx.op("vector", lambda e: e.memset(ones32, 1.0))
            cx.op("vector", lambda e: e.memset(eps_t, EPS))
            vo = cx.op("vector", lambda e: e.memset(sqv, 0.0), s_v)
            cx.wait("tensor", s_v, vo)
            vb = cx.op("tensor", lambda e: e.matmul(PS[7][:, :], ones32[0:1, 0:128], row32[0:1, :], start=True, stop=True), s_pe)
            cx.wait("vector", s_pe, vb)
            vo = cx.op("vector", lambda e: e.tensor_copy(sgn_bc, PS[7][:, :]), s_v)

            ring = Ring([PS[i] for i in range(6)])
            tiles = [(s, ti) for s in range(NSEQ) for ti in range(S // T)]
            if debug and "ntiles" in debug:
                tiles = tiles[:debug["ntiles"]]
            pe_tile_end = []
            last_stores = 0
            for it, (s, ti) in enumerate(tiles):
                g0 = s * S + ti * T
                halo = (s == 2 and not (2 <= ti < 6))
                hb_ = h2[it % 2]
                if it >= 2:
                    cx.wait("sync", s_pe, pe_tile_end[it - 2])
                vin = cx.dma("sync", hb_, h2T_d[:, :, g0:g0 + T].rearrange("k p n -> p k n"), s_in)
                cx.wait("tensor", s_in, vin)
                cx.wait("scalar", s_out, last_stores)
                cx.wait("vector", s_out, last_stores)
                va_last = 0
                for fcn in (range(4, 8) if halo else range(8)):
                    bank, bi = ring.acquire()
                    for kc in range(KC):
                        fn = lambda e, bank=bank, kc=kc, fcn=fcn, hb_=hb_: e.matmul(
                            bank[:, :], Win[:, kc, fcn * 128:(fcn + 1) * 128], hb_[:, kc, :], start=(kc == 0), stop=(kc == KC - 1))
                        if kc == KC - 1:
                            vp = cx.op("tensor", fn, s_pe)
                        else:
                            cx.op("tensor", fn)
                    cx.wait("scalar", s_pe, vp)
                    va_last = cx.op("scalar", lambda e, bank=bank, fcn=fcn: e.activation(
                        out=qk_sb[:, fcn, :], in_=bank[:, :], func=AF.Copy, scale=(0.125 if fcn < 4 else 1.0)), s_a)
                    ring.release(bi, s_a, va_last)
                cx.wait("gpsimd", s_a, va_last)
                if not halo:
                    cx.dma("gpsimd", qT_d[:, :, g0:g0 + T].rearrange("k p n -> p k n"), qk_sb[:, 0:4, :], s_out)
                cx.dma("gpsimd", kT_d[:, :, g0:g0 + T].rearrange("k p n -> p k n"), qk_sb[:, 4:8, :], s_out)
                for tb in range(TB):
                    blk = g0 // 128 + tb
                    bank, bi = ring.acquire()
                    for kc in range(KC):
                        fn = lambda e, bank=bank, kc=kc, tb=tb, hb_=hb_: e.matmul(
                            bank[:, :], hb_[:, kc, tb * 128:(tb + 1) * 128], Win[:, kc, 1024:1536], start=(kc == 0), stop=(kc == KC - 1))
                        if kc == KC - 1:
                            vp = cx.op("tensor", fn, s_pe)
                        else:
                            cx.op("tensor", fn)
                    cx.wait("vector", s_pe, vp)
                    bv = bank[:, :].rearrange("p (h c) -> p h c", h=4)
                    cx.op("vector", lambda e, bv=bv, tb=tb, blk=blk: e.tensor_scalar(
                        V_sb[:, tb, :, 0:64], bv[:, :, 0:64], valid[:, blk:blk + 1], None, ALU.mult))
                    cx.op("vector", lambda e, bv=bv, tb=tb, blk=blk: e.tensor_scalar(
                        V_sb[:, tb, :, 128:192], bv[:, :, 64:128], valid[:, blk:blk + 1], None, ALU.mult))
                    vv = cx.op("vector", lambda e, tb=tb, blk=blk: e.tensor_scalar(
                        V_sb[:, tb, :, 64:128], ones32.rearrange("p (h c) -> p h c", h=4), valid[:, blk:blk + 1], None, ALU.mult), s_v)
                    ring.release(bi, s_v, vv)
                    cx.wait("gpsimd", s_v, vv)
                    cx.dma("gpsimd", V_d[:, g0 + tb * 128:g0 + (tb + 1) * 128, :].rearrange("h p c -> p h c"), V_sb[:, tb], s_out)
                if not halo:
                    vvn = []
                    for tb in range(TB):
                        b = tb % 2
                        bank, bi = ring.acquire()
                        for kc in range(KC):
                            fn = lambda e, bank=bank, kc=kc, tb=tb, hb_=hb_: e.matmul(
                                bank[:, :], hb_[:, kc, tb * 128:(tb + 1) * 128], Win[:, kc, 2048:2560], start=(kc == 0), stop=(kc == KC - 1))
                            if kc == KC - 1:
                                vp = cx.op("tensor", fn, s_pe)
                            else:
                                cx.op("tensor", fn)
                        cx.wait("scalar", s_pe, vp)
                        if tb >= 2:
                            cx.wait("scalar", s_v, vvn[tb - 2])
                        va = cx.op("scalar", lambda e, bank=bank, b=b: e.activation(out=gv[b], in_=bank[:, :], func=AF.Gelu_apprx_tanh), s_a)
                        ring.release(bi, s_a, va)
                        cx.wait("vector", s_a, va)
                        v1 = cx.op("vector", lambda e, b=b: e.tensor_tensor(sqv, gv[b], gv[b], ALU.mult), s_v)
                        cx.wait("vector", s_v, v1)
                        v2 = cx.op("vector", lambda e, tb=tb: e.reduce_sum(ss[:, tb:tb + 1], sqv, mybir.AxisListType.X), s_v)
                        cx.wait("scalar", s_v, v2)
                        va2 = cx.op("scalar", lambda e, tb=tb: e.activation(
                            out=sq1[:, tb:tb + 1], in_=ss[:, tb:tb + 1], func=AF.Sqrt, bias=eps_t[:, 0:1], scale=1.0 / 512), s_a)
                        cx.wait("vector", s_a, va2)
                        v3_ = cx.op("vector", lambda e, tb=tb: e.reciprocal(rs[:, tb:tb + 1], sq1[:, tb:tb + 1]), s_v)
                        cx.wait("vector", s_v, v3_)
                        v4 = cx.op("vector", lambda e, tb=tb, b=b: e.scalar_tensor_tensor(
                            vn[:, tb, :], gv[b], rs[:, tb:tb + 1], sgn_bc, ALU.mult, ALU.mult), s_v)
                        vvn.append(v4)
                    vu = []
                    for g in range(4):
                        bank, bi = ring.acquire()
                        for kc in range(KC):
                            fn = lambda e, bank=bank, kc=kc, g=g, hb_=hb_: e.matmul(
                                bank[:, :], Win[:, kc, 1536 + g * 128:1536 + (g + 1) * 128], hb_[:, kc, :], start=(kc == 0), stop=(kc == KC - 1))
                            if kc == KC - 1:
                                vp = cx.op("tensor", fn, s_pe)
                            else:
                                cx.op("tensor", fn)
                        cx.wait("scalar", s_pe, vp)
                        va = cx.op("scalar", lambda e, bank=bank, g=g: e.activation(out=uT[:, g, :], in_=bank[:, :], func=AF.Gelu_apprx_tanh), s_a)
                        ring.release(bi, s_a, va)
                        vu.append(va)
                    for g in range(4):
                        bank, bi = ring.acquire()
                        for tb in range(TB):
                            if g == 0:
                                cx.wait("tensor", s_v, vvn[tb])
                            cx.op("tensor", lambda e, bank=bank, g=g, tb=tb: e.matmul(
                                bank[:, tb * 128:(tb + 1) * 128], vn[:, tb, g * 128:(g + 1) * 128], wsT[:, g, :], start=True, stop=False))
                            fn = lambda e, bank=bank, g=g, tb=tb: e.matmul(
                                bank[:, tb * 128:(tb + 1) * 128], ones_bf[0:1, 0:128], bs_row[0:1, g * 128:(g + 1) * 128], start=False, stop=True)
                            if tb == TB - 1:
                                vp = cx.op("tensor", fn, s_pe)
                            else:
                                cx.op("tensor", fn)
                        cx.wait("vector", s_pe, vp)
                        cx.wait("vector", s_a, vu[g])
                        vg = cx.op("vector", lambda e, bank=bank, g=g: e.tensor_tensor(gT_sb[:, g, :], bank[:, :], uT[:, g, :], ALU.mult), s_v)
                        ring.release(bi, s_v, vg)
                    cx.wait("gpsimd", s_v, vg)
                    cx.dma("gpsimd", gT_d[:, :, g0:g0 + T].rearrange("k p n -> p k n"), gT_sb, s_out)
                pe_tile_end.append(s_pe.n)
                last_stores = s_out.n
            barrier()

        def segments(Q0, n, L):
            nb = L // 128
            units = list(range(Q0 // 64, (Q0 + n) // 64))
            segs = []
            i = 0
            while i < len(units):
                u = units[i]
                if u % 2 == 1 and i + 1 < len(units) and (u + 1) * 64 < L:
                    P, nq = u * 64, 128
                    i += 2
                else:
                    P, nq = u * 64, 64
                    i += 1
                kbs = []
                mc_ = None
                for kb in range(max(0, (P - 64) // 128), min(nb - 1, (P + nq + 63) // 128) + 1):
                    v = P - 64 - 128 * kb
                    if v not in (-128, -64, 0, 64):
                        continue
                    hh = 1 if v < 0 else 0
                    mc = v + 128 * hh
                    if nq == 128 and mc != 0:
                        continue
                    assert mc_ is None or mc_ == mc
                    mc_ = mc
                    kbs.append((kb, hh))
                segs.append((P, nq, kbs, mc_))
            return segs

        def attn_phase():
            cx.top = persist_top
            s_ld = cx.sem("a_ld")
            s_vl = cx.sem("a_vl")
            s_ps = cx.sem("a_ps")
            s_ex = cx.sem("a_ex")
            s_mk = cx.sem("a_mk")
            s_po = cx.sem("a_po")
            s_ac = cx.sem("a_ac")
            s_nm = cx.sem("a_nm")
            qT = v3(cx.alloc(4 * S, BF16), 4)
            kT = v3(cx.alloc(4 * S, BF16), 4)
            Vt = [v3(cx.alloc(48 * 192, BF16), 48), v3(cx.alloc(48 * 192, BF16), 48)]
            acc = [cx.alloc(2048), cx.alloc(2048)]
            rD = cx.alloc(2048)
            attnT = [cx.alloc(2048, BF16), cx.alloc(2048, BF16)]
            NPB = 4
            pT = [cx.alloc(256, BF16) for _ in range(NPB)]
            pTm = [cx.alloc(256, BF16) for _ in range(NPB)]
            ringS = Ring([PS[0], PS[1], PS[2]])
            ringO = Ring([PS[3], PS[4], PS[5]])
            seg_i = 0
            ex_hist = []
            mk_hist = []
            po_hist = []
            vl_i = 0
            vt_free = [0, 0]
            at_i = 0
            at_store = [0, 0]
            pe_done_seq = 0
            sts = [(0, 0), (0, 2048), (1, 0), (1, 2048), (2, 1024)]
            if debug and "nst" in debug:
                sts = sts[:debug["nst"]]
            cur_seq = -1
            for (s, q0) in sts:
                if s != cur_seq:
                    cur_seq = s
                    cx.wait("sync", s_po, pe_done_seq)
                    cx.wait("sync", s_ps, s_ps.n)
                    cx.dma("sync", qT, qT_d[:, :, s * S:(s + 1) * S].rearrange("k p n -> p k n"), s_ld)
                    vq = cx.dma("sync", kT, kT_d[:, :, s * S:(s + 1) * S].rearrange("k p n -> p k n"), s_ld)
                    cx.wait("tensor", s_ld, vq)
                for hp in range(4):
                    for pi, (_, dil) in enumerate(PATTERNS):
                        L = S // dil
                        Q0, n = q0 // dil, 2048 // dil
                        kb_lo = max(0, (Q0 - 64) // 128)
                        kb_hi = min(L // 128 - 1, (Q0 + n + 63) // 128)
                        nkb = kb_hi - kb_lo + 1
                        vb = vl_i % 2
                        vl_i += 1
                        cx.wait("sync", s_po, vt_free[vb])
                        for r in range(dil):
                            t0 = s * S + r + dil * 128 * kb_lo
                            src = V_d[hp, t0:t0 + dil * (128 * nkb - 1) + 1:dil, :].rearrange("(kb p) c -> p kb c", p=128)
                            vvl = cx.dma("sync", Vt[vb][:, r * nkb:(r + 1) * nkb, :], src, s_vl)
                        cx.wait("tensor", s_vl, vvl)
                        for hd in range(2):
                            h = hp * 2 + hd
                            hb = 64 * hd
                            vcols = slice(0, 128) if hd == 0 else slice(64, 192)
                            midx = h * 3 + pi
                            for r in range(dil):
                                for (P, nq, kbs, mc) in segments(Q0, n, L):
                                    nk = len(kbs)
                                    pb = seg_i % NPB
                                    bankS, bsi = ringS.acquire()
                                    qcols = slice(r + dil * P, r + dil * (P + nq - 1) + 1, dil)
                                    for i, (kb, hh) in enumerate(kbs):
                                        kcols = slice(r + dil * 128 * kb, r + dil * (128 * kb + 127) + 1, dil)
                                        fn = lambda e, bankS=bankS, i=i, nq=nq, kcols=kcols, qcols=qcols, hb=hb, hp=hp: e.matmul(
                                            bankS[:, i * nq:(i + 1) * nq], kT[hb:hb + 64, hp, kcols], qT[hb:hb + 64, hp, qcols], start=True, stop=True)
                                        if i == nk - 1:
                                            vps = cx.op("tensor", fn, s_ps)
                                        else:
                                            cx.op("tensor", fn)
                                    cx.wait("scalar", s_ps, vps)
                                    if seg_i >= NPB:
                                        cx.wait("scalar", s_mk, mk_hist[seg_i - NPB])
                                    vex = cx.op("scalar", lambda e, bankS=bankS, pb=pb, w=nk * nq: e.activation(
                                        out=pT[pb][:, 0:w], in_=bankS[:, 0:w], func=AF.Exp), s_ex)
                                    ringS.release(bsi, s_ex, vex)
                                    cx.wait("vector", s_ex, vex)
                                    if seg_i >= NPB:
                                        cx.wait("vector", s_po, po_hist[seg_i - NPB])
                                    if nk == 2:
                                        m_ap = Emask3[:, midx, :].rearrange("p (h q) -> p h q", h=2)[:, :, mc:mc + nq]
                                        o_ap = pTm[pb][:, 0:2 * nq].rearrange("p (h q) -> p h q", h=2)
                                        i_ap = pT[pb][:, 0:2 * nq].rearrange("p (h q) -> p h q", h=2)
                                    else:
                                        hh = kbs[0][1]
                                        m_ap = Emask3[:, midx, hh * 128 + mc:hh * 128 + mc + nq]
                                        o_ap = pTm[pb][:, 0:nq]
                                        i_ap = pT[pb][:, 0:nq]
                                    vmk = cx.op("vector", lambda e, o_ap=o_ap, i_ap=i_ap, m_ap=m_ap: e.tensor_tensor(o_ap, i_ap, m_ap, ALU.mult), s_mk)
                                    mk_hist.append(vmk)
                                    bankO, boi = ringO.acquire()
                                    cx.wait("tensor", s_mk, vmk)
                                    for i, (kb, hh) in enumerate(kbs):
                                        blk = r * nkb + (kb - kb_lo)
                                        fn = lambda e, bankO=bankO, i=i, nq=nq, blk=blk, vb=vb, vcols=vcols, pb=pb: e.matmul(
                                            bankO[:, 0:nq], Vt[vb][:, blk, vcols], pTm[pb][:, i * nq:(i + 1) * nq], start=(i == 0), stop=(i == nk - 1))
                                        if i == nk - 1:
                                            vpo = cx.op("tensor", fn, s_po)
                                        else:
                                            cx.op("tensor", fn)
                                    po_hist.append(vpo)
                                    c0 = r + dil * (P - Q0)
                                    dst = acc[hd][:, c0:c0 + dil * (nq - 1) + 1:dil]
                                    cx.wait("vector", s_po, vpo)
                                    if pi == 0:
                                        vac = cx.op("vector", lambda e, dst=dst, bankO=bankO, nq=nq: e.tensor_copy(dst, bankO[:, 0:nq]), s_ac)
                                    else:
                                        vac = cx.op("vector", lambda e, dst=dst, bankO=bankO, nq=nq: e.tensor_tensor(dst, bankO[:, 0:nq], dst, ALU.add), s_ac)
                                    ringO.release(boi, s_ac, vac)
                                    seg_i += 1
                        vt_free[vb] = s_po.n
                    ab = at_i % 2
                    at_i += 1
                    cx.wait("vector", s_ac, s_ac.n)
                    cx.wait("vector", s_out, at_store[ab])
                    v1 = cx.op("vector", lambda e: e.tensor_copy(rD[0:64, :], acc[0][64:128, :]), s_nm)
                    v2 = cx.op("vector", lambda e: e.tensor_copy(rD[64:128, :], acc[1][0:64, :]), s_nm)
                    cx.wait("vector", s_nm, v2)
                    v2b = cx.op("vector", lambda e: e.reciprocal(rD, rD), s_nm)
                    cx.wait("vector", s_nm, v2b)
                    v3_ = cx.op("vector", lambda e, ab=ab: e.tensor_tensor(attnT[ab][0:64, :], acc[0][0:64, :], rD[0:64, :], ALU.mult), s_nm)
                    v4 = cx.op("vector", lambda e, ab=ab: e.tensor_tensor(attnT[ab][64:128, :], acc[1][64:128, :], rD[64:128, :], ALU.mult), s_nm)
                    cx.wait("gpsimd", s_nm, v4)
                    cx.wait("vector", s_nm, v4)
                    at_store[ab] = cx.dma("gpsimd", aT_d[hp, :, s * S + q0:s * S + q0 + 2048], attnT[ab], s_out)
                pe_done_seq = s_po.n
            barrier()

        def mix_out_phase():
            cx.top = persist_top
            s_w = cx.sem("o_w")
            s_in = cx.sem("o_in")
            s_pe = cx.sem("o_pe")
            s_res = cx.sem("o_res")
            s_ca = cx.sem("o_ca")
            s_st = cx.sem("o_st")
            s_sq = cx.sem("o_sq")
            s_rs = cx.sem("o_rs")
            s_tA = cx.sem("o_tA")
            s_hT = cx.sem("o_hT")
            Wout = v3(cx.alloc(KC * D, BF16), KC)
            eps_t = cx.alloc(1)
            xT = v3(cx.alloc(KC * T), KC)
            mixT = v3(cx.alloc(KC * T, BF16), KC)
            hT = v3(cx.alloc(KC * T, BF16), KC)
            xsq = v3(cx.alloc(KC * T, BF16), KC)
            rstd = cx.alloc(T)
            sqt = cx.alloc(T)
            tmpA = [cx.alloc(T), cx.alloc(T)]
            woutv = wout_d.rearrange("(kc p) f -> p kc f", p=128)
            for kc in range(KC):
                cx.dma("gpsimd", Wout[:, kc, :], woutv[:, kc, :], s_w)
            cx.wait("tensor", s_w, s_w.n)
            cx.op("vector", lambda e: e.memset(eps_t, EPS))
            ring = Ring([PS[0], PS[1], PS[2]])
            tiles = [(s, ti) for s in range(NSEQ) for ti in range(S // T) if s < 2 or 2 <= ti < 6]
            if debug and "ntiles" in debug:
                tiles = tiles[:debug["ntiles"]]
            hT_hist = []
            last_x = 0
            last_h = 0
            for (s, ti) in tiles:
                g0 = s * S + ti * T
                cx.wait("sync", s_out, max(last_x, last_h))
                cx.wait("sync", s_pe, s_pe.n)
                cx.wait("sync", s_hT, s_hT.n)
                cx.dma("sync", xT, x1T_d[:, :, g0:g0 + T].rearrange("k p n -> p k n"), s_in)
                cx.dma("sync", mixT[:, 0:4, :], aT_d[:, :, g0:g0 + T].rearrange("k p n -> p k n"), s_in)
                vin = cx.dma("sync", mixT[:, 4:8, :], gT_d[:, :, g0:g0 + T].rearrange("k p n -> p k n"), s_in)
                cx.wait("tensor", s_in, vin)
                cx.wait("vector", s_in, vin)
                res_vals = []
                for dc in range(KC):
                    bank, bi = ring.acquire()
                    for kc in range(KC):
                        fn = lambda e, bank=bank, kc=kc, dc=dc: e.matmul(
                            bank[:, :], Wout[:, kc, dc * 128:(dc + 1) * 128], mixT[:, kc, :], start=(kc == 0), stop=(kc == KC - 1))
                        if kc == KC - 1:
                            vp = cx.op("tensor", fn, s_pe)
                        else:
                            cx.op("tensor", fn)
                    cx.wait("vector", s_pe, vp)
                    vres = cx.op("vector", lambda e, bank=bank, dc=dc, s=s: e.scalar_tensor_tensor(
                        xT[:, dc, :], bank[:, :], G_(1, dc, s), xT[:, dc, :], ALU.mult, ALU.add), s_res)
                    ring.release(bi, s_res, vres)
                    res_vals.append(vres)
                ca_vals = []
                for kc in range(KC):
                    cx.wait("scalar", s_res, res_vals[kc])
                    ca_vals.append(cx.op("scalar", lambda e, kc=kc: e.activation(out=xsq[:, kc, :], in_=xT[:, kc, :], func=AF.Square), s_ca))
                cx.wait("gpsimd", s_res, res_vals[-1])
                last_x = cx.dma("gpsimd", x2T_d[:, :, g0:g0 + T].rearrange("k p n -> p k n"), xT, s_out)
                cx.wait("tensor", s_sq, s_sq.n)
                for kc in range(KC):
                    cx.wait("tensor", s_ca, ca_vals[kc])
                    fn = lambda e, kc=kc: e.matmul(PS[3][:, :], ones_bf, xsq[:, kc, :], start=(kc == 0), stop=(kc == KC - 1))
                    if kc == KC - 1:
                        vst = cx.op("tensor", fn, s_st)
                    else:
                        cx.op("tensor", fn)
                cx.wait("scalar", s_st, vst)
                vsq = cx.op("scalar", lambda e: e.activation(out=sqt, in_=PS[3][:, :], func=AF.Sqrt, bias=eps_t[:, 0:1], scale=1.0 / D), s_sq)
                cx.wait("vector", s_sq, vsq)
                vrs = cx.op("vector", lambda e: e.reciprocal(rstd, sqt), s_rs)
                cx.wait("vector", s_rs, vrs)
                cx.wait("scalar", s_out, last_h)
                for kc in range(KC):
                    b = kc % 2
                    if len(hT_hist) >= 2:
                        cx.wait("vector", s_hT, hT_hist[-2])
                    vt = cx.op("vector", lambda e, kc=kc, b=b, s=s: e.scalar_tensor_tensor(
                        tmpA[b], xT[:, kc, :], A_(2, kc, s), rstd, ALU.mult, ALU.mult), s_tA)
                    cx.wait("scalar", s_tA, vt)
                    va = cx.op("scalar", lambda e, kc=kc, b=b, s=s: e.activation(
                        out=hT[:, kc, :], in_=tmpA[b], func=AF.Identity, bias=B_(2, kc, s), scale=1.0), s_hT)
                    hT_hist.append(va)
                cx.wait("gpsimd", s_hT, va)
                last_h = cx.dma("gpsimd", h3T_d[:, :, g0:g0 + T].rearrange("k p n -> p k n"), hT, s_out)
            barrier()

        if stop_after >= 1:
            ffn_phase(0)
        if stop_after >= 2:
            mix_in_phase()
        if stop_after >= 3:
            attn_phase()
        if stop_after >= 4:
            mix_out_phase()
        if stop_after >= 5:
            ffn_phase(1)

        barrier()
        cx.emit_all()
    return nc


def _core_inputs(core, inp):
    xs = inp["x_sample"]
    xp = inp["x_prompt"]
    x = np.zeros((NSEQ, S, D), np.float32)
    x[0] = xs[2 * core]
    x[1] = xs[2 * core + 1]
    pb, qd = core // 4, core % 4
    lo = qd * 2048 - 1024
    valid = np.ones((NSEQ, S), np.float32)
    a, b = max(lo, 0), min(lo + S, 8192)
    x[2, a - lo:b - lo] = xp[pb, a:b]
    valid[2, :] = 0.0
    valid[2, a - lo:b - lo] = 1.0
    c3 = np.stack([inp["c_sample"][2 * core], inp["c_sample"][2 * core + 1], inp["c_prompt"][pb]], 0)
    cT = np.ascontiguousarray(c3.reshape(3, KC, 128).transpose(2, 1, 0)).reshape(128, KC * 3)
    return x.reshape(NTOK, D), cT, np.ascontiguousarray(valid.reshape(NTOK // 128, 128).T)


def _shared_inputs(inp):
    def pcol(v):
        return np.ascontiguousarray(np.asarray(v, np.float32).reshape(-1, 128).T)
    pvec = np.concatenate([pcol(inp["ada_b"][0]), pcol(inp["ffn1_norm"][0]), pcol(inp["mix_norm"][0]),
                           pcol(inp["ffn2_norm"][0]), pcol(inp["final_norm"])], axis=1)
    kp = np.arange(128)[:, None]
    col = np.arange(256)[None, :]
    hh, q = col // 128, col % 128
    rel = 128 * hh + kp - q - 64
    sh = {
        "pvec": pvec.astype(np.float32),
        "ident": np.eye(128, dtype=np.float32),
        "absrel": np.abs(rel).astype(np.float32),
        "band": (np.abs(rel) <= 64).astype(np.float32),
        "ada_w": np.ascontiguousarray(inp["ada_w"][0]),
        "wg1": np.ascontiguousarray(inp["ffn1_w_gate"][0]), "wu1": np.ascontiguousarray(inp["ffn1_w_up"][0]),
        "wd1": np.ascontiguousarray(inp["ffn1_w_down"][0]),
        "wg2": np.ascontiguousarray(inp["ffn2_w_gate"][0]), "wu2": np.ascontiguousarray(inp["ffn2_w_up"][0]),
        "wd2": np.ascontiguousarray(inp["ffn2_w_down"][0]),
        "w_in": np.ascontiguousarray(inp["w_in"][0]),
        "wsT": np.ascontiguousarray(inp["sgu_w"][0].transpose(2, 0, 1)).reshape(128, 512),
        "sgu_b": np.ascontiguousarray(inp["sgu_b"][0]).reshape(1, 512),
        "sgu_norm": np.ascontiguousarray(inp["sgu_norm"][0]).reshape(1, 512),
        "w_out": np.ascontiguousarray(inp["w_out"][0]),
    }
    return sh


def make_in_maps(inp):
    inp = {k: np.asarray(v) for k, v in inp.items()}
    sh = _shared_inputs(inp)
    maps = []
    for core in range(N_CORES):
        x, cT, valid = _core_inputs(core, inp)
        m = dict(sh)
        m["x"] = x
        m["cT"] = cT
        m["valid"] = valid
        maps.append(m)
    return maps


def kernel(**inputs):
    maps = make_in_maps(inputs)
    nc = build_program()
    res = run_bass_kernel_spmd(nc, maps, core_ids=list(range(N_CORES)))
    ys = np.empty((16, S, D), np.float32)
    yp = np.empty((2, 8192, D), np.float32)
    for core in range(N_CORES):
        y = res.results[core]["y"]
        ys[2 * core] = y[0:S]
        ys[2 * core + 1] = y[S:2 * S]
        pb, qd = core // 4, core % 4
        yp[pb, qd * 2048:(qd + 1) * 2048] = y[2 * S:]
    return (yp, ys)
```

```python
import numpy as np
from contextlib import ExitStack
import concourse.bass as bass
import concourse.mybir as mybir
from concourse.bass_utils import run_bass_kernel_spmd

F32 = mybir.dt.float32
BF16 = mybir.dt.bfloat16
AF = mybir.ActivationFunctionType
ALU = mybir.AluOpType

D = 1024
KC = 8
DFF = 2816
FC = 22
S = 4096
NSEQ = 3
NTOK = NSEQ * S
T = 512
TB = 4
NQTOK = 2 * S + 2048
EPS = 1e-6
HEADS = 8
PATTERNS = ((128, 1), (512, 4), (2048, 16))
N_CORES = 8
ARENA_F32 = 53200

ENGINES = ("tensor", "vector", "scalar", "gpsimd", "sync")


class Sem:
    def __init__(self, h):
        self.h = h
        self.n = 0


class Ctx:
    def __init__(self, nc, es):
        self.nc = nc
        self.es = es
        self.q = {e: [] for e in ENGINES}
        self.nsem = 0
        self.arena = es.enter_context(nc.sbuf_tensor("arena", [128, ARENA_F32], F32))
        self.top = 0
        self.psum = [es.enter_context(nc.psum_tensor("ps%d" % i, [128, 512], F32)) for i in range(8)]

    def sem(self, name):
        h = self.es.enter_context(self.nc.semaphore(name))
        self.nsem += 1
        return Sem(h)

    def alloc(self, cols, dtype=F32):
        if dtype == BF16:
            ncol32 = (cols + 1) // 2
        else:
            ncol32 = cols
        a = self.top
        self.top += ncol32
        assert self.top <= ARENA_F32, ("SBUF arena overflow", self.top)
        ap = self.arena[:, a:a + ncol32]
        if dtype == BF16:
            ap = ap.bitcast(BF16)
        return ap

    def op(self, eng, fn, sig=None, k=None):
        if sig is None:
            self.q[eng].append(fn)
            return None
        if k is None:
            k = 16 if eng in ("sync", "gpsimd_dma") else 1
        sig.n += k
        h = sig.h
        self.q[eng].append(lambda e, fn=fn, h=h, k=k: fn(e).then_inc(h, k))
        return sig.n

    def dma(self, eng, out, in_, sig):
        sig.n += 16
        h = sig.h
        self.q[eng].append(lambda e, out=out, in_=in_, h=h: e.dma_start(out=out, in_=in_).then_inc(h, 16))
        return sig.n

    def wait(self, eng, sem, val):
        if val is None or val <= 0:
            return
        h = sem.h
        self.q[eng].append(lambda e, h=h, val=val: e.wait_ge(h, val))

    def emit_all(self):
        nc = self.nc
        with nc.Block() as blk:
            for name in ENGINES:
                ops = self.q[name]

                def body(e, ops=ops):
                    for f in ops:
                        f(e)
                getattr(blk, name)(body)


def v3(ap, a):
    return ap.rearrange("p (a b) -> p a b", a=a)


def build_program(debug=None):
    nc = bass.Bass("TRN2", target_bir_lowering=False)
    dt = nc.dram_tensor

    def din(name, shape, dtype=F32):
        return dt(name, list(shape), dtype, kind="ExternalInput").ap()

    x_d = din("x", [NTOK, D])
    cT_d = din("cT", [128, KC * NSEQ])
    pvec_d = din("pvec", [128, 104])
    valid_d = din("valid", [128, NTOK // 128])
    ident_d = din("ident", [128, 128])
    absrel_d = din("absrel", [128, 256])
    band_d = din("band", [128, 256])
    adaw_d = din("ada_w", [D, 9 * D])
    wg_d = [din("wg1", [D, DFF]), din("wg2", [D, DFF])]
    wu_d = [din("wu1", [D, DFF]), din("wu2", [D, DFF])]
    wd_d = [din("wd1", [DFF, D]), din("wd2", [DFF, D])]
    win_d = din("w_in", [D, 2560])
    wsT_d = din("wsT", [128, 512])
    sgub_d = din("sgu_b", [1, 512])
    sgun_d = din("sgu_norm", [1, 512])
    wout_d = din("w_out", [D, D])
    y_d = dt("y", [NQTOK, D], F32, kind="ExternalOutput").ap()

    skind = "ExternalOutput" if (debug and not debug.get("internal")) else "Internal"
    x1T_d = dt("x1T", [KC, 128, NTOK], F32, kind=skind).ap()
    h2T_d = dt("h2T", [KC, 128, NTOK], BF16, kind=skind).ap()
    qT_d = dt("qT", [4, 128, NTOK], BF16, kind=skind).ap()
    kT_d = dt("kT", [4, 128, NTOK], BF16, kind=skind).ap()
    V_d = dt("V", [4, NTOK, 192], BF16, kind=skind).ap()
    gT_d = dt("gT", [4, 128, NTOK], BF16, kind=skind).ap()
    aT_d = dt("aT", [4, 128, NTOK], BF16, kind=skind).ap()
    x2T_d = dt("x2T", [KC, 128, NTOK], F32, kind=skind).ap()
    h3T_d = dt("h3T", [KC, 128, NTOK], BF16, kind=skind).ap()

    stop_after = debug.get("stop_after", 99) if debug else 99

    with ExitStack() as es:
        cx = Ctx(nc, es)
        PS = cx.psum

        ident = cx.alloc(128)
        ones_bf = cx.alloc(128, BF16)
        pvec = cx.alloc(104)
        valid = cx.alloc(NTOK // 128)
        modT = cx.alloc(72 * 3)
        Amod = cx.alloc(3 * 8 * 3)
        Gmod = cx.alloc(3 * 8 * 3)
        Emask = cx.alloc(24 * 256, BF16)
        persist_top = cx.top

        modT3 = v3(modT, 72)
        Amod4 = Amod.rearrange("p (l k s) -> p l k s", l=3, k=8)
        Gmod4 = Gmod.rearrange("p (l k s) -> p l k s", l=3, k=8)
        Emask3 = v3(Emask, 24)

        def A_(l, kc, s):
            return Amod4[:, l, kc, s:s + 1]

        def B_(l, kc, s):
            return modT3[:, l * 24 + kc, s:s + 1]

        def G_(l, kc, s):
            return Gmod4[:, l, kc, s:s + 1]

        s_out = cx.sem("s_out")

        def barrier():
            for e in ENGINES:
                cx.wait(e, s_out, s_out.n)

        s_ld = cx.sem("p0_ld")
        s_c = cx.sem("p0_c")
        s_aw = cx.sem("p0_aw")
        s_awf = cx.sem("p0_awf")
        s_v = cx.sem("p0_v")
        s_a = cx.sem("p0_a")

        cT = cx.alloc(24)
        scT = cx.alloc(24)
        absrel = cx.alloc(256)
        band = cx.alloc(256)
        etmp = [cx.alloc(256), cx.alloc(256)]
        tmp24 = cx.alloc(24)
        awbuf = [cx.alloc(8 * 1024), cx.alloc(8 * 1024)]

        for dst, src in ((ident, ident_d), (pvec, pvec_d), (valid, valid_d), (cT, cT_d),
                         (absrel, absrel_d), (band, band_d)):
            cx.dma("sync", dst, src, s_ld)
        n_ld = s_ld.n
        cx.wait("scalar", s_ld, n_ld)
        cx.wait("vector", s_ld, n_ld)
        cx.op("vector", lambda e: e.memset(ones_bf, 1.0))
        v_sc = cx.op("scalar", lambda e: e.activation(out=scT, in_=cT, func=AF.Silu), s_c)
        cx.wait("tensor", s_c, v_sc)
        adaw_v = adaw_d.rearrange("(kc p) f -> p kc f", p=128)
        for ch in range(9):
            b = ch % 2
            if ch >= 2:
                cx.wait("sync", s_awf, ch - 1)
            vld = cx.dma("sync", v3(awbuf[b], 8), adaw_v[:, :, ch * 1024:(ch + 1) * 1024], s_aw)
            cx.wait("tensor", s_aw, vld)
            for fl in range(8):
                col = (ch * 8 + fl) * 3
                for kc in range(KC):
                    fn = (lambda e, b=b, fl=fl, kc=kc, col=col: e.matmul(
                        PS[0][:, col:col + 3], v3(awbuf[b], 8)[:, kc, fl * 128:(fl + 1) * 128],
                        scT[:, kc * 3:(kc + 1) * 3], start=(kc == 0), stop=(kc == KC - 1)))
                    if fl == 7 and kc == KC - 1:
                        cx.op("tensor", fn, s_awf)
                    else:
                        cx.op("tensor", fn)
        cx.wait("vector", s_awf, 9)
        psm3 = PS[0][:, 0:216].rearrange("p (a b) -> p a b", b=3)
        for s in range(3):
            vm = cx.op("vector", lambda e, s=s: e.tensor_tensor(modT3[:, :, s], psm3[:, :, s], pvec[:, 0:72], ALU.add), s_v)
        cx.wait("vector", s_v, vm)
        for l in range(3):
            gl = pvec[:, 72 + 8 * l:80 + 8 * l]
            for s in range(3):
                v1 = cx.op("vector", lambda e, l=l, s=s: e.tensor_scalar(
                    tmp24[:, 0:8], modT3[:, l * 24 + 8:l * 24 + 16, s], 1.0, None, ALU.add), s_v)
                cx.wait("vector", s_v, v1)
                v2 = cx.op("vector", lambda e, l=l, s=s, gl=gl: e.tensor_tensor(
                    Amod4[:, l, :, s], tmp24[:, 0:8], gl, ALU.mult), s_v)
                cx.wait("vector", s_v, v2)
                rw = 1.0 if l == 1 else 0.5
                cx.op("vector", lambda e, l=l, s=s, rw=rw: e.tensor_scalar(
                    Gmod4[:, l, :, s], modT3[:, l * 24 + 16:l * 24 + 24, s], rw, None, ALU.mult), s_v)
        i = 0
        for h in range(HEADS):
            slope = 2.0 ** (-8.0 * (h + 1) / HEADS)
            for pi, (_, dil) in enumerate(PATTERNS):
                b = i % 2
                if i >= 2:
                    cx.wait("scalar", s_v, vmask[b])
                va = cx.op("scalar", lambda e, b=b, sc=-slope * dil: e.activation(
                    out=etmp[b], in_=absrel, func=AF.Exp, scale=sc), s_a)
                cx.wait("vector", s_a, va)
                vv = cx.op("vector", lambda e, b=b, idx=h * 3 + pi: e.tensor_tensor(
                    Emask3[:, idx, :], etmp[b], band, ALU.mult), s_v)
                if i == 0:
                    vmask = [0, 0]
                vmask[b] = vv
                i += 1
        v_p0 = s_v.n
        for e in ENGINES:
            cx.wait(e, s_v, v_p0)
            cx.wait(e, s_a, s_a.n)

        def ffn_phase(mode):
            cx.top = persist_top
            pre = "f%d_" % mode
            s_w = cx.sem(pre + "w")
            s_xin = cx.sem(pre + "xin")
            s_trp = cx.sem(pre + "trp")
            s_cv = cx.sem(pre + "cv")
            s_ca = cx.sem(pre + "ca")
            s_st = cx.sem(pre + "st")
            s_sq = cx.sem(pre + "sq")
            s_rs = cx.sem(pre + "rs")
            s_tA = cx.sem(pre + "tA")
            s_hT = cx.sem(pre + "hT")
            s_g = cx.sem(pre + "g")
            s_u = cx.sem(pre + "u")
            s_sg = cx.sem(pre + "sg")
            s_hid = cx.sem(pre + "hid")
            s_dn = cx.sem(pre + "dn")
            s_res = cx.sem(pre + "res")
            s_yt = cx.sem(pre + "yt")
            s_ye = cx.sem(pre + "ye")

            Wg = v3(cx.alloc(KC * DFF, BF16), KC)
            Wu = v3(cx.alloc(KC * DFF, BF16), KC)
            Wd = v3(cx.alloc(FC * D, BF16), FC)
            xT = v3(cx.alloc(KC * T), KC)
            hT = v3(cx.alloc(KC * T, BF16), KC)
            hid_raw32 = cx.alloc(FC * T // 2)
            hid = v3(hid_raw32.bitcast(BF16), FC)
            xtok = hid_raw32[:, 0:TB * D].rearrange("p (a b) -> p a b", a=TB)
            xsq32 = cx.alloc(KC * T // 2)
            xsq = v3(xsq32.bitcast(BF16), KC)
            rstd = cx.alloc(T)
            sqt = cx.alloc(T)
            tmpA = [xsq32[:, 0:T], xsq32[:, T:2 * T]]
            sg = [xsq32[:, 2 * T:3 * T], xsq32[:, 3 * T:4 * T]]

            wgv = wg_d[mode].rearrange("(kc p) f -> p kc f", p=128)
            wuv = wu_d[mode].rearrange("(kc p) f -> p kc f", p=128)
            wdv = wd_d[mode].rearrange("(j p) d -> p j d", p=128)
            for kc in range(KC):
                cx.dma("gpsimd", Wg[:, kc, :], wgv[:, kc, :], s_w)
                cx.dma("gpsimd", Wu[:, kc, :], wuv[:, kc, :], s_w)
            for j0 in range(0, FC, 2):
                cx.dma("gpsimd", Wd[:, j0:j0 + 2, :], wdv[:, j0:j0 + 2, :], s_w)
            cx.wait("tensor", s_w, s_w.n)

            l_in = 0 if mode == 0 else 2

            if mode == 0:
                tiles = [(s, ti) for s in range(NSEQ) for ti in range(S // T)]
            else:
                tiles = [(s, ti) for s in range(NSEQ) for ti in range(S // T) if s < 2 or 2 <= ti < 6]

            sg_hist = []
            hid_hist = []
            dstage = debug.get("stage", 9) if debug else 9
            if debug and "ntiles" in debug:
                tiles = tiles[:debug["ntiles"]]
            bank_i = [0]
            tr_hist = []
            gu_i = [0]
            out_row = 0
            last_store = {"x": 0, "h": 0}
            prev_dn = 0

            def ring_acquire(consumers):
                i = bank_i[0]
                if i >= 2:
                    for sem, val in tr_hist[i - 2]:
                        cx.wait("tensor", sem, val)
                bank_i[0] += 1
                tr_hist.append(None)
                return PS[i % 2], i

            def norm_to_hT(l, s, square_src_ready):
                cx.wait("tensor", s_sq, s_sq.n)
                for kc in range(KC):
                    cx.wait("tensor", s_ca, square_src_ready[kc])
                    fn = lambda e, kc=kc: e.matmul(PS[2][:, :], ones_bf, xsq[:, kc, :], start=(kc == 0), stop=(kc == KC - 1))
                    if kc == KC - 1:
                        vst = cx.op("tensor", fn, s_st)
                    else:
                        cx.op("tensor", fn)
                cx.wait("scalar", s_st, vst)
                vsq = cx.op("scalar", lambda e: e.activation(out=sqt, in_=PS[2][:, :], func=AF.Sqrt, bias=EPS_AP[0], scale=1.0 / D), s_sq)
                cx.wait("vector", s_sq, vsq)
                vrs = cx.op("vector", lambda e: e.reciprocal(rstd, sqt), s_rs)
                cx.wait("vector", s_rs, vrs)
                vh = []
                for kc in range(KC):
                    b = kc % 2
                    if len(hT_hist) >= 2:
                        cx.wait("vector", s_hT, hT_hist[-2])
                    if l is None:
                        vt = cx.op("vector", lambda e, kc=kc: e.scalar_tensor_tensor(
                            xT[:, kc, :], xT[:, kc, :], pvec[:, 96 + kc:97 + kc], rstd, ALU.mult, ALU.mult), s_hT)
                        hT_hist.append(vt)
                        vh.append(vt)
                    else:
                        vt = cx.op("vector", lambda e, kc=kc, b=b: e.scalar_tensor_tensor(
                            tmpA[b], xT[:, kc, :], A_(l, kc, s), rstd, ALU.mult, ALU.mult), s_tA)
                        cx.wait("scalar", s_tA, vt)
                        va = cx.op("scalar", lambda e, kc=kc, b=b: e.activation(
                            out=hT[:, kc, :], in_=tmpA[b], func=AF.Identity, bias=B_(l, kc, s), scale=1.0), s_hT)
                        hT_hist.append(va)
                        vh.append(va)
                return vh

            hT_hist = []
            EPS_AP = [None]
            eps_t = cx.alloc(1)
            cx.op("vector", lambda e: e.memset(eps_t, EPS))
            EPS_AP[0] = eps_t[:, 0:1]

            for (s, ti) in tiles:
                g0 = s * S + ti * T
                if mode == 0:
                    cx.wait("sync", s_dn, prev_dn)
                    vx = cx.dma("sync", xtok, x_d[g0:g0 + T, :].rearrange("(tb p) d -> p tb d", p=128), s_xin)
                    cx.wait("tensor", s_xin, vx)
                    cx.wait("vector", s_out, last_store["x"])
                    cx.wait("scalar", s_out, last_store["h"])
                    ca_vals = []
                    for kc in range(KC if dstage >= 1 else 0):
                        bank, bi = ring_acquire(2)
                        for tb in range(TB):
                            fn = lambda e, bank=bank, tb=tb, kc=kc: e.transpose(
                                bank[:, tb * 128:(tb + 1) * 128], xtok[:, tb, kc * 128:(kc + 1) * 128], ident)
                            if tb == TB - 1:
                                vtr = cx.op("tensor", fn, s_trp)
                            else:
                                cx.op("tensor", fn)
                        cx.wait("vector", s_trp, vtr)
                        vcv = cx.op("vector", lambda e, bank=bank, kc=kc: e.tensor_copy(xT[:, kc, :], bank[:, :]), s_cv)
                        cx.wait("scalar", s_cv, vcv)
                        vca = cx.op("scalar", lambda e, kc=kc: e.activation(
                            out=xsq[:, kc, :], in_=xT[:, kc, :], func=AF.Square), s_ca)
                        tr_hist[bi] = [(s_cv, vcv)]
                        ca_vals.append(vca)
                    if dstage < 2:
                        continue
                    vh = norm_to_hT(0, s, ca_vals)
                    if dstage < 3:
                        continue
                else:
                    cx.wait("sync", s_out, max(last_store["x"], last_store["h"]))
                    cx.wait("sync", s_res, s_res.n)
                    cx.wait("sync", s_yt, s_yt.n)
                    cx.wait("sync", s_u, s_u.n)
                    vx1 = cx.dma("sync", xT, x2T_d[:, :, g0:g0 + T].rearrange("k p n -> p k n"), s_xin)
                    vx2 = cx.dma("sync", hT, h3T_d[:, :, g0:g0 + T].rearrange("k p n -> p k n"), s_xin)
                    cx.wait("tensor", s_xin, vx2)
                    cx.wait("vector", s_xin, vx2)
                    vh = [0] * KC

                for j in range(FC):
                    gi = gu_i[0]
                    gu_i[0] += 1
                    pg = PS[3 + gi % 2]
                    pu = PS[5 + gi % 2]
                    if gi >= 2:
                        cx.wait("tensor", s_sg, sg_hist[gi - 2])
                    for kc in range(KC):
                        if j == 0:
                            cx.wait("tensor", s_hT, vh[kc])
                        fn = lambda e, pg=pg, kc=kc, j=j: e.matmul(
                            pg[:, :], Wg[:, kc, j * 128:(j + 1) * 128], hT[:, kc, :], start=(kc == 0), stop=(kc == KC - 1))
                        if kc == KC - 1:
                            vg = cx.op("tensor", fn, s_g)
                        else:
                            cx.op("tensor", fn)
                    if gi >= 2:
                        cx.wait("tensor", s_hid, hid_hist[gi - 2])
                    for kc in range(KC):
                        fn = lambda e, pu=pu, kc=kc, j=j: e.matmul(
                            pu[:, :], Wu[:, kc, j * 128:(j + 1) * 128], hT[:, kc, :], start=(kc == 0), stop=(kc == KC - 1))
                        if kc == KC - 1:
                            vu = cx.op("tensor", fn, s_u)
                        else:
                            cx.op("tensor", fn)
                    b = gi % 2
                    cx.wait("scalar", s_g, vg)
                    if gi >= 2:
                        cx.wait("scalar", s_hid, hid_hist[gi - 2])
                    vsg = cx.op("scalar", lambda e, pg=pg, b=b: e.activation(out=sg[b], in_=pg[:, :], func=AF.Silu), s_sg)
                    sg_hist.append(vsg)
                    cx.wait("vector", s_sg, vsg)
                    cx.wait("vector", s_u, vu)
                    vhid = cx.op("vector", lambda e, pu=pu, b=b, j=j: e.tensor_tensor(hid[:, j, :], pu[:, :], sg[b], ALU.mult), s_hid)
                    hid_hist.append(vhid)

                if dstage < 4:
                    continue
                res_vals = []
                for dc in range(KC):
                    bank, bi = ring_acquire(1)
                    for j in range(FC):
                        if dc == 0:
                            cx.wait("tensor", s_hid, hid_hist[len(hid_hist) - FC + j])
                        fn = lambda e, bank=bank, j=j, dc=dc: e.matmul(
                            bank[:, :], Wd[:, j, dc * 128:(dc + 1) * 128], hid[:, j, :], start=(j == 0), stop=(j == FC - 1))
                        if j == FC - 1:
                            vdn = cx.op("tensor", fn, s_dn)
                        else:
                            cx.op("tensor", fn)
                    cx.wait("vector", s_dn, vdn)
                    vres = cx.op("vector", lambda e, bank=bank, dc=dc, s=s: e.scalar_tensor_tensor(
                        xT[:, dc, :], bank[:, :], G_(l_in, dc, s), xT[:, dc, :], ALU.mult, ALU.add), s_res)
                    tr_hist[bi] = [(s_res, vres)]
                    res_vals.append(vres)
                prev_dn = vdn

                if dstage < 5:
                    continue
                ca_vals = []
                for kc in range(KC):
                    cx.wait("scalar", s_res, res_vals[kc])
                    vca = cx.op("scalar", lambda e, kc=kc: e.activation(out=xsq[:, kc, :], in_=xT[:, kc, :], func=AF.Square), s_ca)
                    ca_vals.append(vca)
                if mode == 0:
                    cx.wait("gpsimd", s_res, res_vals[-1])
                    last_store["x"] = cx.dma("gpsimd", x1T_d[:, :, g0:g0 + T].rearrange("k p n -> p k n"), xT, s_out)
                    vh2 = norm_to_hT(1, s, ca_vals)
                    cx.wait("gpsimd", s_hT, vh2[-1])
                    last_store["h"] = cx.dma("gpsimd", h2T_d[:, :, g0:g0 + T].rearrange("k p n -> p k n"), hT, s_out)
                else:
                    vy = norm_to_hT(None, s, ca_vals)
                    ytok = xtok
                    for tb in range(TB):
                        for half in range(2):
                            bank, bi = ring_acquire(1)
                            for k4 in range(4):
                                kc = half * 4 + k4
                                if tb == 0:
                                    cx.wait("tensor", s_hT, vy[kc])
                                fn = lambda e, bank=bank, k4=k4, kc=kc, tb=tb: e.transpose(
                                    bank[:, k4 * 128:(k4 + 1) * 128], xT[:, kc, tb * 128:(tb + 1) * 128], ident)
                                if k4 == 3:
                                    vtr = cx.op("tensor", fn, s_yt)
                                else:
                                    cx.op("tensor", fn)
                            cx.wait("vector", s_yt, vtr)
                            vye = cx.op("vector", lambda e, bank=bank, tb=tb, half=half: e.tensor_copy(
                                ytok[:, tb, half * 512:(half + 1) * 512], bank[:, :]), s_ye)
                            tr_hist[bi] = [(s_ye, vye)]
                    cx.wait("gpsimd", s_ye, vye)
                    last_store["x"] = cx.dma("gpsimd", y_d[out_row:out_row + T, :].rearrange("(tb p) d -> p tb d", p=128), ytok, s_out)
                    last_store["h"] = last_store["x"]
                    out_row += T
                    cx.wait("vector", s_out, last_store["x"])
            barrier()

        if debug:
            dbg_d = dt("dbg", [128, persist_top], F32, kind="ExternalOutput").ap()
            cx.dma("gpsimd", dbg_d, cx.arena[:, 0:persist_top], s_out)
            barrier()

        class Ring:
            def __init__(self, banks):
                self.banks = banks
                self.hist = []

            def acquire(self):
                i = len(self.hist)
                n = len(self.banks)
                if i >= n:
                    for sem, val in self.hist[i - n]:
                        cx.wait("tensor", sem, val)
                self.hist.append([])
                return self.banks[i % n], i

            def release(self, i, sem, val):
                self.hist[i].append((sem, val))

        def mix_in_phase():
            cx.top = persist_top
            s_w = cx.sem("m_w")
            s_in = cx.sem("m_in")
            s_pe = cx.sem("m_pe")
            s_a = cx.sem("m_a")
            s_v = cx.sem("m_v")
            Win = v3(cx.alloc(KC * 2560, BF16), KC)
            wsT = v3(cx.alloc(512, BF16), 4)
            bs_row = cx.alloc(512, BF16)
            row32 = cx.alloc(512)
            ones32 = cx.alloc(256)
            sgn_bc = cx.alloc(512)
            eps_t = cx.alloc(1)
            h2 = [v3(cx.alloc(KC * T, BF16), KC), v3(cx.alloc(KC * T, BF16), KC)]
            qk_sb = v3(cx.alloc(8 * T, BF16), 8)
            V_sb = cx.alloc(TB * 4 * 192, BF16).rearrange("p (t h c) -> p t h c", t=TB, h=4)
            vn = v3(cx.alloc(TB * 512, BF16), TB)
            gv = [cx.alloc(512), cx.alloc(512)]
            sqv = cx.alloc(512)
            uT = v3(cx.alloc(4 * T), 4)
            gT_sb = v3(cx.alloc(4 * T, BF16), 4)
            ss = cx.alloc(8)
            sq1 = cx.alloc(8)
            rs = cx.alloc(8)

            winv = win_d.rearrange("(kc p) f -> p kc f", p=128)
            for kc in range(KC):
                cx.dma("gpsimd", Win[:, kc, :], winv[:, kc, :], s_w)
            cx.dma("gpsimd", wsT, wsT_d.rearrange("p (g t) -> p g t", g=4), s_w)
            cx.dma("gpsimd", bs_row[0:1, :], sgub_d, s_w)
            cx.dma("gpsimd", row32[0:1, :], sgun_d, s_w)
            cx.wait("vector", s_w, s_w.n)
            cx.wait("tensor", s_w, s_w.n)
            cx.op("vector", lambda e: e.memset(ones32, 1.0))
            cx.op("vector", lambda e: e.memset(eps_t, EPS))
            vo = cx.op("vector", lambda e: e.memset(sqv, 0.0), s_v)
            cx.wait("tensor", s_v, vo)
            vb = cx.op("tensor", lambda e: e.matmul(PS[7][:, :], ones32[0:1, 0:128], row32[0:1, :], start=True, stop=True), s_pe)
            cx.wait("vector", s_pe, vb)
            vo = cx.op("vector", lambda e: e.tensor_copy(sgn_bc, PS[7][:, :]), s_v)

            ring = Ring([PS[i] for i in range(6)])
            tiles = [(s, ti) for s in range(NSEQ) for ti in range(S // T)]
            if debug and "ntiles" in debug:
                tiles = tiles[:debug["ntiles"]]
            pe_tile_end = []
            last_stores = 0
            for it, (s, ti) in enumerate(tiles):
                g0 = s * S + ti * T
                halo = (s == 2 and not (2 <= ti < 6))
                hb_ = h2[it % 2]
                if it >= 2:
                    cx.wait("sync", s_pe, pe_tile_end[it - 2])
                vin = cx.dma("sync", hb_, h2T_d[:, :, g0:g0 + T].rearrange("k p n -> p k n"), s_in)
                cx.wait("tensor", s_in, vin)
                cx.wait("scalar", s_out, last_stores)
                cx.wait("vector", s_out, last_stores)
                va_last = 0
                for fcn in (range(4, 8) if halo else range(8)):
                    bank, bi = ring.acquire()
                    for kc in range(KC):
                        fn = lambda e, bank=bank, kc=kc, fcn=fcn, hb_=hb_: e.matmul(
                            bank[:, :], Win[:, kc, fcn * 128:(fcn + 1) * 128], hb_[:, kc, :], start=(kc == 0), stop=(kc == KC - 1))
                        if kc == KC - 1:
                            vp = cx.op("tensor", fn, s_pe)
                        else:
                            cx.op("tensor", fn)
                    cx.wait("scalar", s_pe, vp)
                    va_last = cx.op("scalar", lambda e, bank=bank, fcn=fcn: e.activation(
                        out=qk_sb[:, fcn, :], in_=bank[:, :], func=AF.Copy, scale=(0.125 if fcn < 4 else 1.0)), s_a)
                    ring.release(bi, s_a, va_last)
                cx.wait("gpsimd", s_a, va_last)
                if not halo:
                    cx.dma("gpsimd", qT_d[:, :, g0:g0 + T].rearrange("k p n -> p k n"), qk_sb[:, 0:4, :], s_out)
                cx.dma("gpsimd", kT_d[:, :, g0:g0 + T].rearrange("k p n -> p k n"), qk_sb[:, 4:8, :], s_out)
                for tb in range(TB):
                    blk = g0 // 128 + tb
                    bank, bi = ring.acquire()
                    for kc in range(KC):
                        fn = lambda e, bank=bank, kc=kc, tb=tb, hb_=hb_: e.matmul(
                            bank[:, :], hb_[:, kc, tb * 128:(tb + 1) * 128], Win[:, kc, 1024:1536], start=(kc == 0), stop=(kc == KC - 1))
                        if kc == KC - 1:
                            vp = cx.op("tensor", fn, s_pe)
                        else:
                            cx.op("tensor", fn)
                    cx.wait("vector", s_pe, vp)
                    bv = bank[:, :].rearrange("p (h c) -> p h c", h=4)
                    cx.op("vector", lambda e, bv=bv, tb=tb, blk=blk: e.tensor_scalar(
                        V_sb[:, tb, :, 0:64], bv[:, :, 0:64], valid[:, blk:blk + 1], None, ALU.mult))
                    cx.op("vector", lambda e, bv=bv, tb=tb, blk=blk: e.tensor_scalar(
                        V_sb[:, tb, :, 128:192], bv[:, :, 64:128], valid[:, blk:blk + 1], None, ALU.mult))
                    vv = cx.op("vector", lambda e, tb=tb, blk=blk: e.tensor_scalar(
                        V_sb[:, tb, :, 64:128], ones32.rearrange("p (h c) -> p h c", h=4), valid[:, blk:blk + 1], None, ALU.mult), s_v)
                    ring.release(bi, s_v, vv)
                    cx.wait("gpsimd", s_v, vv)
                    cx.dma("gpsimd", V_d[:, g0 + tb * 128:g0 + (tb + 1) * 128, :].rearrange("h p c -> p h c"), V_sb[:, tb], s_out)
                if not halo:
                    vvn = []
                    for tb in range(TB):
                        b = tb % 2
                        bank, bi = ring.acquire()
                        for kc in range(KC):
                            fn = lambda e, bank=bank, kc=kc, tb=tb, hb_=hb_: e.matmul(
                                bank[:, :], hb_[:, kc, tb * 128:(tb + 1) * 128], Win[:, kc, 2048:2560], start=(kc == 0), stop=(kc == KC - 1))
                            if kc == KC - 1:
                                vp = cx.op("tensor", fn, s_pe)
                            else:
                                cx.op("tensor", fn)
                        cx.wait("scalar", s_pe, vp)
                        if tb >= 2:
                            cx.wait("scalar", s_v, vvn[tb - 2])
                        va = cx.op("scalar", lambda e, bank=bank, b=b: e.activation(out=gv[b], in_=bank[:, :], func=AF.Gelu_apprx_tanh), s_a)
                        ring.release(bi, s_a, va)
                        cx.wait("vector", s_a, va)
                        v1 = cx.op("vector", lambda e, b=b: e.tensor_tensor(sqv, gv[b], gv[b], ALU.mult), s_v)
                        cx.wait("vector", s_v, v1)
                        v2 = cx.op("vector", lambda e, tb=tb: e.reduce_sum(ss[:, tb:tb + 1], sqv, mybir.AxisListType.X), s_v)
                        cx.wait("scalar", s_v, v2)
                        va2 = cx.op("scalar", lambda e, tb=tb: e.activation(
                            out=sq1[:, tb:tb + 1], in_=ss[:, tb:tb + 1], func=AF.Sqrt, bias=eps_t[:, 0:1], scale=1.0 / 512), s_a)
                        cx.wait("vector", s_a, va2)
                        v3_ = cx.op("vector", lambda e, tb=tb: e.reciprocal(rs[:, tb:tb + 1], sq1[:, tb:tb + 1]), s_v)
                        cx.wait("vector", s_v, v3_)
                        v4 = cx.op("vector", lambda e, tb=tb, b=b: e.scalar_tensor_tensor(
                            vn[:, tb, :], gv[b], rs[:, tb:tb + 1], sgn_bc, ALU.mult, ALU.mult), s_v)
                        vvn.append(v4)
                    vu = []
                    for g in range(4):
                        bank, bi = ring.acquire()
                        for kc in range(KC):
                            fn = lambda e, bank=bank, kc=kc, g=g, hb_=hb_: e.matmul(
                                bank[:, :], Win[:, kc, 1536 + g * 128:1536 + (g + 1) * 128], hb_[:, kc, :], start=(kc == 0), stop=(kc == KC - 1))
                            if kc == KC - 1:
                                vp = cx.op("tensor", fn, s_pe)
                            else:
                                cx.op("tensor", fn)
                        cx.wait("scalar", s_pe, vp)
                        va = cx.op("scalar", lambda e, bank=bank, g=g: e.activation(out=uT[:, g, :], in_=bank[:, :], func=AF.Gelu_apprx_tanh), s_a)
                        ring.release(bi, s_a, va)
                        vu.append(va)
                    for g in range(4):
                        bank, bi = ring.acquire()
                        for tb in range(TB):
                            if g == 0:
                                cx.wait("tensor", s_v, vvn[tb])
                            cx.op("tensor", lambda e, bank=bank, g=g, tb=tb: e.matmul(
                                bank[:, tb * 128:(tb + 1) * 128], vn[:, tb, g * 128:(g + 1) * 128], wsT[:, g, :], start=True, stop=False))
                            fn = lambda e, bank=bank, g=g, tb=tb: e.matmul(
                                bank[:, tb * 128:(tb + 1) * 128], ones_bf[0:1, 0:128], bs_row[0:1, g * 128:(g + 1) * 128], start=False, stop=True)
                            if tb == TB - 1:
                                vp = cx.op("tensor", fn, s_pe)
                            else:
                                cx.op("tensor", fn)
                        cx.wait("vector", s_pe, vp)
                        cx.wait("vector", s_a, vu[g])
                        vg = cx.op("vector", lambda e, bank=bank, g=g: e.tensor_tensor(gT_sb[:, g, :], bank[:, :], uT[:, g, :], ALU.mult), s_v)
                        ring.release(bi, s_v, vg)
                    cx.wait("gpsimd", s_v, vg)
                    cx.dma("gpsimd", gT_d[:, :, g0:g0 + T].rearrange("k p n -> p k n"), gT_sb, s_out)
                pe_tile_end.append(s_pe.n)
                last_stores = s_out.n
            barrier()

        def segments(Q0, n, L):
            nb = L // 128
            units = list(range(Q0 // 64, (Q0 + n) // 64))
            segs = []
            i = 0
            while i < len(units):
                u = units[i]
                if u % 2 == 1 and i + 1 < len(units) and (u + 1) * 64 < L:
                    P, nq = u * 64, 128
                    i += 2
                else:
                    P, nq = u * 64, 64
                    i += 1
                kbs = []
                mc_ = None
                for kb in range(max(0, (P - 64) // 128), min(nb - 1, (P + nq + 63) // 128) + 1):
                    v = P - 64 - 128 * kb
                    if v not in (-128, -64, 0, 64):
                        continue
                    hh = 1 if v < 0 else 0
                    mc = v + 128 * hh
                    if nq == 128 and mc != 0:
                        continue
                    assert mc_ is None or mc_ == mc
                    mc_ = mc
                    kbs.append((kb, hh))
                segs.append((P, nq, kbs, mc_))
            return segs

        def attn_phase():
            cx.top = persist_top
            s_ld = cx.sem("a_ld")
            s_vl = cx.sem("a_vl")
            s_ps = cx.sem("a_ps")
            s_ex = cx.sem("a_ex")
            s_mk = cx.sem("a_mk")
            s_po = cx.sem("a_po")
            s_ac = cx.sem("a_ac")
            s_nm = cx.sem("a_nm")
            qT = v3(cx.alloc(4 * S, BF16), 4)
            kT = v3(cx.alloc(4 * S, BF16), 4)
            Vt = [v3(cx.alloc(48 * 192, BF16), 48), v3(cx.alloc(48 * 192, BF16), 48)]
            acc = [cx.alloc(2048), cx.alloc(2048)]
            rD = cx.alloc(2048)
            attnT = [cx.alloc(2048, BF16), cx.alloc(2048, BF16)]
            NPB = 4
            pT = [cx.alloc(256, BF16) for _ in range(NPB)]
            pTm = [cx.alloc(256, BF16) for _ in range(NPB)]
            ringS = Ring([PS[0], PS[1], PS[2], PS[3]])
            ringO = Ring([PS[4], PS[5], PS[6]])
            LOOK = 2
            MASK_ENG = "gpsimd"
            pending_store = [None]

            def flush_store():
                hp_, s_, q0_, ab_, v4_ = pending_store[0]
                cx.wait("sync", s_nm, v4_)
                at_store[ab_] = cx.dma("sync", aT_d[hp_, :, s_ * S + q0_:s_ * S + q0_ + 2048], attnT[ab_], s_out)
                pending_store[0] = None

            vt_ready = [0, 0]
            seg_i = 0
            ex_hist = []
            mk_hist = []
            po_hist = []
            vl_i = 0
            vt_free = [0, 0]
            at_i = 0
            at_store = [0, 0]
            pe_done_seq = 0
            sts = [(0, 0), (0, 2048), (1, 0), (1, 2048), (2, 1024)]
            if debug and "nst" in debug:
                sts = sts[:debug["nst"]]
            cur_seq = -1
            for (s, q0) in sts:
                if s != cur_seq:
                    cur_seq = s
                    cx.wait("sync", s_po, pe_done_seq)
                    cx.wait("sync", s_ps, s_ps.n)
                    cx.dma("sync", qT, qT_d[:, :, s * S:(s + 1) * S].rearrange("k p n -> p k n"), s_ld)
                    vq = cx.dma("sync", kT, kT_d[:, :, s * S:(s + 1) * S].rearrange("k p n -> p k n"), s_ld)
                    cx.wait("tensor", s_ld, vq)
                for hp in range(4):
                    items = []
                    for pi, (_, dil) in enumerate(PATTERNS):
                        L = S // dil
                        Q0, n = q0 // dil, 2048 // dil
                        kb_lo = max(0, (Q0 - 64) // 128)
                        kb_hi = min(L // 128 - 1, (Q0 + n + 63) // 128)
                        nkb = kb_hi - kb_lo + 1
                        vb = vl_i % 2
                        vl_i += 1
                        first = True
                        for hd in range(2):
                            for r in range(dil):
                                for (P, nq, kbs, mc) in segments(Q0, n, L):
                                    items.append(dict(pi=pi, dil=dil, Q0=Q0, kb_lo=kb_lo, nkb=nkb, vb=vb, hd=hd, r=r,
                                                      P=P, nq=nq, kbs=kbs, mc=mc, vload=first, s=s, hp=hp))
                                    first = False

                    def emit_front(it):
                        nonlocal seg_i
                        dil, r, P, nq, kbs, mc, hd = it["dil"], it["r"], it["P"], it["nq"], it["kbs"], it["mc"], it["hd"]
                        if it["vload"]:
                            vb, nkb, kb_lo = it["vb"], it["nkb"], it["kb_lo"]
                            cx.wait("sync", s_po, vt_free[vb])
                            for rr in range(dil):
                                t0 = it["s"] * S + rr + dil * 128 * kb_lo
                                src_ = V_d[it["hp"], t0:t0 + dil * (128 * nkb - 1) + 1:dil, :].rearrange("(kb p) c -> p kb c", p=128)
                                vvl = cx.dma("sync", Vt[vb][:, rr * nkb:(rr + 1) * nkb, :], src_, s_vl)
                            vt_ready[vb] = vvl
                        it["vready"] = vt_ready[it["vb"]]
                        h = it["hp"] * 2 + hd
                        hb = 64 * hd
                        midx = h * 3 + it["pi"]
                        nk = len(kbs)
                        it["seg"] = seg_i
                        pb = seg_i % NPB
                        it["pb"] = pb
                        bankS, bsi = ringS.acquire()
                        qcols = slice(r + dil * P, r + dil * (P + nq - 1) + 1, dil)
                        for i, (kb, hh) in enumerate(kbs):
                            kcols = slice(r + dil * 128 * kb, r + dil * (128 * kb + 127) + 1, dil)
                            fn = lambda e, bankS=bankS, i=i, nq=nq, kcols=kcols, qcols=qcols, hb=hb, hp=it["hp"]: e.matmul(
                                bankS[:, i * nq:(i + 1) * nq], kT[hb:hb + 64, hp, kcols], qT[hb:hb + 64, hp, qcols], start=True, stop=True)
                            if i == nk - 1:
                                vps = cx.op("tensor", fn, s_ps)
                            else:
                                cx.op("tensor", fn)
                        cx.wait("scalar", s_ps, vps)
                        if seg_i >= NPB:
                            cx.wait("scalar", s_mk, mk_hist[seg_i - NPB])
                        vex = cx.op("scalar", lambda e, bankS=bankS, pb=pb, w=nk * nq: e.activation(
                            out=pT[pb][:, 0:w], in_=bankS[:, 0:w], func=AF.Exp), s_ex)
                        ringS.release(bsi, s_ex, vex)
                        cx.wait(MASK_ENG, s_ex, vex)
                        if seg_i >= NPB:
                            cx.wait(MASK_ENG, s_po, po_hist[seg_i - NPB])
                        if nk == 2:
                            m_ap = Emask3[:, midx, :].rearrange("p (h q) -> p h q", h=2)[:, :, mc:mc + nq]
                            o_ap = pTm[pb][:, 0:2 * nq].rearrange("p (h q) -> p h q", h=2)
                            i_ap = pT[pb][:, 0:2 * nq].rearrange("p (h q) -> p h q", h=2)
                        else:
                            hh = kbs[0][1]
                            m_ap = Emask3[:, midx, hh * 128 + mc:hh * 128 + mc + nq]
                            o_ap = pTm[pb][:, 0:nq]
                            i_ap = pT[pb][:, 0:nq]
                        vmk = cx.op(MASK_ENG, lambda e, o_ap=o_ap, i_ap=i_ap, m_ap=m_ap: e.tensor_tensor(o_ap, i_ap, m_ap, ALU.mult), s_mk, k=1)
                        mk_hist.append(vmk)
                        it["vmk"] = vmk
                        seg_i += 1

                    def emit_back(it):
                        dil, r, P, nq, kbs, hd, vb = it["dil"], it["r"], it["P"], it["nq"], it["kbs"], it["hd"], it["vb"]
                        nk = len(kbs)
                        pb = it["pb"]
                        vcols = slice(0, 128) if hd == 0 else slice(64, 192)
                        bankO, boi = ringO.acquire()
                        cx.wait("tensor", s_vl, it["vready"])
                        cx.wait("tensor", s_mk, it["vmk"])
                        for i, (kb, hh) in enumerate(kbs):
                            blk = r * it["nkb"] + (kb - it["kb_lo"])
                            fn = lambda e, bankO=bankO, i=i, nq=nq, blk=blk, vb=vb, vcols=vcols, pb=pb, nk=nk: e.matmul(
                                bankO[:, 0:nq], Vt[vb][:, blk, vcols], pTm[pb][:, i * nq:(i + 1) * nq], start=(i == 0), stop=(i == nk - 1))
                            if i == nk - 1:
                                vpo = cx.op("tensor", fn, s_po)
                            else:
                                cx.op("tensor", fn)
                        po_hist.append(vpo)
                        vt_free[vb] = vpo
                        c0 = r + dil * (P - it["Q0"])
                        dst = acc[hd][:, c0:c0 + dil * (nq - 1) + 1:dil]
                        cx.wait("vector", s_po, vpo)
                        if it["pi"] == 0:
                            vac = cx.op("vector", lambda e, dst=dst, bankO=bankO, nq=nq: e.tensor_copy(dst, bankO[:, 0:nq]), s_ac)
                        else:
                            vac = cx.op("vector", lambda e, dst=dst, bankO=bankO, nq=nq: e.tensor_tensor(dst, bankO[:, 0:nq], dst, ALU.add), s_ac)
                        ringO.release(boi, s_ac, vac)

                    for idx in range(len(items) + LOOK):
                        if idx < len(items):
                            emit_front(items[idx])
                        if idx - LOOK >= 0:
                            emit_back(items[idx - LOOK])
                    ab = at_i % 2
                    at_i += 1
                    cx.wait("vector", s_ac, s_ac.n)
                    cx.wait("vector", s_out, at_store[ab])
                    v1 = cx.op("vector", lambda e: e.tensor_copy(rD[0:64, :], acc[0][64:128, :]), s_nm)
                    v2 = cx.op("vector", lambda e: e.tensor_copy(rD[64:128, :], acc[1][0:64, :]), s_nm)
                    cx.wait("vector", s_nm, v2)
                    v2b = cx.op("vector", lambda e: e.reciprocal(rD, rD), s_nm)
                    cx.wait("vector", s_nm, v2b)
                    v3_ = cx.op("vector", lambda e, ab=ab: e.tensor_tensor(attnT[ab][0:64, :], acc[0][0:64, :], rD[0:64, :], ALU.mult), s_nm)
                    v4 = cx.op("vector", lambda e, ab=ab: e.tensor_tensor(attnT[ab][64:128, :], acc[1][64:128, :], rD[64:128, :], ALU.mult), s_nm)
                    if pending_store[0] is not None:
                        flush_store()
                    pending_store[0] = (hp, s, q0, ab, v4)
                if pending_store[0] is not None:
                    flush_store()
                pe_done_seq = s_po.n
            barrier()

        def mix_out_phase():
            cx.top = persist_top
            s_w = cx.sem("o_w")
            s_in = cx.sem("o_in")
            s_pe = cx.sem("o_pe")
            s_res = cx.sem("o_res")
            s_ca = cx.sem("o_ca")
            s_st = cx.sem("o_st")
            s_sq = cx.sem("o_sq")
            s_rs = cx.sem("o_rs")
            s_tA = cx.sem("o_tA")
            s_hT = cx.sem("o_hT")
            Wout = v3(cx.alloc(KC * D, BF16), KC)
            eps_t = cx.alloc(1)
            xT = v3(cx.alloc(KC * T), KC)
            mixT = v3(cx.alloc(KC * T, BF16), KC)
            hT = v3(cx.alloc(KC * T, BF16), KC)
            xsq = v3(cx.alloc(KC * T, BF16), KC)
            rstd = cx.alloc(T)
            sqt = cx.alloc(T)
            tmpA = [cx.alloc(T), cx.alloc(T)]
            woutv = wout_d.rearrange("(kc p) f -> p kc f", p=128)
            for kc in range(KC):
                cx.dma("gpsimd", Wout[:, kc, :], woutv[:, kc, :], s_w)
            cx.wait("tensor", s_w, s_w.n)
            cx.op("vector", lambda e: e.memset(eps_t, EPS))
            ring = Ring([PS[0], PS[1], PS[2]])
            tiles = [(s, ti) for s in range(NSEQ) for ti in range(S // T) if s < 2 or 2 <= ti < 6]
            if debug and "ntiles" in debug:
                tiles = tiles[:debug["ntiles"]]
            hT_hist = []
            last_x = 0
            last_h = 0
            for (s, ti) in tiles:
                g0 = s * S + ti * T
                cx.wait("sync", s_out, max(last_x, last_h))
                cx.wait("sync", s_pe, s_pe.n)
                cx.wait("sync", s_hT, s_hT.n)
                cx.dma("sync", xT, x1T_d[:, :, g0:g0 + T].rearrange("k p n -> p k n"), s_in)
                cx.dma("sync", mixT[:, 0:4, :], aT_d[:, :, g0:g0 + T].rearrange("k p n -> p k n"), s_in)
                vin = cx.dma("sync", mixT[:, 4:8, :], gT_d[:, :, g0:g0 + T].rearrange("k p n -> p k n"), s_in)
                cx.wait("tensor", s_in, vin)
                cx.wait("vector", s_in, vin)
                res_vals = []
                for dc in range(KC):
                    bank, bi = ring.acquire()
                    for kc in range(KC):
                        fn = lambda e, bank=bank, kc=kc, dc=dc: e.matmul(
                            bank[:, :], Wout[:, kc, dc * 128:(dc + 1) * 128], mixT[:, kc, :], start=(kc == 0), stop=(kc == KC - 1))
                        if kc == KC - 1:
                            vp = cx.op("tensor", fn, s_pe)
                        else:
                            cx.op("tensor", fn)
                    cx.wait("vector", s_pe, vp)
                    vres = cx.op("vector", lambda e, bank=bank, dc=dc, s=s: e.scalar_tensor_tensor(
                        xT[:, dc, :], bank[:, :], G_(1, dc, s), xT[:, dc, :], ALU.mult, ALU.add), s_res)
                    ring.release(bi, s_res, vres)
                    res_vals.append(vres)
                ca_vals = []
                for kc in range(KC):
                    cx.wait("scalar", s_res, res_vals[kc])
                    ca_vals.append(cx.op("scalar", lambda e, kc=kc: e.activation(out=xsq[:, kc, :], in_=xT[:, kc, :], func=AF.Square), s_ca))
                cx.wait("gpsimd", s_res, res_vals[-1])
                last_x = cx.dma("gpsimd", x2T_d[:, :, g0:g0 + T].rearrange("k p n -> p k n"), xT, s_out)
                cx.wait("tensor", s_sq, s_sq.n)
                for kc in range(KC):
                    cx.wait("tensor", s_ca, ca_vals[kc])
                    fn = lambda e, kc=kc: e.matmul(PS[3][:, :], ones_bf, xsq[:, kc, :], start=(kc == 0), stop=(kc == KC - 1))
                    if kc == KC - 1:
                        vst = cx.op("tensor", fn, s_st)
                    else:
                        cx.op("tensor", fn)
                cx.wait("scalar", s_st, vst)
                vsq = cx.op("scalar", lambda e: e.activation(out=sqt, in_=PS[3][:, :], func=AF.Sqrt, bias=eps_t[:, 0:1], scale=1.0 / D), s_sq)
                cx.wait("vector", s_sq, vsq)
                vrs = cx.op("vector", lambda e: e.reciprocal(rstd, sqt), s_rs)
                cx.wait("vector", s_rs, vrs)
                cx.wait("scalar", s_out, last_h)
                for kc in range(KC):
                    b = kc % 2
                    if len(hT_hist) >= 2:
                        cx.wait("vector", s_hT, hT_hist[-2])
                    vt = cx.op("vector", lambda e, kc=kc, b=b, s=s: e.scalar_tensor_tensor(
                        tmpA[b], xT[:, kc, :], A_(2, kc, s), rstd, ALU.mult, ALU.mult), s_tA)
                    cx.wait("scalar", s_tA, vt)
                    va = cx.op("scalar", lambda e, kc=kc, b=b, s=s: e.activation(
                        out=hT[:, kc, :], in_=tmpA[b], func=AF.Identity, bias=B_(2, kc, s), scale=1.0), s_hT)
                    hT_hist.append(va)
                cx.wait("gpsimd", s_hT, va)
                last_h = cx.dma("gpsimd", h3T_d[:, :, g0:g0 + T].rearrange("k p n -> p k n"), hT, s_out)
            barrier()

        if stop_after >= 1:
            ffn_phase(0)
        if stop_after >= 2:
            mix_in_phase()
        if stop_after >= 3:
            attn_phase()
        if stop_after >= 4:
            mix_out_phase()
        if stop_after >= 5:
            ffn_phase(1)

        barrier()
        cx.emit_all()
    return nc


def _core_inputs(core, inp):
    xs = inp["x_sample"]
    xp = inp["x_prompt"]
    x = np.zeros((NSEQ, S, D), np.float32)
    x[0] = xs[2 * core]
    x[1] = xs[2 * core + 1]
    pb, qd = core // 4, core % 4
    lo = qd * 2048 - 1024
    valid = np.ones((NSEQ, S), np.float32)
    a, b = max(lo, 0), min(lo + S, 8192)
    x[2, a - lo:b - lo] = xp[pb, a:b]
    valid[2, :] = 0.0
    valid[2, a - lo:b - lo] = 1.0
    c3 = np.stack([inp["c_sample"][2 * core], inp["c_sample"][2 * core + 1], inp["c_prompt"][pb]], 0)
    cT = np.ascontiguousarray(c3.reshape(3, KC, 128).transpose(2, 1, 0)).reshape(128, KC * 3)
    return x.reshape(NTOK, D), cT, np.ascontiguousarray(valid.reshape(NTOK // 128, 128).T)


def _shared_inputs(inp):
    def pcol(v):
        return np.ascontiguousarray(np.asarray(v, np.float32).reshape(-1, 128).T)
    pvec = np.concatenate([pcol(inp["ada_b"][0]), pcol(inp["ffn1_norm"][0]), pcol(inp["mix_norm"][0]),
                           pcol(inp["ffn2_norm"][0]), pcol(inp["final_norm"])], axis=1)
    kp = np.arange(128)[:, None]
    col = np.arange(256)[None, :]
    hh, q = col // 128, col % 128
    rel = 128 * hh + kp - q - 64
    sh = {
        "pvec": pvec.astype(np.float32),
        "ident": np.eye(128, dtype=np.float32),
        "absrel": np.abs(rel).astype(np.float32),
        "band": (np.abs(rel) <= 64).astype(np.float32),
        "ada_w": np.ascontiguousarray(inp["ada_w"][0]),
        "wg1": np.ascontiguousarray(inp["ffn1_w_gate"][0]), "wu1": np.ascontiguousarray(inp["ffn1_w_up"][0]),
        "wd1": np.ascontiguousarray(inp["ffn1_w_down"][0]),
        "wg2": np.ascontiguousarray(inp["ffn2_w_gate"][0]), "wu2": np.ascontiguousarray(inp["ffn2_w_up"][0]),
        "wd2": np.ascontiguousarray(inp["ffn2_w_down"][0]),
        "w_in": np.ascontiguousarray(inp["w_in"][0]),
        "wsT": np.ascontiguousarray(inp["sgu_w"][0].transpose(2, 0, 1)).reshape(128, 512),
        "sgu_b": np.ascontiguousarray(inp["sgu_b"][0]).reshape(1, 512),
        "sgu_norm": np.ascontiguousarray(inp["sgu_norm"][0]).reshape(1, 512),
        "w_out": np.ascontiguousarray(inp["w_out"][0]),
    }
    return sh


def make_in_maps(inp):
    inp = {k: np.asarray(v) for k, v in inp.items()}
    sh = _shared_inputs(inp)
    maps = []
    for core in range(N_CORES):
        x, cT, valid = _core_inputs(core, inp)
        m = dict(sh)
        m["x"] = x
        m["cT"] = cT
        m["valid"] = valid
        maps.append(m)
    return maps


def kernel(**inputs):
    maps = make_in_maps(inputs)
    nc = build_program()
    res = run_bass_kernel_spmd(nc, maps, core_ids=list(range(N_CORES)))
    ys = np.empty((16, S, D), np.float32)
    yp = np.empty((2, 8192, D), np.float32)
    for core in range(N_CORES):
        y = res.results[core]["y"]
        ys[2 * core] = y[0:S]
        ys[2 * core + 1] = y[S:2 * S]
        pb, qd = core // 4, core % 4
        yp[pb, qd * 2048:(qd + 1) * 2048] = y[2 * S:]
    return (yp, ys)
```

```python
import numpy as np
from contextlib import ExitStack
import concourse.bass as bass
import concourse.mybir as mybir
from concourse.bass_utils import run_bass_kernel_spmd

F32 = mybir.dt.float32
BF16 = mybir.dt.bfloat16
AF = mybir.ActivationFunctionType
ALU = mybir.AluOpType

D = 1024
KC = 8
DFF = 2816
FC = 22
S = 4096
NSEQ = 3
NTOK = NSEQ * S
T = 512
TB = 4
NQTOK = 2 * S + 2048
EPS = 1e-6
HEADS = 8
PATTERNS = ((128, 1), (512, 4), (2048, 16))
N_CORES = 8
ARENA_F32 = 53200

ENGINES = ("tensor", "vector", "scalar", "gpsimd", "sync")


class Sem:
    def __init__(self, h):
        self.h = h
        self.n = 0


class Ctx:
    def __init__(self, nc, es):
        self.nc = nc
        self.es = es
        self.q = {e: [] for e in ENGINES}
        self.nsem = 0
        self.arena = es.enter_context(nc.sbuf_tensor("arena", [128, ARENA_F32], F32))
        self.top = 0
        self.psum = [es.enter_context(nc.psum_tensor("ps%d" % i, [128, 512], F32)) for i in range(8)]

    def sem(self, name):
        h = self.es.enter_context(self.nc.semaphore(name))
        self.nsem += 1
        return Sem(h)

    def alloc(self, cols, dtype=F32):
        if dtype == BF16:
            ncol32 = (cols + 1) // 2
        else:
            ncol32 = cols
        a = self.top
        self.top += ncol32
        assert self.top <= ARENA_F32, ("SBUF arena overflow", self.top)
        ap = self.arena[:, a:a + ncol32]
        if dtype == BF16:
            ap = ap.bitcast(BF16)
        return ap

    def op(self, eng, fn, sig=None, k=None):
        if sig is None:
            self.q[eng].append(fn)
            return None
        if k is None:
            k = 16 if eng in ("sync", "gpsimd_dma") else 1
        sig.n += k
        h = sig.h
        self.q[eng].append(lambda e, fn=fn, h=h, k=k: fn(e).then_inc(h, k))
        return sig.n

    def dma(self, eng, out, in_, sig):
        sig.n += 16
        h = sig.h
        self.q[eng].append(lambda e, out=out, in_=in_, h=h: e.dma_start(out=out, in_=in_).then_inc(h, 16))
        return sig.n

    def wait(self, eng, sem, val):
        if val is None or val <= 0:
            return
        h = sem.h
        self.q[eng].append(lambda e, h=h, val=val: e.wait_ge(h, val))

    def emit_all(self):
        nc = self.nc
        with nc.Block() as blk:
            for name in ENGINES:
                ops = self.q[name]

                def body(e, ops=ops):
                    for f in ops:
                        f(e)
                getattr(blk, name)(body)


def v3(ap, a):
    return ap.rearrange("p (a b) -> p a b", a=a)


def build_program(debug=None):
    nc = bass.Bass("TRN2", target_bir_lowering=False)
    dt = nc.dram_tensor

    def din(name, shape, dtype=F32):
        return dt(name, list(shape), dtype, kind="ExternalInput").ap()

    x_d = din("x", [NTOK, D])
    cT_d = din("cT", [128, KC * NSEQ])
    pvec_d = din("pvec", [128, 104])
    valid_d = din("valid", [128, NTOK // 128])
    ident_d = din("ident", [128, 128])
    absrel_d = din("absrel", [128, 256])
    band_d = din("band", [128, 256])
    adaw_d = din("ada_w", [D, 9 * D])
    wg_d = [din("wg1", [D, DFF]), din("wg2", [D, DFF])]
    wu_d = [din("wu1", [D, DFF]), din("wu2", [D, DFF])]
    wd_d = [din("wd1", [DFF, D]), din("wd2", [DFF, D])]
    win_d = din("w_in", [D, 2560])
    wsT_d = din("wsT", [128, 512])
    sgub_d = din("sgu_b", [1, 512])
    sgun_d = din("sgu_norm", [1, 512])
    wout_d = din("w_out", [D, D])
    y_d = dt("y", [NQTOK, D], F32, kind="ExternalOutput").ap()

    skind = "ExternalOutput" if (debug and not debug.get("internal")) else "Internal"
    x1T_d = dt("x1T", [KC, 128, NTOK], F32, kind=skind).ap()
    h2T_d = dt("h2T", [KC, 128, NTOK], BF16, kind=skind).ap()
    qT_d = dt("qT", [4, 128, NTOK], BF16, kind=skind).ap()
    kT_d = dt("kT", [4, 128, NTOK], BF16, kind=skind).ap()
    V_d = dt("V", [4, NTOK, 192], BF16, kind=skind).ap()
    gT_d = dt("gT", [4, 128, NTOK], BF16, kind=skind).ap()
    aT_d = dt("aT", [4, 128, NTOK], BF16, kind=skind).ap()
    x2T_d = dt("x2T", [KC, 128, NTOK], F32, kind=skind).ap()
    h3T_d = dt("h3T", [KC, 128, NTOK], BF16, kind=skind).ap()

    stop_after = debug.get("stop_after", 99) if debug else 99

    with ExitStack() as es:
        cx = Ctx(nc, es)
        PS = cx.psum

        ident = cx.alloc(128)
        ones_bf = cx.alloc(128, BF16)
        pvec = cx.alloc(104)
        valid = cx.alloc(NTOK // 128)
        modT = cx.alloc(72 * 3)
        Amod = cx.alloc(3 * 8 * 3)
        Gmod = cx.alloc(3 * 8 * 3)
        Emask = cx.alloc(24 * 256, BF16)
        persist_top = cx.top

        modT3 = v3(modT, 72)
        Amod4 = Amod.rearrange("p (l k s) -> p l k s", l=3, k=8)
        Gmod4 = Gmod.rearrange("p (l k s) -> p l k s", l=3, k=8)
        Emask3 = v3(Emask, 24)

        def A_(l, kc, s):
            return Amod4[:, l, kc, s:s + 1]

        def B_(l, kc, s):
            return modT3[:, l * 24 + kc, s:s + 1]

        def G_(l, kc, s):
            return Gmod4[:, l, kc, s:s + 1]

        s_out = cx.sem("s_out")

        def barrier():
            for e in ENGINES:
                cx.wait(e, s_out, s_out.n)

        s_ld = cx.sem("p0_ld")
        s_c = cx.sem("p0_c")
        s_aw = cx.sem("p0_aw")
        s_awf = cx.sem("p0_awf")
        s_v = cx.sem("p0_v")
        s_a = cx.sem("p0_a")

        cx.top = ARENA_F32 - 9400
        cT = cx.alloc(24)
        scT = cx.alloc(24)
        absrel = cx.alloc(256)
        band = cx.alloc(256)
        etmp = [cx.alloc(256), cx.alloc(256)]
        tmp24 = cx.alloc(24)
        awbuf = [cx.alloc(8 * 512), cx.alloc(8 * 512)]

        for dst, src in ((ident, ident_d), (pvec, pvec_d), (valid, valid_d), (cT, cT_d),
                         (absrel, absrel_d), (band, band_d)):
            cx.dma("sync", dst, src, s_ld)
        n_ld = s_ld.n
        cx.wait("scalar", s_ld, n_ld)
        cx.wait("vector", s_ld, n_ld)
        cx.op("vector", lambda e: e.memset(ones_bf, 1.0))
        v_sc = cx.op("scalar", lambda e: e.activation(out=scT, in_=cT, func=AF.Silu), s_c)
        cx.wait("tensor", s_c, v_sc)
        adaw_v = adaw_d.rearrange("(kc p) f -> p kc f", p=128)
        for ch in range(18):
            b = ch % 2
            if ch >= 2:
                cx.wait("sync", s_awf, ch - 1)
            vld = cx.dma("sync", v3(awbuf[b], 8), adaw_v[:, :, ch * 512:(ch + 1) * 512], s_aw)
            cx.wait("tensor", s_aw, vld)
            for fl in range(4):
                col = (ch * 4 + fl) * 3
                for kc in range(KC):
                    fn = (lambda e, b=b, fl=fl, kc=kc, col=col: e.matmul(
                        PS[0][:, col:col + 3], v3(awbuf[b], 8)[:, kc, fl * 128:(fl + 1) * 128],
                        scT[:, kc * 3:(kc + 1) * 3], start=(kc == 0), stop=(kc == KC - 1)))
                    if fl == 3 and kc == KC - 1:
                        cx.op("tensor", fn, s_awf)
                    else:
                        cx.op("tensor", fn)
        cx.wait("vector", s_awf, 18)
        psm3 = PS[0][:, 0:216].rearrange("p (a b) -> p a b", b=3)
        for s in range(3):
            vm = cx.op("vector", lambda e, s=s: e.tensor_tensor(modT3[:, :, s], psm3[:, :, s], pvec[:, 0:72], ALU.add), s_v)
        cx.wait("vector", s_v, vm)
        for l in range(3):
            gl = pvec[:, 72 + 8 * l:80 + 8 * l]
            for s in range(3):
                v1 = cx.op("vector", lambda e, l=l, s=s: e.tensor_scalar(
                    tmp24[:, 0:8], modT3[:, l * 24 + 8:l * 24 + 16, s], 1.0, None, ALU.add), s_v)
                cx.wait("vector", s_v, v1)
                v2 = cx.op("vector", lambda e, l=l, s=s, gl=gl: e.tensor_tensor(
                    Amod4[:, l, :, s], tmp24[:, 0:8], gl, ALU.mult), s_v)
                cx.wait("vector", s_v, v2)
                rw = 1.0 if l == 1 else 0.5
                cx.op("vector", lambda e, l=l, s=s, rw=rw: e.tensor_scalar(
                    Gmod4[:, l, :, s], modT3[:, l * 24 + 16:l * 24 + 24, s], rw, None, ALU.mult), s_v)
        i = 0
        for h in range(HEADS):
            slope = 2.0 ** (-8.0 * (h + 1) / HEADS)
            for pi, (_, dil) in enumerate(PATTERNS):
                b = i % 2
                if i >= 2:
                    cx.wait("scalar", s_v, vmask[b])
                va = cx.op("scalar", lambda e, b=b, sc=-slope * dil: e.activation(
                    out=etmp[b], in_=absrel, func=AF.Exp, scale=sc), s_a)
                cx.wait("vector", s_a, va)
                vv = cx.op("vector", lambda e, b=b, idx=h * 3 + pi: e.tensor_tensor(
                    Emask3[:, idx, :], etmp[b], band, ALU.mult), s_v)
                if i == 0:
                    vmask = [0, 0]
                vmask[b] = vv
                i += 1
        v_p0 = s_v.n
        for e in ENGINES:
            if e == "gpsimd":
                continue
            cx.wait(e, s_v, v_p0)
            cx.wait(e, s_a, s_a.n)

        TF = 256
        TBF = 2
        W_BASE = persist_top
        WCOLS = KC * DFF // 2
        FFN_TILE_BASE = W_BASE + 3 * WCOLS

        def ffn_weight_aps():
            a = W_BASE
            Wg = v3(cx.arena[:, a:a + WCOLS].bitcast(BF16), KC)
            Wu = v3(cx.arena[:, a + WCOLS:a + 2 * WCOLS].bitcast(BF16), KC)
            Wd = v3(cx.arena[:, a + 2 * WCOLS:a + 3 * WCOLS].bitcast(BF16), FC)
            return Wg, Wu, Wd

        ffn_wsem = {}

        def ffn_issue_weights(mode, parts):
            if mode not in ffn_wsem:
                ffn_wsem[mode] = (cx.sem("f%d_wgu" % mode), cx.sem("f%d_wd" % mode))
            s_wgu, s_wd = ffn_wsem[mode]
            Wg, Wu, Wd = ffn_weight_aps()
            if "gu" in parts:
                wgv = wg_d[mode].rearrange("(kc p) f -> p kc f", p=128)
                wuv = wu_d[mode].rearrange("(kc p) f -> p kc f", p=128)
                for kc in range(KC):
                    cx.dma("gpsimd", Wg[:, kc, :], wgv[:, kc, :], s_wgu)
                    cx.dma("gpsimd", Wu[:, kc, :], wuv[:, kc, :], s_wgu)
            if "d" in parts:
                wdv = wd_d[mode].rearrange("(j p) d -> p j d", p=128)
                for j0 in range(0, FC, 2):
                    cx.dma("gpsimd", Wd[:, j0:j0 + 2, :], wdv[:, j0:j0 + 2, :], s_wd)

        ffn_issue_weights(0, ("gu", "d"))

        def ffn_phase(mode, preloaded=()):
            cx.top = FFN_TILE_BASE
            pre = "f%d_" % mode
            for part in ("gu", "d"):
                if part not in preloaded:
                    ffn_issue_weights(mode, (part,))
            s_wgu, s_wd = ffn_wsem[mode]
            names = ["xin", "trp", "cv", "ca", "st1", "sq1", "rs1", "st2", "sq2", "rs2", "tA", "hT",
                     "g", "u", "sg", "hid", "dn", "res", "cb", "yt", "ye"]
            sm = {n: cx.sem(pre + n) for n in names}
            Wg, Wu, Wd = ffn_weight_aps()

            xT = [v3(cx.alloc(KC * TF), KC), v3(cx.alloc(KC * TF), KC)]
            hT = [v3(cx.alloc(KC * TF, BF16), KC), v3(cx.alloc(KC * TF, BF16), KC)]
            hid = v3(cx.alloc(FC * TF, BF16), FC)
            xsqA = v3(cx.alloc(KC * TF, BF16), KC)
            xsqB = v3(cx.alloc(KC * TF, BF16), KC)
            tok = cx.alloc(TBF * D).rearrange("p (a b) -> p a b", a=TBF)
            h2buf = xsqA if mode == 0 else None
            tmpA = [cx.alloc(TF), cx.alloc(TF)]
            sg = [cx.alloc(TF), cx.alloc(TF)]
            rstd1 = rstd2 = cx.alloc(TF)
            sqt1 = sqt2 = cx.alloc(TF)
            last_rs = [None]
            eps_t = cx.alloc(1)
            cx.op("vector", lambda e: e.memset(eps_t, EPS))

            l_in = 0 if mode == 0 else 2
            if mode == 0:
                tiles = [(s, s * S + k * TF) for s in range(NSEQ) for k in range(S // TF)]
            else:
                tiles = [(s, s * S + k * TF) for s in range(NSEQ) for k in range(S // TF) if s < 2 or 4 <= k < 12]
            if debug and "ntiles" in debug:
                tiles = tiles[:debug["ntiles"]]
            n = len(tiles)
            st = [dict() for _ in range(n)]
            ring = Ring([PS[0], PS[1]])
            gu_hist = {"sg": [], "hid": []}
            tA_hist = []

            def load_in(i):
                s, g0 = tiles[i]
                if mode == 0:
                    if i >= 1:
                        cx.wait("sync", sm["trp"], st[i - 1]["tr_done"])
                    st[i]["xin"] = cx.dma("sync", tok, x_d[g0:g0 + TF, :].rearrange("(tb p) d -> p tb d", p=128), sm["xin"])
                else:
                    if i >= 2:
                        cx.wait("sync", sm["yt"], st[i - 2]["ytr_done"])
                        cx.wait("sync", sm["u"], st[i - 2]["gu_done"])
                    cx.dma("sync", xT[i % 2], x2T_d[:, :, g0:g0 + TF].rearrange("k p n -> p k n"), sm["xin"])
                    st[i]["xin"] = cx.dma("sync", hT[i % 2], h3T_d[:, :, g0:g0 + TF].rearrange("k p n -> p k n"), sm["xin"])

            def transposes_in(i):
                xb = xT[i % 2]
                cx.wait("tensor", sm["xin"], st[i]["xin"])
                if i >= 2:
                    cx.wait("vector", s_out, st[i - 2]["x_store"])
                sq = []
                hs = [st[k]["h_store"] for k in range(i) if "h_store" in st[k]]
                if hs:
                    cx.wait("scalar", s_out, hs[-1])
                for kc in range(KC):
                    bank, bi = ring.acquire()
                    for tb in range(TBF):
                        fn = lambda e, bank=bank, tb=tb, kc=kc: e.transpose(
                            bank[:, tb * 128:(tb + 1) * 128], tok[:, tb, kc * 128:(kc + 1) * 128], ident)
                        if tb == TBF - 1:
                            vtr = cx.op("tensor", fn, sm["trp"])
                        else:
                            cx.op("tensor", fn)
                    cx.wait("vector", sm["trp"], vtr)
                    vcv = cx.op("vector", lambda e, bank=bank, kc=kc, xb=xb: e.tensor_copy(xb[:, kc, :], bank[:, 0:TF]), sm["cv"])
                    ring.release(bi, sm["cv"], vcv)
                    cx.wait("scalar", sm["cv"], vcv)
                    sq.append(cx.op("scalar", lambda e, kc=kc, xb=xb: e.activation(
                        out=xsqA[:, kc, :], in_=xb[:, kc, :], func=AF.Square), sm["ca"]))
                st[i]["tr_done"] = vtr
                st[i]["sqA"] = sq

            def stats_norm(i, which):
                s, g0 = tiles[i]
                xb = xT[i % 2]
                if which == 1:
                    xsq, bankst, s_st, s_sq, s_rs, rstd, sqt, sq_sem, sq_vals = xsqA, PS[2], sm["st1"], sm["sq1"], sm["rs1"], rstd1, sqt1, sm["ca"], st[i]["sqA"]
                else:
                    xsq, bankst, s_st, s_sq, s_rs, rstd, sqt, sq_sem, sq_vals = xsqB, PS[7], sm["st2"], sm["sq2"], sm["rs2"], rstd2, sqt2, sm["cb"], st[i]["sqB"]
                cx.wait("tensor", s_sq, s_sq.n)
                for kc in range(KC):
                    cx.wait("tensor", sq_sem, sq_vals[kc])
                    fn = lambda e, kc=kc, xsq=xsq, bankst=bankst: e.matmul(bankst[:, 0:TF], ones_bf, xsq[:, kc, :], start=(kc == 0), stop=(kc == KC - 1))
                    if kc == KC - 1:
                        vst = cx.op("tensor", fn, s_st)
                    else:
                        cx.op("tensor", fn)
                cx.wait("scalar", s_st, vst)
                if last_rs[0] is not None:
                    cx.wait("scalar", last_rs[0][0], last_rs[0][1])
                vsq = cx.op("scalar", lambda e, sqt=sqt, bankst=bankst: e.activation(
                    out=sqt, in_=bankst[:, 0:TF], func=AF.Sqrt, bias=eps_t[:, 0:1], scale=1.0 / D), s_sq)
                cx.wait("vector", s_sq, vsq)
                vrs = cx.op("vector", lambda e, rstd=rstd, sqt=sqt: e.reciprocal(rstd, sqt), s_rs)
                last_rs[0] = (s_rs, vrs)
                cx.wait("vector", s_rs, vrs)
                if which == 3:
                    vy = []
                    for kc in range(KC):
                        vy.append(cx.op("vector", lambda e, kc=kc, xb=xb, rstd=rstd: e.scalar_tensor_tensor(
                            xb[:, kc, :], xb[:, kc, :], pvec[:, 96 + kc:97 + kc], rstd, ALU.mult, ALU.mult), sm["hT"]))
                    st[i]["y_ready"] = vy
                    return
                l = 0 if which == 1 else 1
                dstb = hT[i % 2] if which == 1 else h2buf
                if which == 2 and i >= 1:
                    cx.wait("scalar", s_out, st[i - 1]["h_store"])
                vh = []
                for kc in range(KC):
                    b = len(tA_hist) % 2
                    if len(tA_hist) >= 2:
                        cx.wait("vector", sm["hT"], tA_hist[-2])
                    vt = cx.op("vector", lambda e, kc=kc, b=b, xb=xb, rstd=rstd, l=l, s=s: e.scalar_tensor_tensor(
                        tmpA[b], xb[:, kc, :], A_(l, kc, s), rstd, ALU.mult, ALU.mult), sm["tA"])
                    cx.wait("scalar", sm["tA"], vt)
                    va = cx.op("scalar", lambda e, kc=kc, b=b, dstb=dstb, l=l, s=s: e.activation(
                        out=dstb[:, kc, :], in_=tmpA[b], func=AF.Identity, bias=B_(l, kc, s), scale=1.0), sm["hT"])
                    tA_hist.append(va)
                    vh.append(va)
                if which == 1:
                    st[i]["h_ready"] = vh
                else:
                    cx.wait("gpsimd", sm["hT"], vh[-1])
                    st[i]["h_store"] = cx.dma("gpsimd", h2T_d[:, :, g0:g0 + TF].rearrange("k p n -> p k n"), h2buf, s_out)

            def out_transposes(i):
                xb = xT[i % 2]
                if i >= 1:
                    cx.wait("vector", s_out, st[i - 1]["y_store"])
                for tb in range(TBF):
                    for half in range(2):
                        bank, bi = ring.acquire()
                        for k4 in range(4):
                            kc = half * 4 + k4
                            if tb == 0:
                                cx.wait("tensor", sm["hT"], st[i]["y_ready"][kc])
                            fn = lambda e, bank=bank, k4=k4, kc=kc, tb=tb, xb=xb: e.transpose(
                                bank[:, k4 * 128:(k4 + 1) * 128], xb[:, kc, tb * 128:(tb + 1) * 128], ident)
                            if k4 == 3:
                                vtr = cx.op("tensor", fn, sm["yt"])
                            else:
                                cx.op("tensor", fn)
                        cx.wait("vector", sm["yt"], vtr)
                        vye = cx.op("vector", lambda e, bank=bank, tb=tb, half=half: e.tensor_copy(
                            tok[:, tb, half * 512:(half + 1) * 512], bank[:, :]), sm["ye"])
                        ring.release(bi, sm["ye"], vye)
                st[i]["ytr_done"] = vtr
                cx.wait("gpsimd", sm["ye"], vye)
                r0 = i * TF
                st[i]["y_store"] = cx.dma("gpsimd", y_d[r0:r0 + TF, :].rearrange("(tb p) d -> p tb d", p=128), tok, s_out)

            def gate_up(i, hooks):
                hb = hT[i % 2]
                if i == 0:
                    cx.wait("tensor", s_wgu, s_wgu.n)
                if mode == 1:
                    cx.wait("tensor", sm["xin"], st[i]["xin"])
                for j in range(FC):
                    gi = len(gu_hist["sg"])
                    pg = PS[3 + gi % 2]
                    pu = PS[5 + gi % 2]
                    if gi >= 2:
                        cx.wait("tensor", sm["sg"], gu_hist["sg"][gi - 2])
                    for kc in range(KC):
                        if j == 0 and mode == 0:
                            cx.wait("tensor", sm["hT"], st[i]["h_ready"][kc])
                        fn = lambda e, pg=pg, kc=kc, j=j, hb=hb: e.matmul(
                            pg[:, 0:TF], Wg[:, kc, j * 128:(j + 1) * 128], hb[:, kc, :], start=(kc == 0), stop=(kc == KC - 1))
                        if kc == KC - 1:
                            vg = cx.op("tensor", fn, sm["g"])
                        else:
                            cx.op("tensor", fn)
                    if gi >= 2:
                        cx.wait("tensor", sm["hid"], gu_hist["hid"][gi - 2])
                    for kc in range(KC):
                        fn = lambda e, pu=pu, kc=kc, j=j, hb=hb: e.matmul(
                            pu[:, 0:TF], Wu[:, kc, j * 128:(j + 1) * 128], hb[:, kc, :], start=(kc == 0), stop=(kc == KC - 1))
                        if kc == KC - 1:
                            vu = cx.op("tensor", fn, sm["u"])
                        else:
                            cx.op("tensor", fn)
                    b = gi % 2
                    cx.wait("scalar", sm["g"], vg)
                    if gi >= 2:
                        cx.wait("scalar", sm["hid"], gu_hist["hid"][gi - 2])
                    vsg = cx.op("scalar", lambda e, pg=pg, b=b: e.activation(out=sg[b], in_=pg[:, 0:TF], func=AF.Silu), sm["sg"])
                    gu_hist["sg"].append(vsg)
                    cx.wait("vector", sm["sg"], vsg)
                    cx.wait("vector", sm["u"], vu)
                    vhid = cx.op("vector", lambda e, pu=pu, b=b, j=j: e.tensor_tensor(hid[:, j, :], pu[:, 0:TF], sg[b], ALU.mult), sm["hid"])
                    gu_hist["hid"].append(vhid)
                    for f in hooks.get(j, ()):
                        f()
                st[i]["gu_done"] = vu

            def down(i):
                s, g0 = tiles[i]
                xb = xT[i % 2]
                if i == 0:
                    cx.wait("tensor", s_wd, s_wd.n)
                if mode == 1:
                    cx.wait("vector", sm["xin"], st[i]["xin"])
                sq = []
                for dc in range(KC):
                    bank, bi = ring.acquire()
                    for j in range(FC):
                        if dc == 0:
                            cx.wait("tensor", sm["hid"], gu_hist["hid"][len(gu_hist["hid"]) - FC + j])
                        fn = lambda e, bank=bank, j=j, dc=dc: e.matmul(
                            bank[:, 0:TF], Wd[:, j, dc * 128:(dc + 1) * 128], hid[:, j, :], start=(j == 0), stop=(j == FC - 1))
                        if j == FC - 1:
                            vdn = cx.op("tensor", fn, sm["dn"])
                        else:
                            cx.op("tensor", fn)
                    cx.wait("vector", sm["dn"], vdn)
                    vres = cx.op("vector", lambda e, bank=bank, dc=dc, s=s, xb=xb: e.scalar_tensor_tensor(
                        xb[:, dc, :], bank[:, 0:TF], G_(l_in, dc, s), xb[:, dc, :], ALU.mult, ALU.add), sm["res"])
                    ring.release(bi, sm["res"], vres)
                    cx.wait("scalar", sm["res"], vres)
                    sq.append(cx.op("scalar", lambda e, dc=dc, xb=xb: e.activation(
                        out=xsqB[:, dc, :], in_=xb[:, dc, :], func=AF.Square), sm["cb"]))
                st[i]["sqB"] = sq
                if mode == 0:
                    cx.wait("gpsimd", sm["res"], vres)
                    st[i]["x_store"] = cx.dma("gpsimd", x1T_d[:, :, g0:g0 + TF].rearrange("k p n -> p k n"), xb, s_out)

            load_in(0)
            if mode == 0:
                transposes_in(0)
                stats_norm(0, 1)
                if n > 1:
                    load_in(1)
            for i in range(n):
                hooks = {}
                if i >= 1:
                    hooks.setdefault(1, []).append(lambda i=i: stats_norm(i - 1, 2 if mode == 0 else 3))
                    if mode == 1:
                        hooks.setdefault(10, []).append(lambda i=i: out_transposes(i - 1))
                if mode == 1 and i + 1 < n:
                    hooks.setdefault(13, []).append(lambda i=i: load_in(i + 1))
                if mode == 0 and i + 1 < n:
                    hooks.setdefault(16, []).append(lambda i=i: transposes_in(i + 1))
                    hooks.setdefault(21, []).append(lambda i=i: stats_norm(i + 1, 1))
                    if i + 2 < n:
                        hooks.setdefault(21, []).append(lambda i=i: load_in(i + 2))
                gate_up(i, hooks)
                down(i)
            stats_norm(n - 1, 2 if mode == 0 else 3)
            if mode == 1:
                out_transposes(n - 1)
            barrier()

        class Ring:
            def __init__(self, banks):
                self.banks = banks
                self.hist = []

            def acquire(self):
                i = len(self.hist)
                n = len(self.banks)
                if i >= n:
                    for sem, val in self.hist[i - n]:
                        cx.wait("tensor", sem, val)
                self.hist.append([])
                return self.banks[i % n], i

            def release(self, i, sem, val):
                self.hist[i].append((sem, val))

        def mix_in_phase():
            cx.top = persist_top
            s_w = cx.sem("m_w")
            s_in = cx.sem("m_in")
            s_pe = cx.sem("m_pe")
            s_a = cx.sem("m_a")
            s_v = cx.sem("m_v")
            Win = v3(cx.alloc(KC * 2560, BF16), KC)
            wsT = v3(cx.alloc(512, BF16), 4)
            bs_row = cx.alloc(512, BF16)
            row32 = cx.alloc(512)
            ones32 = cx.alloc(256)
            sgn_bc = cx.alloc(512)
            eps_t = cx.alloc(1)
            h2 = [v3(cx.alloc(KC * T, BF16), KC), v3(cx.alloc(KC * T, BF16), KC)]
            qk_sb = v3(cx.alloc(8 * T, BF16), 8)
            V_sb = cx.alloc(TB * 4 * 192, BF16).rearrange("p (t h c) -> p t h c", t=TB, h=4)
            vn = v3(cx.alloc(TB * 512, BF16), TB)
            gv = [cx.alloc(512), cx.alloc(512)]
            sqv = cx.alloc(512)
            uT = v3(cx.alloc(4 * T), 4)
            gT_sb = v3(cx.alloc(4 * T, BF16), 4)
            ss = cx.alloc(8)
            sq1 = cx.alloc(8)
            rs = cx.alloc(8)

            winv = win_d.rearrange("(kc p) f -> p kc f", p=128)
            for kc in range(KC):
                cx.dma("gpsimd", Win[:, kc, :], winv[:, kc, :], s_w)
            cx.dma("gpsimd", wsT, wsT_d.rearrange("p (g t) -> p g t", g=4), s_w)
            cx.dma("gpsimd", bs_row[0:1, :], sgub_d, s_w)
            cx.dma("gpsimd", row32[0:1, :], sgun_d, s_w)
            cx.wait("vector", s_w, s_w.n)
            cx.wait("tensor", s_w, s_w.n)
            cx.op("vector", lambda e: e.memset(ones32, 1.0))
            cx.op("vector", lambda e: e.memset(eps_t, EPS))
            vo = cx.op("vector", lambda e: e.memset(sqv, 0.0), s_v)
            cx.wait("tensor", s_v, vo)
            vb = cx.op("tensor", lambda e: e.matmul(PS[7][:, :], ones32[0:1, 0:128], row32[0:1, :], start=True, stop=True), s_pe)
            cx.wait("vector", s_pe, vb)
            vo = cx.op("vector", lambda e: e.tensor_copy(sgn_bc, PS[7][:, :]), s_v)

            ring = Ring([PS[i] for i in range(6)])
            tiles = [(s, ti) for s in range(NSEQ) for ti in range(S // T)]
            if debug and "ntiles" in debug:
                tiles = tiles[:debug["ntiles"]]
            pe_tile_end = []
            last_stores = 0
            for it, (s, ti) in enumerate(tiles):
                g0 = s * S + ti * T
                halo = (s == 2 and not (2 <= ti < 6))
                hb_ = h2[it % 2]
                if it >= 2:
                    cx.wait("sync", s_pe, pe_tile_end[it - 2])
                vin = cx.dma("sync", hb_, h2T_d[:, :, g0:g0 + T].rearrange("k p n -> p k n"), s_in)
                cx.wait("tensor", s_in, vin)
                cx.wait("scalar", s_out, last_stores)
                cx.wait("vector", s_out, last_stores)
                va_last = 0
                for fcn in (range(4, 8) if halo else range(8)):
                    bank, bi = ring.acquire()
                    for kc in range(KC):
                        fn = lambda e, bank=bank, kc=kc, fcn=fcn, hb_=hb_: e.matmul(
                            bank[:, :], Win[:, kc, fcn * 128:(fcn + 1) * 128], hb_[:, kc, :], start=(kc == 0), stop=(kc == KC - 1))
                        if kc == KC - 1:
                            vp = cx.op("tensor", fn, s_pe)
                        else:
                            cx.op("tensor", fn)
                    cx.wait("scalar", s_pe, vp)
                    va_last = cx.op("scalar", lambda e, bank=bank, fcn=fcn: e.activation(
                        out=qk_sb[:, fcn, :], in_=bank[:, :], func=AF.Copy, scale=(0.125 if fcn < 4 else 1.0)), s_a)
                    ring.release(bi, s_a, va_last)
                cx.wait("gpsimd", s_a, va_last)
                if not halo:
                    cx.dma("gpsimd", qT_d[:, :, g0:g0 + T].rearrange("k p n -> p k n"), qk_sb[:, 0:4, :], s_out)
                cx.dma("gpsimd", kT_d[:, :, g0:g0 + T].rearrange("k p n -> p k n"), qk_sb[:, 4:8, :], s_out)
                for tb in range(TB):
                    blk = g0 // 128 + tb
                    bank, bi = ring.acquire()
                    for kc in range(KC):
                        fn = lambda e, bank=bank, kc=kc, tb=tb, hb_=hb_: e.matmul(
                            bank[:, :], hb_[:, kc, tb * 128:(tb + 1) * 128], Win[:, kc, 1024:1536], start=(kc == 0), stop=(kc == KC - 1))
                        if kc == KC - 1:
                            vp = cx.op("tensor", fn, s_pe)
                        else:
                            cx.op("tensor", fn)
                    cx.wait("vector", s_pe, vp)
                    bv = bank[:, :].rearrange("p (h c) -> p h c", h=4)
                    cx.op("vector", lambda e, bv=bv, tb=tb, blk=blk: e.tensor_scalar(
                        V_sb[:, tb, :, 0:64], bv[:, :, 0:64], valid[:, blk:blk + 1], None, ALU.mult))
                    cx.op("vector", lambda e, bv=bv, tb=tb, blk=blk: e.tensor_scalar(
                        V_sb[:, tb, :, 128:192], bv[:, :, 64:128], valid[:, blk:blk + 1], None, ALU.mult))
                    vv = cx.op("vector", lambda e, tb=tb, blk=blk: e.tensor_scalar(
                        V_sb[:, tb, :, 64:128], ones32.rearrange("p (h c) -> p h c", h=4), valid[:, blk:blk + 1], None, ALU.mult), s_v)
                    ring.release(bi, s_v, vv)
                    cx.wait("gpsimd", s_v, vv)
                    cx.dma("gpsimd", V_d[:, g0 + tb * 128:g0 + (tb + 1) * 128, :].rearrange("h p c -> p h c"), V_sb[:, tb], s_out)
                if not halo:
                    vvn = []
                    for tb in range(TB):
                        b = tb % 2
                        bank, bi = ring.acquire()
                        for kc in range(KC):
                            fn = lambda e, bank=bank, kc=kc, tb=tb, hb_=hb_: e.matmul(
                                bank[:, :], hb_[:, kc, tb * 128:(tb + 1) * 128], Win[:, kc, 2048:2560], start=(kc == 0), stop=(kc == KC - 1))
                            if kc == KC - 1:
                                vp = cx.op("tensor", fn, s_pe)
                            else:
                                cx.op("tensor", fn)
                        cx.wait("scalar", s_pe, vp)
                        if tb >= 2:
                            cx.wait("scalar", s_v, vvn[tb - 2])
                        va = cx.op("scalar", lambda e, bank=bank, b=b: e.activation(out=gv[b], in_=bank[:, :], func=AF.Gelu_apprx_tanh), s_a)
                        ring.release(bi, s_a, va)
                        cx.wait("vector", s_a, va)
                        v1 = cx.op("vector", lambda e, b=b: e.tensor_tensor(sqv, gv[b], gv[b], ALU.mult), s_v)
                        cx.wait("vector", s_v, v1)
                        v2 = cx.op("vector", lambda e, tb=tb: e.reduce_sum(ss[:, tb:tb + 1], sqv, mybir.AxisListType.X), s_v)
                        cx.wait("scalar", s_v, v2)
                        va2 = cx.op("scalar", lambda e, tb=tb: e.activation(
                            out=sq1[:, tb:tb + 1], in_=ss[:, tb:tb + 1], func=AF.Sqrt, bias=eps_t[:, 0:1], scale=1.0 / 512), s_a)
                        cx.wait("vector", s_a, va2)
                        v3_ = cx.op("vector", lambda e, tb=tb: e.reciprocal(rs[:, tb:tb + 1], sq1[:, tb:tb + 1]), s_v)
                        cx.wait("vector", s_v, v3_)
                        v4 = cx.op("vector", lambda e, tb=tb, b=b: e.scalar_tensor_tensor(
                            vn[:, tb, :], gv[b], rs[:, tb:tb + 1], sgn_bc, ALU.mult, ALU.mult), s_v)
                        vvn.append(v4)
                    vu = []
                    for g in range(4):
                        bank, bi = ring.acquire()
                        for kc in range(KC):
                            fn = lambda e, bank=bank, kc=kc, g=g, hb_=hb_: e.matmul(
                                bank[:, :], Win[:, kc, 1536 + g * 128:1536 + (g + 1) * 128], hb_[:, kc, :], start=(kc == 0), stop=(kc == KC - 1))
                            if kc == KC - 1:
                                vp = cx.op("tensor", fn, s_pe)
                            else:
                                cx.op("tensor", fn)
                        cx.wait("scalar", s_pe, vp)
                        va = cx.op("scalar", lambda e, bank=bank, g=g: e.activation(out=uT[:, g, :], in_=bank[:, :], func=AF.Gelu_apprx_tanh), s_a)
                        ring.release(bi, s_a, va)
                        vu.append(va)
                    for g in range(4):
                        bank, bi = ring.acquire()
                        for tb in range(TB):
                            if g == 0:
                                cx.wait("tensor", s_v, vvn[tb])
                            cx.op("tensor", lambda e, bank=bank, g=g, tb=tb: e.matmul(
                                bank[:, tb * 128:(tb + 1) * 128], vn[:, tb, g * 128:(g + 1) * 128], wsT[:, g, :], start=True, stop=False))
                            fn = lambda e, bank=bank, g=g, tb=tb: e.matmul(
                                bank[:, tb * 128:(tb + 1) * 128], ones_bf[0:1, 0:128], bs_row[0:1, g * 128:(g + 1) * 128], start=False, stop=True)
                            if tb == TB - 1:
                                vp = cx.op("tensor", fn, s_pe)
                            else:
                                cx.op("tensor", fn)
                        cx.wait("vector", s_pe, vp)
                        cx.wait("vector", s_a, vu[g])
                        vg = cx.op("vector", lambda e, bank=bank, g=g: e.tensor_tensor(gT_sb[:, g, :], bank[:, :], uT[:, g, :], ALU.mult), s_v)
                        ring.release(bi, s_v, vg)
                    cx.wait("gpsimd", s_v, vg)
                    cx.dma("gpsimd", gT_d[:, :, g0:g0 + T].rearrange("k p n -> p k n"), gT_sb, s_out)
                pe_tile_end.append(s_pe.n)
                last_stores = s_out.n
            barrier()

        def segments(Q0, n, L):
            nb = L // 128
            units = list(range(Q0 // 64, (Q0 + n) // 64))
            segs = []
            i = 0
            while i < len(units):
                u = units[i]
                if u % 2 == 1 and i + 1 < len(units) and (u + 1) * 64 < L:
                    P, nq = u * 64, 128
                    i += 2
                else:
                    P, nq = u * 64, 64
                    i += 1
                kbs = []
                mc_ = None
                for kb in range(max(0, (P - 64) // 128), min(nb - 1, (P + nq + 63) // 128) + 1):
                    v = P - 64 - 128 * kb
                    if v not in (-128, -64, 0, 64):
                        continue
                    hh = 1 if v < 0 else 0
                    mc = v + 128 * hh
                    if nq == 128 and mc != 0:
                        continue
                    assert mc_ is None or mc_ == mc
                    mc_ = mc
                    kbs.append((kb, hh))
                segs.append((P, nq, kbs, mc_))
            return segs

        def attn_phase():
            cx.top = persist_top
            s_ld = cx.sem("a_ld")
            s_vl = cx.sem("a_vl")
            s_ps = cx.sem("a_ps")
            s_ex = cx.sem("a_ex")
            s_mk = cx.sem("a_mk")
            s_po = cx.sem("a_po")
            s_ac = cx.sem("a_ac")
            s_nm = cx.sem("a_nm")
            qT = v3(cx.alloc(4 * S, BF16), 4)
            kT = v3(cx.alloc(4 * S, BF16), 4)
            Vt = [v3(cx.alloc(48 * 192, BF16), 48), v3(cx.alloc(48 * 192, BF16), 48)]
            acc = [cx.alloc(2048), cx.alloc(2048)]
            rD = cx.alloc(2048)
            attnT = [cx.alloc(2048, BF16), cx.alloc(2048, BF16)]
            NPB = 4
            pT = [cx.alloc(256, BF16) for _ in range(NPB)]
            pTm = [cx.alloc(256, BF16) for _ in range(NPB)]
            ringS = Ring([PS[0], PS[1], PS[2], PS[3]])
            ringO = Ring([PS[4], PS[5], PS[6]])
            LOOK = 2
            MASK_ENG = "gpsimd"
            pending_store = [None]

            def flush_store():
                hp_, s_, q0_, ab_, v4_ = pending_store[0]
                cx.wait("sync", s_nm, v4_)
                at_store[ab_] = cx.dma("sync", aT_d[hp_, :, s_ * S + q0_:s_ * S + q0_ + 2048], attnT[ab_], s_out)
                pending_store[0] = None

            vt_ready = [0, 0]
            seg_i = 0
            ex_hist = []
            mk_hist = []
            po_hist = []
            vl_i = 0
            vt_free = [0, 0]
            at_i = 0
            at_store = [0, 0]
            pe_done_seq = 0
            sts = [(0, 0), (0, 2048), (1, 0), (1, 2048), (2, 1024)]
            if debug and "nst" in debug:
                sts = sts[:debug["nst"]]
            cur_seq = -1
            for (s, q0) in sts:
                if s != cur_seq:
                    cur_seq = s
                    cx.wait("sync", s_po, pe_done_seq)
                    cx.wait("sync", s_ps, s_ps.n)
                    cx.dma("sync", qT, qT_d[:, :, s * S:(s + 1) * S].rearrange("k p n -> p k n"), s_ld)
                    vq = cx.dma("sync", kT, kT_d[:, :, s * S:(s + 1) * S].rearrange("k p n -> p k n"), s_ld)
                    cx.wait("tensor", s_ld, vq)
                for hp in range(4):
                    items = []
                    for pi, (_, dil) in enumerate(PATTERNS):
                        L = S // dil
                        Q0, n = q0 // dil, 2048 // dil
                        kb_lo = max(0, (Q0 - 64) // 128)
                        kb_hi = min(L // 128 - 1, (Q0 + n + 63) // 128)
                        nkb = kb_hi - kb_lo + 1
                        vb = vl_i % 2
                        vl_i += 1
                        first = True
                        for hd in range(2):
                            for r in range(dil):
                                for (P, nq, kbs, mc) in segments(Q0, n, L):
                                    items.append(dict(pi=pi, dil=dil, Q0=Q0, kb_lo=kb_lo, nkb=nkb, vb=vb, hd=hd, r=r,
                                                      P=P, nq=nq, kbs=kbs, mc=mc, vload=first, s=s, hp=hp))
                                    first = False

                    def emit_front(it):
                        nonlocal seg_i
                        dil, r, P, nq, kbs, mc, hd = it["dil"], it["r"], it["P"], it["nq"], it["kbs"], it["mc"], it["hd"]
                        if it["vload"]:
                            vb, nkb, kb_lo = it["vb"], it["nkb"], it["kb_lo"]
                            cx.wait("sync", s_po, vt_free[vb])
                            for rr in range(dil):
                                t0 = it["s"] * S + rr + dil * 128 * kb_lo
                                src_ = V_d[it["hp"], t0:t0 + dil * (128 * nkb - 1) + 1:dil, :].rearrange("(kb p) c -> p kb c", p=128)
                                vvl = cx.dma("sync", Vt[vb][:, rr * nkb:(rr + 1) * nkb, :], src_, s_vl)
                            vt_ready[vb] = vvl
                        it["vready"] = vt_ready[it["vb"]]
                        h = it["hp"] * 2 + hd
                        hb = 64 * hd
                        midx = h * 3 + it["pi"]
                        nk = len(kbs)
                        it["seg"] = seg_i
                        pb = seg_i % NPB
                        it["pb"] = pb
                        bankS, bsi = ringS.acquire()
                        qcols = slice(r + dil * P, r + dil * (P + nq - 1) + 1, dil)
                        for i, (kb, hh) in enumerate(kbs):
                            kcols = slice(r + dil * 128 * kb, r + dil * (128 * kb + 127) + 1, dil)
                            fn = lambda e, bankS=bankS, i=i, nq=nq, kcols=kcols, qcols=qcols, hb=hb, hp=it["hp"]: e.matmul(
                                bankS[:, i * nq:(i + 1) * nq], kT[hb:hb + 64, hp, kcols], qT[hb:hb + 64, hp, qcols], start=True, stop=True)
                            if i == nk - 1:
                                vps = cx.op("tensor", fn, s_ps)
                            else:
                                cx.op("tensor", fn)
                        cx.wait("scalar", s_ps, vps)
                        if seg_i >= NPB:
                            cx.wait("scalar", s_mk, mk_hist[seg_i - NPB])
                        vex = cx.op("scalar", lambda e, bankS=bankS, pb=pb, w=nk * nq: e.activation(
                            out=pT[pb][:, 0:w], in_=bankS[:, 0:w], func=AF.Exp), s_ex)
                        ringS.release(bsi, s_ex, vex)
                        cx.wait(MASK_ENG, s_ex, vex)
                        if seg_i >= NPB:
                            cx.wait(MASK_ENG, s_po, po_hist[seg_i - NPB])
                        if nk == 2:
                            m_ap = Emask3[:, midx, :].rearrange("p (h q) -> p h q", h=2)[:, :, mc:mc + nq]
                            o_ap = pTm[pb][:, 0:2 * nq].rearrange("p (h q) -> p h q", h=2)
                            i_ap = pT[pb][:, 0:2 * nq].rearrange("p (h q) -> p h q", h=2)
                        else:
                            hh = kbs[0][1]
                            m_ap = Emask3[:, midx, hh * 128 + mc:hh * 128 + mc + nq]
                            o_ap = pTm[pb][:, 0:nq]
                            i_ap = pT[pb][:, 0:nq]
                        vmk = cx.op(MASK_ENG, lambda e, o_ap=o_ap, i_ap=i_ap, m_ap=m_ap: e.tensor_tensor(o_ap, i_ap, m_ap, ALU.mult), s_mk, k=1)
                        mk_hist.append(vmk)
                        it["vmk"] = vmk
                        seg_i += 1

                    def emit_back(it):
                        dil, r, P, nq, kbs, hd, vb = it["dil"], it["r"], it["P"], it["nq"], it["kbs"], it["hd"], it["vb"]
                        nk = len(kbs)
                        pb = it["pb"]
                        vcols = slice(0, 128) if hd == 0 else slice(64, 192)
                        bankO, boi = ringO.acquire()
                        cx.wait("tensor", s_vl, it["vready"])
                        cx.wait("tensor", s_mk, it["vmk"])
                        for i, (kb, hh) in enumerate(kbs):
                            blk = r * it["nkb"] + (kb - it["kb_lo"])
                            fn = lambda e, bankO=bankO, i=i, nq=nq, blk=blk, vb=vb, vcols=vcols, pb=pb, nk=nk: e.matmul(
                                bankO[:, 0:nq], Vt[vb][:, blk, vcols], pTm[pb][:, i * nq:(i + 1) * nq], start=(i == 0), stop=(i == nk - 1))
                            if i == nk - 1:
                                vpo = cx.op("tensor", fn, s_po)
                            else:
                                cx.op("tensor", fn)
                        po_hist.append(vpo)
                        vt_free[vb] = vpo
                        c0 = r + dil * (P - it["Q0"])
                        dst = acc[hd][:, c0:c0 + dil * (nq - 1) + 1:dil]
                        cx.wait("vector", s_po, vpo)
                        if it["pi"] == 0:
                            vac = cx.op("vector", lambda e, dst=dst, bankO=bankO, nq=nq: e.tensor_copy(dst, bankO[:, 0:nq]), s_ac)
                        else:
                            vac = cx.op("vector", lambda e, dst=dst, bankO=bankO, nq=nq: e.tensor_tensor(dst, bankO[:, 0:nq], dst, ALU.add), s_ac)
                        ringO.release(boi, s_ac, vac)

                    for idx in range(len(items) + LOOK):
                        if idx < len(items):
                            emit_front(items[idx])
                        if idx - LOOK >= 0:
                            emit_back(items[idx - LOOK])
                    ab = at_i % 2
                    at_i += 1
                    cx.wait("vector", s_ac, s_ac.n)
                    cx.wait("vector", s_out, at_store[ab])
                    v1 = cx.op("vector", lambda e: e.tensor_copy(rD[0:64, :], acc[0][64:128, :]), s_nm)
                    v2 = cx.op("vector", lambda e: e.tensor_copy(rD[64:128, :], acc[1][0:64, :]), s_nm)
                    cx.wait("vector", s_nm, v2)
                    v2b = cx.op("vector", lambda e: e.reciprocal(rD, rD), s_nm)
                    cx.wait("vector", s_nm, v2b)
                    v3_ = cx.op("vector", lambda e, ab=ab: e.tensor_tensor(attnT[ab][0:64, :], acc[0][0:64, :], rD[0:64, :], ALU.mult), s_nm)
                    v4 = cx.op("vector", lambda e, ab=ab: e.tensor_tensor(attnT[ab][64:128, :], acc[1][64:128, :], rD[64:128, :], ALU.mult), s_nm)
                    if pending_store[0] is not None:
                        flush_store()
                    pending_store[0] = (hp, s, q0, ab, v4)
                if pending_store[0] is not None:
                    flush_store()
                pe_done_seq = s_po.n
            barrier()

        def mix_out_phase():
            cx.top = persist_top
            s_w = cx.sem("o_w")
            s_in = cx.sem("o_in")
            s_pe = cx.sem("o_pe")
            s_res = cx.sem("o_res")
            s_ca = cx.sem("o_ca")
            s_st = cx.sem("o_st")
            s_sq = cx.sem("o_sq")
            s_rs = cx.sem("o_rs")
            s_tA = cx.sem("o_tA")
            s_hT = cx.sem("o_hT")
            Wout = v3(cx.alloc(KC * D, BF16), KC)
            eps_t = cx.alloc(1)
            xT = v3(cx.alloc(KC * T), KC)
            mixT = v3(cx.alloc(KC * T, BF16), KC)
            hT = v3(cx.alloc(KC * T, BF16), KC)
            xsq = v3(cx.alloc(KC * T, BF16), KC)
            rstd = cx.alloc(T)
            sqt = cx.alloc(T)
            tmpA = [cx.alloc(T), cx.alloc(T)]
            woutv = wout_d.rearrange("(kc p) f -> p kc f", p=128)
            for kc in range(KC):
                cx.dma("gpsimd", Wout[:, kc, :], woutv[:, kc, :], s_w)
            cx.wait("tensor", s_w, s_w.n)
            cx.op("vector", lambda e: e.memset(eps_t, EPS))
            ring = Ring([PS[0], PS[1], PS[2]])
            tiles = [(s, ti) for s in range(NSEQ) for ti in range(S // T) if s < 2 or 2 <= ti < 6]
            if debug and "ntiles" in debug:
                tiles = tiles[:debug["ntiles"]]
            hT_hist = []
            last_x = 0
            last_h = 0
            for (s, ti) in tiles:
                g0 = s * S + ti * T
                cx.wait("sync", s_out, max(last_x, last_h))
                cx.wait("sync", s_pe, s_pe.n)
                cx.wait("sync", s_hT, s_hT.n)
                cx.dma("sync", xT, x1T_d[:, :, g0:g0 + T].rearrange("k p n -> p k n"), s_in)
                cx.dma("sync", mixT[:, 0:4, :], aT_d[:, :, g0:g0 + T].rearrange("k p n -> p k n"), s_in)
                vin = cx.dma("sync", mixT[:, 4:8, :], gT_d[:, :, g0:g0 + T].rearrange("k p n -> p k n"), s_in)
                cx.wait("tensor", s_in, vin)
                cx.wait("vector", s_in, vin)
                res_vals = []
                for dc in range(KC):
                    bank, bi = ring.acquire()
                    for kc in range(KC):
                        fn = lambda e, bank=bank, kc=kc, dc=dc: e.matmul(
                            bank[:, :], Wout[:, kc, dc * 128:(dc + 1) * 128], mixT[:, kc, :], start=(kc == 0), stop=(kc == KC - 1))
                        if kc == KC - 1:
                            vp = cx.op("tensor", fn, s_pe)
                        else:
                            cx.op("tensor", fn)
                    cx.wait("vector", s_pe, vp)
                    vres = cx.op("vector", lambda e, bank=bank, dc=dc, s=s: e.scalar_tensor_tensor(
                        xT[:, dc, :], bank[:, :], G_(1, dc, s), xT[:, dc, :], ALU.mult, ALU.add), s_res)
                    ring.release(bi, s_res, vres)
                    res_vals.append(vres)
                ca_vals = []
                for kc in range(KC):
                    cx.wait("scalar", s_res, res_vals[kc])
                    ca_vals.append(cx.op("scalar", lambda e, kc=kc: e.activation(out=xsq[:, kc, :], in_=xT[:, kc, :], func=AF.Square), s_ca))
                cx.wait("gpsimd", s_res, res_vals[-1])
                last_x = cx.dma("gpsimd", x2T_d[:, :, g0:g0 + T].rearrange("k p n -> p k n"), xT, s_out)
                cx.wait("tensor", s_sq, s_sq.n)
                for kc in range(KC):
                    cx.wait("tensor", s_ca, ca_vals[kc])
                    fn = lambda e, kc=kc: e.matmul(PS[3][:, :], ones_bf, xsq[:, kc, :], start=(kc == 0), stop=(kc == KC - 1))
                    if kc == KC - 1:
                        vst = cx.op("tensor", fn, s_st)
                    else:
                        cx.op("tensor", fn)
                cx.wait("scalar", s_st, vst)
                vsq = cx.op("scalar", lambda e: e.activation(out=sqt, in_=PS[3][:, :], func=AF.Sqrt, bias=eps_t[:, 0:1], scale=1.0 / D), s_sq)
                cx.wait("vector", s_sq, vsq)
                vrs = cx.op("vector", lambda e: e.reciprocal(rstd, sqt), s_rs)
                cx.wait("vector", s_rs, vrs)
                cx.wait("scalar", s_out, last_h)
                for kc in range(KC):
                    b = kc % 2
                    if len(hT_hist) >= 2:
                        cx.wait("vector", s_hT, hT_hist[-2])
                    vt = cx.op("vector", lambda e, kc=kc, b=b, s=s: e.scalar_tensor_tensor(
                        tmpA[b], xT[:, kc, :], A_(2, kc, s), rstd, ALU.mult, ALU.mult), s_tA)
                    cx.wait("scalar", s_tA, vt)
                    va = cx.op("scalar", lambda e, kc=kc, b=b, s=s: e.activation(
                        out=hT[:, kc, :], in_=tmpA[b], func=AF.Identity, bias=B_(2, kc, s), scale=1.0), s_hT)
                    hT_hist.append(va)
                cx.wait("gpsimd", s_hT, va)
                last_h = cx.dma("gpsimd", h3T_d[:, :, g0:g0 + T].rearrange("k p n -> p k n"), hT, s_out)
            barrier()

        if stop_after >= 1:
            ffn_phase(0, preloaded=("gu", "d"))
        if stop_after >= 2:
            mix_in_phase()
        if stop_after >= 3:
            attn_phase()
        if stop_after >= 4:
            mix_out_phase()
        if stop_after >= 5:
            ffn_phase(1)

        barrier()
        cx.emit_all()
    return nc


def _core_inputs(core, inp):
    xs = inp["x_sample"]
    xp = inp["x_prompt"]
    x = np.zeros((NSEQ, S, D), np.float32)
    x[0] = xs[2 * core]
    x[1] = xs[2 * core + 1]
    pb, qd = core // 4, core % 4
    lo = qd * 2048 - 1024
    valid = np.ones((NSEQ, S), np.float32)
    a, b = max(lo, 0), min(lo + S, 8192)
    x[2, a - lo:b - lo] = xp[pb, a:b]
    valid[2, :] = 0.0
    valid[2, a - lo:b - lo] = 1.0
    c3 = np.stack([inp["c_sample"][2 * core], inp["c_sample"][2 * core + 1], inp["c_prompt"][pb]], 0)
    cT = np.ascontiguousarray(c3.reshape(3, KC, 128).transpose(2, 1, 0)).reshape(128, KC * 3)
    return x.reshape(NTOK, D), cT, np.ascontiguousarray(valid.reshape(NTOK // 128, 128).T)


def _shared_inputs(inp):
    def pcol(v):
        return np.ascontiguousarray(np.asarray(v, np.float32).reshape(-1, 128).T)
    pvec = np.concatenate([pcol(inp["ada_b"][0]), pcol(inp["ffn1_norm"][0]), pcol(inp["mix_norm"][0]),
                           pcol(inp["ffn2_norm"][0]), pcol(inp["final_norm"])], axis=1)
    kp = np.arange(128)[:, None]
    col = np.arange(256)[None, :]
    hh, q = col // 128, col % 128
    rel = 128 * hh + kp - q - 64
    sh = {
        "pvec": pvec.astype(np.float32),
        "ident": np.eye(128, dtype=np.float32),
        "absrel": np.abs(rel).astype(np.float32),
        "band": (np.abs(rel) <= 64).astype(np.float32),
        "ada_w": np.ascontiguousarray(inp["ada_w"][0]),
        "wg1": np.ascontiguousarray(inp["ffn1_w_gate"][0]), "wu1": np.ascontiguousarray(inp["ffn1_w_up"][0]),
        "wd1": np.ascontiguousarray(inp["ffn1_w_down"][0]),
        "wg2": np.ascontiguousarray(inp["ffn2_w_gate"][0]), "wu2": np.ascontiguousarray(inp["ffn2_w_up"][0]),
        "wd2": np.ascontiguousarray(inp["ffn2_w_down"][0]),
        "w_in": np.ascontiguousarray(inp["w_in"][0]),
        "wsT": np.ascontiguousarray(inp["sgu_w"][0].transpose(2, 0, 1)).reshape(128, 512),
        "sgu_b": np.ascontiguousarray(inp["sgu_b"][0]).reshape(1, 512),
        "sgu_norm": np.ascontiguousarray(inp["sgu_norm"][0]).reshape(1, 512),
        "w_out": np.ascontiguousarray(inp["w_out"][0]),
    }
    return sh


def make_in_maps(inp):
    inp = {k: np.asarray(v) for k, v in inp.items()}
    sh = _shared_inputs(inp)
    maps = []
    for core in range(N_CORES):
        x, cT, valid = _core_inputs(core, inp)
        m = dict(sh)
        m["x"] = x
        m["cT"] = cT
        m["valid"] = valid
        maps.append(m)
    return maps


def kernel(**inputs):
    maps = make_in_maps(inputs)
    nc = build_program()
    res = run_bass_kernel_spmd(nc, maps, core_ids=list(range(N_CORES)))
    ys = np.empty((16, S, D), np.float32)
    yp = np.empty((2, 8192, D), np.float32)
    for core in range(N_CORES):
        y = res.results[core]["y"]
        ys[2 * core] = y[0:S]
        ys[2 * core + 1] = y[S:2 * S]
        pb, qd = core // 4, core % 4
        yp[pb, qd * 2048:(qd + 1) * 2048] = y[2 * S:]
    return (yp, ys)
```

```python
import numpy as np
from contextlib import ExitStack
import concourse.bass as bass
import concourse.mybir as mybir
from concourse.bass_utils import run_bass_kernel_spmd

F32 = mybir.dt.float32
BF16 = mybir.dt.bfloat16
AF = mybir.ActivationFunctionType
ALU = mybir.AluOpType

D = 1024
KC = 8
DFF = 2816
FC = 22
S = 4096
NSEQ = 3
NTOK = NSEQ * S
T = 512
TB = 4
NQTOK = 2 * S + 2048
EPS = 1e-6
HEADS = 8
PATTERNS = ((128, 1), (512, 4), (2048, 16))
N_CORES = 8
ARENA_F32 = 53200

ENGINES = ("tensor", "vector", "scalar", "gpsimd", "sync")


class Sem:
    def __init__(self, h):
        self.h = h
        self.n = 0


class Ctx:
    def __init__(self, nc, es):
        self.nc = nc
        self.es = es
        self.q = {e: [] for e in ENGINES}
        self.nsem = 0
        self.arena = es.enter_context(nc.sbuf_tensor("arena", [128, ARENA_F32], F32))
        self.top = 0
        self.psum = [es.enter_context(nc.psum_tensor("ps%d" % i, [128, 512], F32)) for i in range(8)]

    def sem(self, name):
        h = self.es.enter_context(self.nc.semaphore(name))
        self.nsem += 1
        return Sem(h)

    def alloc(self, cols, dtype=F32):
        if dtype == BF16:
            ncol32 = (cols + 1) // 2
        else:
            ncol32 = cols
        a = self.top
        self.top += ncol32
        assert self.top <= ARENA_F32, ("SBUF arena overflow", self.top)
        ap = self.arena[:, a:a + ncol32]
        if dtype == BF16:
            ap = ap.bitcast(BF16)
        return ap

    def op(self, eng, fn, sig=None, k=None):
        if sig is None:
            self.q[eng].append(fn)
            return None
        if k is None:
            k = 16 if eng in ("sync", "gpsimd_dma") else 1
        sig.n += k
        h = sig.h
        self.q[eng].append(lambda e, fn=fn, h=h, k=k: fn(e).then_inc(h, k))
        return sig.n

    def dma(self, eng, out, in_, sig):
        sig.n += 16
        h = sig.h
        self.q[eng].append(lambda e, out=out, in_=in_, h=h: e.dma_start(out=out, in_=in_).then_inc(h, 16))
        return sig.n

    def wait(self, eng, sem, val):
        if val is None or val <= 0:
            return
        h = sem.h
        self.q[eng].append(lambda e, h=h, val=val: e.wait_ge(h, val))

    def emit_all(self):
        nc = self.nc
        with nc.Block() as blk:
            for name in ENGINES:
                ops = self.q[name]

                def body(e, ops=ops):
                    for f in ops:
                        f(e)
                getattr(blk, name)(body)


def v3(ap, a):
    return ap.rearrange("p (a b) -> p a b", a=a)


def build_program(debug=None):
    nc = bass.Bass("TRN2", target_bir_lowering=False)
    dt = nc.dram_tensor

    def din(name, shape, dtype=F32):
        return dt(name, list(shape), dtype, kind="ExternalInput").ap()

    x_d = din("x", [NTOK, D])
    cT_d = din("cT", [128, KC * NSEQ])
    pvec_d = din("pvec", [128, 104])
    valid_d = din("valid", [128, NTOK // 128])
    ident_d = din("ident", [128, 128])
    absrel_d = din("absrel", [128, 256])
    band_d = din("band", [128, 256])
    adaw_d = din("ada_w", [D, 9 * D])
    wg_d = [din("wg1", [D, DFF]), din("wg2", [D, DFF])]
    wu_d = [din("wu1", [D, DFF]), din("wu2", [D, DFF])]
    wd_d = [din("wd1", [DFF, D]), din("wd2", [DFF, D])]
    win_d = din("w_in", [D, 2560])
    wsT_d = din("wsT", [128, 512])
    sgub_d = din("sgu_b", [1, 512])
    sgun_d = din("sgu_norm", [1, 512])
    wout_d = din("w_out", [D, D])
    y_d = dt("y", [NQTOK, D], F32, kind="ExternalOutput").ap()

    skind = "ExternalOutput" if (debug and not debug.get("internal")) else "Internal"
    x1T_d = dt("x1T", [KC, 128, NTOK], F32, kind=skind).ap()
    h2T_d = dt("h2T", [KC, 128, NTOK], BF16, kind=skind).ap()
    qT_d = dt("qT", [4, 128, NTOK], BF16, kind=skind).ap()
    kT_d = dt("kT", [4, 128, NTOK], BF16, kind=skind).ap()
    V_d = dt("V", [4, NTOK, 192], BF16, kind=skind).ap()
    gT_d = dt("gT", [4, 128, NTOK], BF16, kind=skind).ap()
    aT_d = dt("aT", [4, 128, NTOK], BF16, kind=skind).ap()
    x2T_d = dt("x2T", [KC, 128, NTOK], F32, kind=skind).ap()
    h3T_d = dt("h3T", [KC, 128, NTOK], BF16, kind=skind).ap()

    stop_after = debug.get("stop_after", 99) if debug else 99

    with ExitStack() as es:
        cx = Ctx(nc, es)
        PS = cx.psum

        ident = cx.alloc(128)
        ones_bf = cx.alloc(128, BF16)
        pvec = cx.alloc(104)
        valid = cx.alloc(NTOK // 128)
        modT = cx.alloc(72 * 3)
        Amod = cx.alloc(3 * 8 * 3)
        Gmod = cx.alloc(3 * 8 * 3)
        Emask = cx.alloc(24 * 256, BF16)
        persist_top = cx.top

        modT3 = v3(modT, 72)
        Amod4 = Amod.rearrange("p (l k s) -> p l k s", l=3, k=8)
        Gmod4 = Gmod.rearrange("p (l k s) -> p l k s", l=3, k=8)
        Emask3 = v3(Emask, 24)

        def A_(l, kc, s):
            return Amod4[:, l, kc, s:s + 1]

        def B_(l, kc, s):
            return modT3[:, l * 24 + kc, s:s + 1]

        def G_(l, kc, s):
            return Gmod4[:, l, kc, s:s + 1]

        s_out = cx.sem("s_out")

        def barrier():
            for e in ENGINES:
                cx.wait(e, s_out, s_out.n)

        s_ld = cx.sem("p0_ld")
        s_c = cx.sem("p0_c")
        s_aw = cx.sem("p0_aw")
        s_awf = cx.sem("p0_awf")
        s_v = cx.sem("p0_v")
        s_a = cx.sem("p0_a")

        cx.top = ARENA_F32 - 9400
        cT = cx.alloc(24)
        scT = cx.alloc(24)
        absrel = cx.alloc(256)
        band = cx.alloc(256)
        etmp = [cx.alloc(256), cx.alloc(256)]
        tmp24 = cx.alloc(24)
        awbuf = [cx.alloc(8 * 512), cx.alloc(8 * 512)]

        for dst, src in ((ident, ident_d), (pvec, pvec_d), (valid, valid_d), (cT, cT_d),
                         (absrel, absrel_d), (band, band_d)):
            cx.dma("sync", dst, src, s_ld)
        n_ld = s_ld.n
        cx.wait("scalar", s_ld, n_ld)
        cx.wait("vector", s_ld, n_ld)
        cx.op("vector", lambda e: e.memset(ones_bf, 1.0))
        v_sc = cx.op("scalar", lambda e: e.activation(out=scT, in_=cT, func=AF.Silu), s_c)
        cx.wait("tensor", s_c, v_sc)
        adaw_v = adaw_d.rearrange("(kc p) f -> p kc f", p=128)
        for ch in range(18):
            b = ch % 2
            if ch >= 2:
                cx.wait("sync", s_awf, ch - 1)
            vld = cx.dma("sync", v3(awbuf[b], 8), adaw_v[:, :, ch * 512:(ch + 1) * 512], s_aw)
            cx.wait("tensor", s_aw, vld)
            for fl in range(4):
                col = (ch * 4 + fl) * 3
                for kc in range(KC):
                    fn = (lambda e, b=b, fl=fl, kc=kc, col=col: e.matmul(
                        PS[0][:, col:col + 3], v3(awbuf[b], 8)[:, kc, fl * 128:(fl + 1) * 128],
                        scT[:, kc * 3:(kc + 1) * 3], start=(kc == 0), stop=(kc == KC - 1)))
                    if fl == 3 and kc == KC - 1:
                        cx.op("tensor", fn, s_awf)
                    else:
                        cx.op("tensor", fn)
        cx.wait("vector", s_awf, 18)
        psm3 = PS[0][:, 0:216].rearrange("p (a b) -> p a b", b=3)
        for s in range(3):
            vm = cx.op("vector", lambda e, s=s: e.tensor_tensor(modT3[:, :, s], psm3[:, :, s], pvec[:, 0:72], ALU.add), s_v)
        cx.wait("vector", s_v, vm)
        for l in range(3):
            gl = pvec[:, 72 + 8 * l:80 + 8 * l]
            for s in range(3):
                v1 = cx.op("vector", lambda e, l=l, s=s: e.tensor_scalar(
                    tmp24[:, 0:8], modT3[:, l * 24 + 8:l * 24 + 16, s], 1.0, None, ALU.add), s_v)
                cx.wait("vector", s_v, v1)
                v2 = cx.op("vector", lambda e, l=l, s=s, gl=gl: e.tensor_tensor(
                    Amod4[:, l, :, s], tmp24[:, 0:8], gl, ALU.mult), s_v)
                cx.wait("vector", s_v, v2)
                rw = 1.0 if l == 1 else 0.5
                cx.op("vector", lambda e, l=l, s=s, rw=rw: e.tensor_scalar(
                    Gmod4[:, l, :, s], modT3[:, l * 24 + 16:l * 24 + 24, s], rw, None, ALU.mult), s_v)
        i = 0
        for h in range(HEADS):
            slope = 2.0 ** (-8.0 * (h + 1) / HEADS)
            for pi, (_, dil) in enumerate(PATTERNS):
                b = i % 2
                if i >= 2:
                    cx.wait("scalar", s_v, vmask[b])
                va = cx.op("scalar", lambda e, b=b, sc=-slope * dil: e.activation(
                    out=etmp[b], in_=absrel, func=AF.Exp, scale=sc), s_a)
                cx.wait("vector", s_a, va)
                vv = cx.op("vector", lambda e, b=b, idx=h * 3 + pi: e.tensor_tensor(
                    Emask3[:, idx, :], etmp[b], band, ALU.mult), s_v)
                if i == 0:
                    vmask = [0, 0]
                vmask[b] = vv
                i += 1
        v_p0 = s_v.n
        for e in ENGINES:
            if e == "gpsimd":
                continue
            cx.wait(e, s_v, v_p0)
            cx.wait(e, s_a, s_a.n)

        TF = 256
        TBF = 2
        W_BASE = persist_top
        WCOLS = KC * DFF // 2
        FFN_TILE_BASE = W_BASE + 3 * WCOLS

        def ffn_weight_aps():
            a = W_BASE
            Wg = v3(cx.arena[:, a:a + WCOLS].bitcast(BF16), KC)
            Wu = v3(cx.arena[:, a + WCOLS:a + 2 * WCOLS].bitcast(BF16), KC)
            Wd = v3(cx.arena[:, a + 2 * WCOLS:a + 3 * WCOLS].bitcast(BF16), FC)
            return Wg, Wu, Wd

        ffn_wsem = {}

        def ffn_issue_weights(mode, parts):
            if mode not in ffn_wsem:
                ffn_wsem[mode] = (cx.sem("f%d_wgu" % mode), cx.sem("f%d_wd" % mode))
            s_wgu, s_wd = ffn_wsem[mode]
            Wg, Wu, Wd = ffn_weight_aps()
            if "gu" in parts:
                wgv = wg_d[mode].rearrange("(kc p) f -> p kc f", p=128)
                wuv = wu_d[mode].rearrange("(kc p) f -> p kc f", p=128)
                for kc in range(KC):
                    cx.dma("gpsimd", Wg[:, kc, :], wgv[:, kc, :], s_wgu)
                    cx.dma("gpsimd", Wu[:, kc, :], wuv[:, kc, :], s_wgu)
            if "d" in parts:
                wdv = wd_d[mode].rearrange("(j p) d -> p j d", p=128)
                for j0 in range(0, FC, 2):
                    cx.dma("gpsimd", Wd[:, j0:j0 + 2, :], wdv[:, j0:j0 + 2, :], s_wd)

        ffn_issue_weights(0, ("gu", "d"))

        def ffn_phase(mode, preloaded=()):
            cx.top = FFN_TILE_BASE
            pre = "f%d_" % mode
            for part in ("gu", "d"):
                if part not in preloaded:
                    ffn_issue_weights(mode, (part,))
            s_wgu, s_wd = ffn_wsem[mode]
            names = ["xin", "trp", "cv", "ca", "st1", "sq1", "rs1", "st2", "sq2", "rs2", "tA", "hT",
                     "g", "u", "sg", "hid", "dn", "res", "cb", "yt", "ye"]
            sm = {n: cx.sem(pre + n) for n in names}
            Wg, Wu, Wd = ffn_weight_aps()

            xT = [v3(cx.alloc(KC * TF), KC), v3(cx.alloc(KC * TF), KC)]
            hT = [v3(cx.alloc(KC * TF, BF16), KC), v3(cx.alloc(KC * TF, BF16), KC)]
            hid = v3(cx.alloc(FC * TF, BF16), FC)
            xsqA = v3(cx.alloc(KC * TF, BF16), KC)
            xsqB = v3(cx.alloc(KC * TF, BF16), KC)
            tok = cx.alloc(TBF * D).rearrange("p (a b) -> p a b", a=TBF)
            h2buf = xsqA if mode == 0 else None
            tmpA = [cx.alloc(TF), cx.alloc(TF)]
            sg = [cx.alloc(TF), cx.alloc(TF)]
            rstd1 = rstd2 = cx.alloc(TF)
            sqt1 = sqt2 = cx.alloc(TF)
            last_rs = [None]
            eps_t = cx.alloc(1)
            cx.op("vector", lambda e: e.memset(eps_t, EPS))

            l_in = 0 if mode == 0 else 2
            if mode == 0:
                tiles = [(s, s * S + k * TF) for s in range(NSEQ) for k in range(S // TF)]
            else:
                tiles = [(s, s * S + k * TF) for s in range(NSEQ) for k in range(S // TF) if s < 2 or 4 <= k < 12]
            if debug and "ntiles" in debug:
                tiles = tiles[:debug["ntiles"]]
            n = len(tiles)
            st = [dict() for _ in range(n)]
            ring = Ring([PS[0], PS[1]])
            gu_hist = {"sg": [], "hid": []}
            tA_hist = []

            def load_in(i):
                s, g0 = tiles[i]
                if mode == 0:
                    if i >= 1:
                        cx.wait("sync", sm["trp"], st[i - 1]["tr_done"])
                    st[i]["xin"] = cx.dma("sync", tok, x_d[g0:g0 + TF, :].rearrange("(tb p) d -> p tb d", p=128), sm["xin"])
                else:
                    if i >= 2:
                        cx.wait("sync", sm["yt"], st[i - 2]["ytr_done"])
                        cx.wait("sync", sm["u"], st[i - 2]["gu_done"])
                    cx.dma("sync", xT[i % 2], x2T_d[:, :, g0:g0 + TF].rearrange("k p n -> p k n"), sm["xin"])
                    st[i]["xin"] = cx.dma("sync", hT[i % 2], h3T_d[:, :, g0:g0 + TF].rearrange("k p n -> p k n"), sm["xin"])

            def transposes_in(i):
                xb = xT[i % 2]
                cx.wait("tensor", sm["xin"], st[i]["xin"])
                if i >= 2:
                    cx.wait("vector", s_out, st[i - 2]["x_store"])
                sq = []
                hs = [st[k]["h_store"] for k in range(i) if "h_store" in st[k]]
                if hs:
                    cx.wait("scalar", s_out, hs[-1])
                for kc in range(KC):
                    bank, bi = ring.acquire()
                    for tb in range(TBF):
                        fn = lambda e, bank=bank, tb=tb, kc=kc: e.transpose(
                            bank[:, tb * 128:(tb + 1) * 128], tok[:, tb, kc * 128:(kc + 1) * 128], ident)
                        if tb == TBF - 1:
                            vtr = cx.op("tensor", fn, sm["trp"])
                        else:
                            cx.op("tensor", fn)
                    cx.wait("vector", sm["trp"], vtr)
                    vcv = cx.op("vector", lambda e, bank=bank, kc=kc, xb=xb: e.tensor_copy(xb[:, kc, :], bank[:, 0:TF]), sm["cv"])
                    ring.release(bi, sm["cv"], vcv)
                    cx.wait("scalar", sm["cv"], vcv)
                    sq.append(cx.op("scalar", lambda e, kc=kc, xb=xb: e.activation(
                        out=xsqA[:, kc, :], in_=xb[:, kc, :], func=AF.Square), sm["ca"]))
                    if kc < KC - 1:
                        yield
                st[i]["tr_done"] = vtr
                st[i]["sqA"] = sq
                yield

            def stats_norm(i, which):
                s, g0 = tiles[i]
                xb = xT[i % 2]
                if which == 1:
                    xsq, bankst, s_st, s_sq, s_rs, rstd, sqt, sq_sem, sq_vals = xsqA, PS[2], sm["st1"], sm["sq1"], sm["rs1"], rstd1, sqt1, sm["ca"], st[i]["sqA"]
                else:
                    xsq, bankst, s_st, s_sq, s_rs, rstd, sqt, sq_sem, sq_vals = xsqB, PS[7], sm["st2"], sm["sq2"], sm["rs2"], rstd2, sqt2, sm["cb"], st[i]["sqB"]
                cx.wait("tensor", s_sq, s_sq.n)
                for kc in range(KC):
                    cx.wait("tensor", sq_sem, sq_vals[kc])
                    fn = lambda e, kc=kc, xsq=xsq, bankst=bankst: e.matmul(bankst[:, 0:TF], ones_bf, xsq[:, kc, :], start=(kc == 0), stop=(kc == KC - 1))
                    if kc == KC - 1:
                        vst = cx.op("tensor", fn, s_st)
                    else:
                        cx.op("tensor", fn)
                cx.wait("scalar", s_st, vst)
                if last_rs[0] is not None:
                    cx.wait("scalar", last_rs[0][0], last_rs[0][1])
                vsq = cx.op("scalar", lambda e, sqt=sqt, bankst=bankst: e.activation(
                    out=sqt, in_=bankst[:, 0:TF], func=AF.Sqrt, bias=eps_t[:, 0:1], scale=1.0 / D), s_sq)
                cx.wait("vector", s_sq, vsq)
                vrs = cx.op("vector", lambda e, rstd=rstd, sqt=sqt: e.reciprocal(rstd, sqt), s_rs)
                last_rs[0] = (s_rs, vrs)
                yield
                cx.wait("vector", s_rs, vrs)
                if which == 3:
                    vy = []
                    for kc in range(KC):
                        vy.append(cx.op("vector", lambda e, kc=kc, xb=xb, rstd=rstd: e.scalar_tensor_tensor(
                            xb[:, kc, :], xb[:, kc, :], pvec[:, 96 + kc:97 + kc], rstd, ALU.mult, ALU.mult), sm["hT"]))
                        if kc < KC - 1:
                            yield
                    st[i]["y_ready"] = vy
                    yield
                    return
                l = 0 if which == 1 else 1
                dstb = hT[i % 2] if which == 1 else h2buf
                if which == 2 and i >= 1:
                    cx.wait("scalar", s_out, st[i - 1]["h_store"])
                vh = []
                for kc in range(KC):
                    b = len(tA_hist) % 2
                    if len(tA_hist) >= 2:
                        cx.wait("vector", sm["hT"], tA_hist[-2])
                    vt = cx.op("vector", lambda e, kc=kc, b=b, xb=xb, rstd=rstd, l=l, s=s: e.scalar_tensor_tensor(
                        tmpA[b], xb[:, kc, :], A_(l, kc, s), rstd, ALU.mult, ALU.mult), sm["tA"])
                    cx.wait("scalar", sm["tA"], vt)
                    va = cx.op("scalar", lambda e, kc=kc, b=b, dstb=dstb, l=l, s=s: e.activation(
                        out=dstb[:, kc, :], in_=tmpA[b], func=AF.Identity, bias=B_(l, kc, s), scale=1.0), sm["hT"])
                    tA_hist.append(va)
                    vh.append(va)
                    if kc < KC - 1:
                        yield
                if which == 1:
                    st[i]["h_ready"] = vh
                else:
                    cx.wait("gpsimd", sm["hT"], vh[-1])
                    st[i]["h_store"] = cx.dma("gpsimd", h2T_d[:, :, g0:g0 + TF].rearrange("k p n -> p k n"), h2buf, s_out)
                yield

            def out_transposes(i):
                xb = xT[i % 2]
                if i >= 1:
                    cx.wait("vector", s_out, st[i - 1]["y_store"])
                for tb in range(TBF):
                    for half in range(2):
                        bank, bi = ring.acquire()
                        for k4 in range(4):
                            kc = half * 4 + k4
                            if tb == 0:
                                cx.wait("tensor", sm["hT"], st[i]["y_ready"][kc])
                            fn = lambda e, bank=bank, k4=k4, kc=kc, tb=tb, xb=xb: e.transpose(
                                bank[:, k4 * 128:(k4 + 1) * 128], xb[:, kc, tb * 128:(tb + 1) * 128], ident)
                            if k4 == 3:
                                vtr = cx.op("tensor", fn, sm["yt"])
                            else:
                                cx.op("tensor", fn)
                        cx.wait("vector", sm["yt"], vtr)
                        vye = cx.op("vector", lambda e, bank=bank, tb=tb, half=half: e.tensor_copy(
                            tok[:, tb, half * 512:(half + 1) * 512], bank[:, :]), sm["ye"])
                        ring.release(bi, sm["ye"], vye)
                        if not (tb == TBF - 1 and half == 1):
                            yield
                st[i]["ytr_done"] = vtr
                cx.wait("gpsimd", sm["ye"], vye)
                r0 = i * TF
                st[i]["y_store"] = cx.dma("gpsimd", y_d[r0:r0 + TF, :].rearrange("(tb p) d -> p tb d", p=128), tok, s_out)
                yield

            def gate_up(i, hooks):
                hb = hT[i % 2]
                if i == 0:
                    cx.wait("tensor", s_wgu, s_wgu.n)
                if mode == 1:
                    cx.wait("tensor", sm["xin"], st[i]["xin"])
                for j in range(FC):
                    gi = len(gu_hist["sg"])
                    pg = PS[3 + gi % 2]
                    pu = PS[5 + gi % 2]
                    if gi >= 2:
                        cx.wait("tensor", sm["sg"], gu_hist["sg"][gi - 2])
                    for kc in range(KC):
                        if j == 0 and mode == 0:
                            cx.wait("tensor", sm["hT"], st[i]["h_ready"][kc])
                        fn = lambda e, pg=pg, kc=kc, j=j, hb=hb: e.matmul(
                            pg[:, 0:TF], Wg[:, kc, j * 128:(j + 1) * 128], hb[:, kc, :], start=(kc == 0), stop=(kc == KC - 1))
                        if kc == KC - 1:
                            vg = cx.op("tensor", fn, sm["g"])
                        else:
                            cx.op("tensor", fn)
                    if gi >= 2:
                        cx.wait("tensor", sm["hid"], gu_hist["hid"][gi - 2])
                    for kc in range(KC):
                        fn = lambda e, pu=pu, kc=kc, j=j, hb=hb: e.matmul(
                            pu[:, 0:TF], Wu[:, kc, j * 128:(j + 1) * 128], hb[:, kc, :], start=(kc == 0), stop=(kc == KC - 1))
                        if kc == KC - 1:
                            vu = cx.op("tensor", fn, sm["u"])
                        else:
                            cx.op("tensor", fn)
                    b = gi % 2
                    cx.wait("scalar", sm["g"], vg)
                    if gi >= 2:
                        cx.wait("scalar", sm["hid"], gu_hist["hid"][gi - 2])
                    vsg = cx.op("scalar", lambda e, pg=pg, b=b: e.activation(out=sg[b], in_=pg[:, 0:TF], func=AF.Silu), sm["sg"])
                    gu_hist["sg"].append(vsg)
                    cx.wait("vector", sm["sg"], vsg)
                    cx.wait("vector", sm["u"], vu)
                    vhid = cx.op("vector", lambda e, pu=pu, b=b, j=j: e.tensor_tensor(hid[:, j, :], pu[:, 0:TF], sg[b], ALU.mult), sm["hid"])
                    gu_hist["hid"].append(vhid)
                    if hooks is not None:
                        hooks(j)
                st[i]["gu_done"] = vu

            def down(i, hooks=None):
                s, g0 = tiles[i]
                xb = xT[i % 2]
                if i == 0:
                    cx.wait("tensor", s_wd, s_wd.n)
                if mode == 1:
                    cx.wait("vector", sm["xin"], st[i]["xin"])
                sq = []
                for dc in range(KC):
                    bank, bi = ring.acquire()
                    for j in range(FC):
                        if dc == 0:
                            cx.wait("tensor", sm["hid"], gu_hist["hid"][len(gu_hist["hid"]) - FC + j])
                        fn = lambda e, bank=bank, j=j, dc=dc: e.matmul(
                            bank[:, 0:TF], Wd[:, j, dc * 128:(dc + 1) * 128], hid[:, j, :], start=(j == 0), stop=(j == FC - 1))
                        if j == FC - 1:
                            vdn = cx.op("tensor", fn, sm["dn"])
                        else:
                            cx.op("tensor", fn)
                    cx.wait("vector", sm["dn"], vdn)
                    vres = cx.op("vector", lambda e, bank=bank, dc=dc, s=s, xb=xb: e.scalar_tensor_tensor(
                        xb[:, dc, :], bank[:, 0:TF], G_(l_in, dc, s), xb[:, dc, :], ALU.mult, ALU.add), sm["res"])
                    ring.release(bi, sm["res"], vres)
                    cx.wait("scalar", sm["res"], vres)
                    sq.append(cx.op("scalar", lambda e, dc=dc, xb=xb: e.activation(
                        out=xsqB[:, dc, :], in_=xb[:, dc, :], func=AF.Square), sm["cb"]))
                    if hooks is not None:
                        hooks(dc)
                st[i]["sqB"] = sq
                if mode == 0:
                    cx.wait("gpsimd", sm["res"], vres)
                    st[i]["x_store"] = cx.dma("gpsimd", x1T_d[:, :, g0:g0 + TF].rearrange("k p n -> p k n"), xb, s_out)

            def drain(g):
                for _ in g:
                    pass

            def gen_load(i):
                load_in(i)
                yield

            class Stepper:
                def __init__(self, gens):
                    self.gens = list(gens)

                def step(self):
                    while self.gens:
                        try:
                            next(self.gens[0])
                            return True
                        except StopIteration:
                            self.gens.pop(0)
                    return False

            load_in(0)
            if mode == 0:
                drain(transposes_in(0))
                drain(stats_norm(0, 1))
                if n > 1:
                    load_in(1)
            for i in range(n):
                gens = []
                if i >= 1:
                    gens.append(stats_norm(i - 1, 2 if mode == 0 else 3))
                    if mode == 1:
                        gens.append(out_transposes(i - 1))
                if mode == 1 and i + 1 < n:
                    gens.append(gen_load(i + 1))
                if mode == 0 and i + 1 < n:
                    gens.append(transposes_in(i + 1))
                    if i + 2 < n:
                        gens.append(gen_load(i + 2))
                    gens.append(stats_norm(i + 1, 1))
                stp = Stepper(gens)
                gate_up(i, lambda j: (stp.step() if j >= 1 else None))
                down(i, lambda dc: stp.step())
                while stp.step():
                    pass
            drain(stats_norm(n - 1, 2 if mode == 0 else 3))
            if mode == 1:
                drain(out_transposes(n - 1))
            barrier()

        class Ring:
            def __init__(self, banks):
                self.banks = banks
                self.hist = []

            def acquire(self):
                i = len(self.hist)
                n = len(self.banks)
                if i >= n:
                    for sem, val in self.hist[i - n]:
                        cx.wait("tensor", sem, val)
                self.hist.append([])
                return self.banks[i % n], i

            def release(self, i, sem, val):
                self.hist[i].append((sem, val))

        def mix_in_phase():
            cx.top = persist_top
            s_w = cx.sem("m_w")
            s_in = cx.sem("m_in")
            s_pe = cx.sem("m_pe")
            s_a = cx.sem("m_a")
            s_v = cx.sem("m_v")
            Win = v3(cx.alloc(KC * 2560, BF16), KC)
            wsT = v3(cx.alloc(512, BF16), 4)
            bs_row = cx.alloc(512, BF16)
            row32 = cx.alloc(512)
            ones32 = cx.alloc(256)
            sgn_bc = cx.alloc(512)
            eps_t = cx.alloc(1)
            h2 = [v3(cx.alloc(KC * T, BF16), KC), v3(cx.alloc(KC * T, BF16), KC)]
            qk_sb = v3(cx.alloc(8 * T, BF16), 8)
            V_sb = cx.alloc(TB * 4 * 192, BF16).rearrange("p (t h c) -> p t h c", t=TB, h=4)
            vn = v3(cx.alloc(TB * 512, BF16), TB)
            gv = [cx.alloc(512), cx.alloc(512)]
            sqv = cx.alloc(512)
            uT = v3(cx.alloc(4 * T), 4)
            gT_sb = v3(cx.alloc(4 * T, BF16), 4)
            ss = cx.alloc(8)
            sq1 = cx.alloc(8)
            rs = cx.alloc(8)

            winv = win_d.rearrange("(kc p) f -> p kc f", p=128)
            for kc in range(KC):
                cx.dma("gpsimd", Win[:, kc, :], winv[:, kc, :], s_w)
            cx.dma("gpsimd", wsT, wsT_d.rearrange("p (g t) -> p g t", g=4), s_w)
            cx.dma("gpsimd", bs_row[0:1, :], sgub_d, s_w)
            cx.dma("gpsimd", row32[0:1, :], sgun_d, s_w)
            cx.wait("vector", s_w, s_w.n)
            cx.wait("tensor", s_w, s_w.n)
            cx.op("vector", lambda e: e.memset(ones32, 1.0))
            cx.op("vector", lambda e: e.memset(eps_t, EPS))
            vo = cx.op("vector", lambda e: e.memset(sqv, 0.0), s_v)
            cx.wait("tensor", s_v, vo)
            vb = cx.op("tensor", lambda e: e.matmul(PS[7][:, :], ones32[0:1, 0:128], row32[0:1, :], start=True, stop=True), s_pe)
            cx.wait("vector", s_pe, vb)
            vo = cx.op("vector", lambda e: e.tensor_copy(sgn_bc, PS[7][:, :]), s_v)

            ring = Ring([PS[i] for i in range(6)])
            tiles = [(s, ti) for s in range(NSEQ) for ti in range(S // T)]
            if debug and "ntiles" in debug:
                tiles = tiles[:debug["ntiles"]]
            pe_tile_end = []
            last_stores = 0
            for it, (s, ti) in enumerate(tiles):
                g0 = s * S + ti * T
                halo = (s == 2 and not (2 <= ti < 6))
                hb_ = h2[it % 2]
                if it >= 2:
                    cx.wait("sync", s_pe, pe_tile_end[it - 2])
                vin = cx.dma("sync", hb_, h2T_d[:, :, g0:g0 + T].rearrange("k p n -> p k n"), s_in)
                cx.wait("tensor", s_in, vin)
                cx.wait("scalar", s_out, last_stores)
                cx.wait("vector", s_out, last_stores)
                va_last = 0
                for fcn in (range(4, 8) if halo else range(8)):
                    bank, bi = ring.acquire()
                    for kc in range(KC):
                        fn = lambda e, bank=bank, kc=kc, fcn=fcn, hb_=hb_: e.matmul(
                            bank[:, :], Win[:, kc, fcn * 128:(fcn + 1) * 128], hb_[:, kc, :], start=(kc == 0), stop=(kc == KC - 1))
                        if kc == KC - 1:
                            vp = cx.op("tensor", fn, s_pe)
                        else:
                            cx.op("tensor", fn)
                    cx.wait("scalar", s_pe, vp)
                    va_last = cx.op("scalar", lambda e, bank=bank, fcn=fcn: e.activation(
                        out=qk_sb[:, fcn, :], in_=bank[:, :], func=AF.Copy, scale=(0.125 if fcn < 4 else 1.0)), s_a)
                    ring.release(bi, s_a, va_last)
                cx.wait("gpsimd", s_a, va_last)
                if not halo:
                    cx.dma("gpsimd", qT_d[:, :, g0:g0 + T].rearrange("k p n -> p k n"), qk_sb[:, 0:4, :], s_out)
                cx.dma("gpsimd", kT_d[:, :, g0:g0 + T].rearrange("k p n -> p k n"), qk_sb[:, 4:8, :], s_out)
                for tb in range(TB):
                    blk = g0 // 128 + tb
                    bank, bi = ring.acquire()
                    for kc in range(KC):
                        fn = lambda e, bank=bank, kc=kc, tb=tb, hb_=hb_: e.matmul(
                            bank[:, :], hb_[:, kc, tb * 128:(tb + 1) * 128], Win[:, kc, 1024:1536], start=(kc == 0), stop=(kc == KC - 1))
                        if kc == KC - 1:
                            vp = cx.op("tensor", fn, s_pe)
                        else:
                            cx.op("tensor", fn)
                    cx.wait("vector", s_pe, vp)
                    bv = bank[:, :].rearrange("p (h c) -> p h c", h=4)
                    cx.op("vector", lambda e, bv=bv, tb=tb, blk=blk: e.tensor_scalar(
                        V_sb[:, tb, :, 0:64], bv[:, :, 0:64], valid[:, blk:blk + 1], None, ALU.mult))
                    cx.op("vector", lambda e, bv=bv, tb=tb, blk=blk: e.tensor_scalar(
                        V_sb[:, tb, :, 128:192], bv[:, :, 64:128], valid[:, blk:blk + 1], None, ALU.mult))
                    vv = cx.op("vector", lambda e, tb=tb, blk=blk: e.tensor_scalar(
                        V_sb[:, tb, :, 64:128], ones32.rearrange("p (h c) -> p h c", h=4), valid[:, blk:blk + 1], None, ALU.mult), s_v)
                    ring.release(bi, s_v, vv)
                    cx.wait("gpsimd", s_v, vv)
                    cx.dma("gpsimd", V_d[:, g0 + tb * 128:g0 + (tb + 1) * 128, :].rearrange("h p c -> p h c"), V_sb[:, tb], s_out)
                if not halo:
                    vvn = []
                    for tb in range(TB):
                        b = tb % 2
                        bank, bi = ring.acquire()
                        for kc in range(KC):
                            fn = lambda e, bank=bank, kc=kc, tb=tb, hb_=hb_: e.matmul(
                                bank[:, :], hb_[:, kc, tb * 128:(tb + 1) * 128], Win[:, kc, 2048:2560], start=(kc == 0), stop=(kc == KC - 1))
                            if kc == KC - 1:
                                vp = cx.op("tensor", fn, s_pe)
                            else:
                                cx.op("tensor", fn)
                        cx.wait("scalar", s_pe, vp)
                        if tb >= 2:
                            cx.wait("scalar", s_v, vvn[tb - 2])
                        va = cx.op("scalar", lambda e, bank=bank, b=b: e.activation(out=gv[b], in_=bank[:, :], func=AF.Gelu_apprx_tanh), s_a)
                        ring.release(bi, s_a, va)
                        cx.wait("vector", s_a, va)
                        v1 = cx.op("vector", lambda e, b=b: e.tensor_tensor(sqv, gv[b], gv[b], ALU.mult), s_v)
                        cx.wait("vector", s_v, v1)
                        v2 = cx.op("vector", lambda e, tb=tb: e.reduce_sum(ss[:, tb:tb + 1], sqv, mybir.AxisListType.X), s_v)
                        cx.wait("scalar", s_v, v2)
                        va2 = cx.op("scalar", lambda e, tb=tb: e.activation(
                            out=sq1[:, tb:tb + 1], in_=ss[:, tb:tb + 1], func=AF.Sqrt, bias=eps_t[:, 0:1], scale=1.0 / 512), s_a)
                        cx.wait("vector", s_a, va2)
                        v3_ = cx.op("vector", lambda e, tb=tb: e.reciprocal(rs[:, tb:tb + 1], sq1[:, tb:tb + 1]), s_v)
                        cx.wait("vector", s_v, v3_)
                        v4 = cx.op("vector", lambda e, tb=tb, b=b: e.scalar_tensor_tensor(
                            vn[:, tb, :], gv[b], rs[:, tb:tb + 1], sgn_bc, ALU.mult, ALU.mult), s_v)
                        vvn.append(v4)
                    vu = []
                    for g in range(4):
                        bank, bi = ring.acquire()
                        for kc in range(KC):
                            fn = lambda e, bank=bank, kc=kc, g=g, hb_=hb_: e.matmul(
                                bank[:, :], Win[:, kc, 1536 + g * 128:1536 + (g + 1) * 128], hb_[:, kc, :], start=(kc == 0), stop=(kc == KC - 1))
                            if kc == KC - 1:
                                vp = cx.op("tensor", fn, s_pe)
                            else:
                                cx.op("tensor", fn)
                        cx.wait("scalar", s_pe, vp)
                        va = cx.op("scalar", lambda e, bank=bank, g=g: e.activation(out=uT[:, g, :], in_=bank[:, :], func=AF.Gelu_apprx_tanh), s_a)
                        ring.release(bi, s_a, va)
                        vu.append(va)
                    for g in range(4):
                        bank, bi = ring.acquire()
                        for tb in range(TB):
                            if g == 0:
                                cx.wait("tensor", s_v, vvn[tb])
                            cx.op("tensor", lambda e, bank=bank, g=g, tb=tb: e.matmul(
                                bank[:, tb * 128:(tb + 1) * 128], vn[:, tb, g * 128:(g + 1) * 128], wsT[:, g, :], start=True, stop=False))
                            fn = lambda e, bank=bank, g=g, tb=tb: e.matmul(
                                bank[:, tb * 128:(tb + 1) * 128], ones_bf[0:1, 0:128], bs_row[0:1, g * 128:(g + 1) * 128], start=False, stop=True)
                            if tb == TB - 1:
                                vp = cx.op("tensor", fn, s_pe)
                            else:
                                cx.op("tensor", fn)
                        cx.wait("vector", s_pe, vp)
                        cx.wait("vector", s_a, vu[g])
                        vg = cx.op("vector", lambda e, bank=bank, g=g: e.tensor_tensor(gT_sb[:, g, :], bank[:, :], uT[:, g, :], ALU.mult), s_v)
                        ring.release(bi, s_v, vg)
                    cx.wait("gpsimd", s_v, vg)
                    cx.dma("gpsimd", gT_d[:, :, g0:g0 + T].rearrange("k p n -> p k n"), gT_sb, s_out)
                pe_tile_end.append(s_pe.n)
                last_stores = s_out.n
            barrier()

        def segments(Q0, n, L):
            nb = L // 128
            units = list(range(Q0 // 64, (Q0 + n) // 64))
            segs = []
            i = 0
            while i < len(units):
                u = units[i]
                if u % 2 == 1 and i + 1 < len(units) and (u + 1) * 64 < L:
                    P, nq = u * 64, 128
                    i += 2
                else:
                    P, nq = u * 64, 64
                    i += 1
                kbs = []
                mc_ = None
                for kb in range(max(0, (P - 64) // 128), min(nb - 1, (P + nq + 63) // 128) + 1):
                    v = P - 64 - 128 * kb
                    if v not in (-128, -64, 0, 64):
                        continue
                    hh = 1 if v < 0 else 0
                    mc = v + 128 * hh
                    if nq == 128 and mc != 0:
                        continue
                    assert mc_ is None or mc_ == mc
                    mc_ = mc
                    kbs.append((kb, hh))
                segs.append((P, nq, kbs, mc_))
            return segs

        def attn_phase():
            cx.top = persist_top
            s_ld = cx.sem("a_ld")
            s_vl = cx.sem("a_vl")
            s_ps = cx.sem("a_ps")
            s_ex = cx.sem("a_ex")
            s_mk = cx.sem("a_mk")
            s_po = cx.sem("a_po")
            s_ac = cx.sem("a_ac")
            s_nm = cx.sem("a_nm")
            qT = v3(cx.alloc(4 * S, BF16), 4)
            kT = v3(cx.alloc(4 * S, BF16), 4)
            Vt = [v3(cx.alloc(48 * 192, BF16), 48), v3(cx.alloc(48 * 192, BF16), 48)]
            acc = [cx.alloc(2048), cx.alloc(2048)]
            rD = cx.alloc(2048)
            attnT = [cx.alloc(2048, BF16), cx.alloc(2048, BF16)]
            NPB = 4
            pT = [cx.alloc(256, BF16) for _ in range(NPB)]
            pTm = [cx.alloc(256, BF16) for _ in range(NPB)]
            ringS = Ring([PS[0], PS[1], PS[2], PS[3]])
            ringO = Ring([PS[4], PS[5], PS[6]])
            LOOK = 2
            MASK_ENG = "gpsimd"
            pending_store = [None]

            def flush_store():
                hp_, s_, q0_, ab_, v4_ = pending_store[0]
                cx.wait("sync", s_nm, v4_)
                at_store[ab_] = cx.dma("sync", aT_d[hp_, :, s_ * S + q0_:s_ * S + q0_ + 2048], attnT[ab_], s_out)
                pending_store[0] = None

            vt_ready = [0, 0]
            seg_i = 0
            ex_hist = []
            mk_hist = []
            po_hist = []
            vl_i = 0
            vt_free = [0, 0]
            at_i = 0
            at_store = [0, 0]
            pe_done_seq = 0
            sts = [(0, 0), (0, 2048), (1, 0), (1, 2048), (2, 1024)]
            if debug and "nst" in debug:
                sts = sts[:debug["nst"]]
            cur_seq = -1
            for (s, q0) in sts:
                if s != cur_seq:
                    cur_seq = s
                    cx.wait("sync", s_po, pe_done_seq)
                    cx.wait("sync", s_ps, s_ps.n)
                    cx.dma("sync", qT, qT_d[:, :, s * S:(s + 1) * S].rearrange("k p n -> p k n"), s_ld)
                    vq = cx.dma("sync", kT, kT_d[:, :, s * S:(s + 1) * S].rearrange("k p n -> p k n"), s_ld)
                    cx.wait("tensor", s_ld, vq)
                for hp in range(4):
                    items = []
                    for pi, (_, dil) in enumerate(PATTERNS):
                        L = S // dil
                        Q0, n = q0 // dil, 2048 // dil
                        kb_lo = max(0, (Q0 - 64) // 128)
                        kb_hi = min(L // 128 - 1, (Q0 + n + 63) // 128)
                        nkb = kb_hi - kb_lo + 1
                        vb = vl_i % 2
                        vl_i += 1
                        first = True
                        for hd in range(2):
                            for r in range(dil):
                                for (P, nq, kbs, mc) in segments(Q0, n, L):
                                    items.append(dict(pi=pi, dil=dil, Q0=Q0, kb_lo=kb_lo, nkb=nkb, vb=vb, hd=hd, r=r,
                                                      P=P, nq=nq, kbs=kbs, mc=mc, vload=first, s=s, hp=hp))
                                    first = False

                    def emit_front(it):
                        nonlocal seg_i
                        dil, r, P, nq, kbs, mc, hd = it["dil"], it["r"], it["P"], it["nq"], it["kbs"], it["mc"], it["hd"]
                        if it["vload"]:
                            vb, nkb, kb_lo = it["vb"], it["nkb"], it["kb_lo"]
                            cx.wait("sync", s_po, vt_free[vb])
                            for rr in range(dil):
                                t0 = it["s"] * S + rr + dil * 128 * kb_lo
                                src_ = V_d[it["hp"], t0:t0 + dil * (128 * nkb - 1) + 1:dil, :].rearrange("(kb p) c -> p kb c", p=128)
                                vvl = cx.dma("sync", Vt[vb][:, rr * nkb:(rr + 1) * nkb, :], src_, s_vl)
                            vt_ready[vb] = vvl
                        it["vready"] = vt_ready[it["vb"]]
                        h = it["hp"] * 2 + hd
                        hb = 64 * hd
                        midx = h * 3 + it["pi"]
                        nk = len(kbs)
                        it["seg"] = seg_i
                        pb = seg_i % NPB
                        it["pb"] = pb
                        bankS, bsi = ringS.acquire()
                        qcols = slice(r + dil * P, r + dil * (P + nq - 1) + 1, dil)
                        for i, (kb, hh) in enumerate(kbs):
                            kcols = slice(r + dil * 128 * kb, r + dil * (128 * kb + 127) + 1, dil)
                            fn = lambda e, bankS=bankS, i=i, nq=nq, kcols=kcols, qcols=qcols, hb=hb, hp=it["hp"]: e.matmul(
                                bankS[:, i * nq:(i + 1) * nq], kT[hb:hb + 64, hp, kcols], qT[hb:hb + 64, hp, qcols], start=True, stop=True)
                            if i == nk - 1:
                                vps = cx.op("tensor", fn, s_ps)
                            else:
                                cx.op("tensor", fn)
                        cx.wait("scalar", s_ps, vps)
                        if seg_i >= NPB:
                            cx.wait("scalar", s_mk, mk_hist[seg_i - NPB])
                        vex = cx.op("scalar", lambda e, bankS=bankS, pb=pb, w=nk * nq: e.activation(
                            out=pT[pb][:, 0:w], in_=bankS[:, 0:w], func=AF.Exp), s_ex)
                        ringS.release(bsi, s_ex, vex)
                        cx.wait(MASK_ENG, s_ex, vex)
                        if seg_i >= NPB:
                            cx.wait(MASK_ENG, s_po, po_hist[seg_i - NPB])
                        if nk == 2:
                            m_ap = Emask3[:, midx, :].rearrange("p (h q) -> p h q", h=2)[:, :, mc:mc + nq]
                            o_ap = pTm[pb][:, 0:2 * nq].rearrange("p (h q) -> p h q", h=2)
                            i_ap = pT[pb][:, 0:2 * nq].rearrange("p (h q) -> p h q", h=2)
                        else:
                            hh = kbs[0][1]
                            m_ap = Emask3[:, midx, hh * 128 + mc:hh * 128 + mc + nq]
                            o_ap = pTm[pb][:, 0:nq]
                            i_ap = pT[pb][:, 0:nq]
                        vmk = cx.op(MASK_ENG, lambda e, o_ap=o_ap, i_ap=i_ap, m_ap=m_ap: e.tensor_tensor(o_ap, i_ap, m_ap, ALU.mult), s_mk, k=1)
                        mk_hist.append(vmk)
                        it["vmk"] = vmk
                        seg_i += 1

                    def emit_back(it):
                        dil, r, P, nq, kbs, hd, vb = it["dil"], it["r"], it["P"], it["nq"], it["kbs"], it["hd"], it["vb"]
                        nk = len(kbs)
                        pb = it["pb"]
                        vcols = slice(0, 128) if hd == 0 else slice(64, 192)
                        bankO, boi = ringO.acquire()
                        cx.wait("tensor", s_vl, it["vready"])
                        cx.wait("tensor", s_mk, it["vmk"])
                        for i, (kb, hh) in enumerate(kbs):
                            blk = r * it["nkb"] + (kb - it["kb_lo"])
                            fn = lambda e, bankO=bankO, i=i, nq=nq, blk=blk, vb=vb, vcols=vcols, pb=pb, nk=nk: e.matmul(
                                bankO[:, 0:nq], Vt[vb][:, blk, vcols], pTm[pb][:, i * nq:(i + 1) * nq], start=(i == 0), stop=(i == nk - 1))
                            if i == nk - 1:
                                vpo = cx.op("tensor", fn, s_po)
                            else:
                                cx.op("tensor", fn)
                        po_hist.append(vpo)
                        vt_free[vb] = vpo
                        c0 = r + dil * (P - it["Q0"])
                        dst = acc[hd][:, c0:c0 + dil * (nq - 1) + 1:dil]
                        cx.wait("vector", s_po, vpo)
                        if it["pi"] == 0:
                            vac = cx.op("vector", lambda e, dst=dst, bankO=bankO, nq=nq: e.tensor_copy(dst, bankO[:, 0:nq]), s_ac)
                        else:
                            vac = cx.op("vector", lambda e, dst=dst, bankO=bankO, nq=nq: e.tensor_tensor(dst, bankO[:, 0:nq], dst, ALU.add), s_ac)
                        ringO.release(boi, s_ac, vac)

                    for idx in range(len(items) + LOOK):
                        if idx < len(items):
                            emit_front(items[idx])
                        if idx - LOOK >= 0:
                            emit_back(items[idx - LOOK])
                    ab = at_i % 2
                    at_i += 1
                    cx.wait("vector", s_ac, s_ac.n)
                    cx.wait("vector", s_out, at_store[ab])
                    v1 = cx.op("vector", lambda e: e.tensor_copy(rD[0:64, :], acc[0][64:128, :]), s_nm)
                    v2 = cx.op("vector", lambda e: e.tensor_copy(rD[64:128, :], acc[1][0:64, :]), s_nm)
                    cx.wait("vector", s_nm, v2)
                    v2b = cx.op("vector", lambda e: e.reciprocal(rD, rD), s_nm)
                    cx.wait("vector", s_nm, v2b)
                    v3_ = cx.op("vector", lambda e, ab=ab: e.tensor_tensor(attnT[ab][0:64, :], acc[0][0:64, :], rD[0:64, :], ALU.mult), s_nm)
                    v4 = cx.op("vector", lambda e, ab=ab: e.tensor_tensor(attnT[ab][64:128, :], acc[1][64:128, :], rD[64:128, :], ALU.mult), s_nm)
                    if pending_store[0] is not None:
                        flush_store()
                    pending_store[0] = (hp, s, q0, ab, v4)
                if pending_store[0] is not None:
                    flush_store()
                pe_done_seq = s_po.n
            barrier()

        def mix_out_phase():
            cx.top = persist_top
            s_w = cx.sem("o_w")
            s_in = cx.sem("o_in")
            s_pe = cx.sem("o_pe")
            s_res = cx.sem("o_res")
            s_ca = cx.sem("o_ca")
            s_st = cx.sem("o_st")
            s_sq = cx.sem("o_sq")
            s_rs = cx.sem("o_rs")
            s_tA = cx.sem("o_tA")
            s_hT = cx.sem("o_hT")
            Wout = v3(cx.alloc(KC * D, BF16), KC)
            eps_t = cx.alloc(1)
            xT = v3(cx.alloc(KC * T), KC)
            mixT = v3(cx.alloc(KC * T, BF16), KC)
            hT = v3(cx.alloc(KC * T, BF16), KC)
            xsq = v3(cx.alloc(KC * T, BF16), KC)
            rstd = cx.alloc(T)
            sqt = cx.alloc(T)
            tmpA = [cx.alloc(T), cx.alloc(T)]
            woutv = wout_d.rearrange("(kc p) f -> p kc f", p=128)
            for kc in range(KC):
                cx.dma("gpsimd", Wout[:, kc, :], woutv[:, kc, :], s_w)
            cx.wait("tensor", s_w, s_w.n)
            cx.op("vector", lambda e: e.memset(eps_t, EPS))
            ring = Ring([PS[0], PS[1], PS[2]])
            tiles = [(s, ti) for s in range(NSEQ) for ti in range(S // T) if s < 2 or 2 <= ti < 6]
            if debug and "ntiles" in debug:
                tiles = tiles[:debug["ntiles"]]
            hT_hist = []
            last_x = 0
            last_h = 0
            for (s, ti) in tiles:
                g0 = s * S + ti * T
                cx.wait("sync", s_out, max(last_x, last_h))
                cx.wait("sync", s_pe, s_pe.n)
                cx.wait("sync", s_hT, s_hT.n)
                cx.dma("sync", xT, x1T_d[:, :, g0:g0 + T].rearrange("k p n -> p k n"), s_in)
                cx.dma("sync", mixT[:, 0:4, :], aT_d[:, :, g0:g0 + T].rearrange("k p n -> p k n"), s_in)
                vin = cx.dma("sync", mixT[:, 4:8, :], gT_d[:, :, g0:g0 + T].rearrange("k p n -> p k n"), s_in)
                cx.wait("tensor", s_in, vin)
                cx.wait("vector", s_in, vin)
                res_vals = []
                for dc in range(KC):
                    bank, bi = ring.acquire()
                    for kc in range(KC):
                        fn = lambda e, bank=bank, kc=kc, dc=dc: e.matmul(
                            bank[:, :], Wout[:, kc, dc * 128:(dc + 1) * 128], mixT[:, kc, :], start=(kc == 0), stop=(kc == KC - 1))
                        if kc == KC - 1:
                            vp = cx.op("tensor", fn, s_pe)
                        else:
                            cx.op("tensor", fn)
                    cx.wait("vector", s_pe, vp)
                    vres = cx.op("vector", lambda e, bank=bank, dc=dc, s=s: e.scalar_tensor_tensor(
                        xT[:, dc, :], bank[:, :], G_(1, dc, s), xT[:, dc, :], ALU.mult, ALU.add), s_res)
                    ring.release(bi, s_res, vres)
                    res_vals.append(vres)
                ca_vals = []
                for kc in range(KC):
                    cx.wait("scalar", s_res, res_vals[kc])
                    ca_vals.append(cx.op("scalar", lambda e, kc=kc: e.activation(out=xsq[:, kc, :], in_=xT[:, kc, :], func=AF.Square), s_ca))
                cx.wait("gpsimd", s_res, res_vals[-1])
                last_x = cx.dma("gpsimd", x2T_d[:, :, g0:g0 + T].rearrange("k p n -> p k n"), xT, s_out)
                cx.wait("tensor", s_sq, s_sq.n)
                for kc in range(KC):
                    cx.wait("tensor", s_ca, ca_vals[kc])
                    fn = lambda e, kc=kc: e.matmul(PS[3][:, :], ones_bf, xsq[:, kc, :], start=(kc == 0), stop=(kc == KC - 1))
                    if kc == KC - 1:
                        vst = cx.op("tensor", fn, s_st)
                    else:
                        cx.op("tensor", fn)
                cx.wait("scalar", s_st, vst)
                vsq = cx.op("scalar", lambda e: e.activation(out=sqt, in_=PS[3][:, :], func=AF.Sqrt, bias=eps_t[:, 0:1], scale=1.0 / D), s_sq)
                cx.wait("vector", s_sq, vsq)
                vrs = cx.op("vector", lambda e: e.reciprocal(rstd, sqt), s_rs)
                cx.wait("vector", s_rs, vrs)
                cx.wait("scalar", s_out, last_h)
                for kc in range(KC):
                    b = kc % 2
                    if len(hT_hist) >= 2:
                        cx.wait("vector", s_hT, hT_hist[-2])
                    vt = cx.op("vector", lambda e, kc=kc, b=b, s=s: e.scalar_tensor_tensor(
                        tmpA[b], xT[:, kc, :], A_(2, kc, s), rstd, ALU.mult, ALU.mult), s_tA)
                    cx.wait("scalar", s_tA, vt)
                    va = cx.op("scalar", lambda e, kc=kc, b=b, s=s: e.activation(
                        out=hT[:, kc, :], in_=tmpA[b], func=AF.Identity, bias=B_(2, kc, s), scale=1.0), s_hT)
                    hT_hist.append(va)
                cx.wait("gpsimd", s_hT, va)
                last_h = cx.dma("gpsimd", h3T_d[:, :, g0:g0 + T].rearrange("k p n -> p k n"), hT, s_out)
            barrier()

        if stop_after >= 1:
            ffn_phase(0, preloaded=("gu", "d"))
        if stop_after >= 2:
            mix_in_phase()
        if stop_after >= 3:
            attn_phase()
        if stop_after >= 4:
            mix_out_phase()
        if stop_after >= 5:
            ffn_phase(1)

        barrier()
        cx.emit_all()
    return nc


def _core_inputs(core, inp):
    xs = inp["x_sample"]
    xp = inp["x_prompt"]
    x = np.zeros((NSEQ, S, D), np.float32)
    x[0] = xs[2 * core]
    x[1] = xs[2 * core + 1]
    pb, qd = core // 4, core % 4
    lo = qd * 2048 - 1024
    valid = np.ones((NSEQ, S), np.float32)
    a, b = max(lo, 0), min(lo + S, 8192)
    x[2, a - lo:b - lo] = xp[pb, a:b]
    valid[2, :] = 0.0
    valid[2, a - lo:b - lo] = 1.0
    c3 = np.stack([inp["c_sample"][2 * core], inp["c_sample"][2 * core + 1], inp["c_prompt"][pb]], 0)
    cT = np.ascontiguousarray(c3.reshape(3, KC, 128).transpose(2, 1, 0)).reshape(128, KC * 3)
    return x.reshape(NTOK, D), cT, np.ascontiguousarray(valid.reshape(NTOK // 128, 128).T)


def _shared_inputs(inp):
    def pcol(v):
        return np.ascontiguousarray(np.asarray(v, np.float32).reshape(-1, 128).T)
    pvec = np.concatenate([pcol(inp["ada_b"][0]), pcol(inp["ffn1_norm"][0]), pcol(inp["mix_norm"][0]),
                           pcol(inp["ffn2_norm"][0]), pcol(inp["final_norm"])], axis=1)
    kp = np.arange(128)[:, None]
    col = np.arange(256)[None, :]
    hh, q = col // 128, col % 128
    rel = 128 * hh + kp - q - 64
    sh = {
        "pvec": pvec.astype(np.float32),
        "ident": np.eye(128, dtype=np.float32),
        "absrel": np.abs(rel).astype(np.float32),
        "band": (np.abs(rel) <= 64).astype(np.float32),
        "ada_w": np.ascontiguousarray(inp["ada_w"][0]),
        "wg1": np.ascontiguousarray(inp["ffn1_w_gate"][0]), "wu1": np.ascontiguousarray(inp["ffn1_w_up"][0]),
        "wd1": np.ascontiguousarray(inp["ffn1_w_down"][0]),
        "wg2": np.ascontiguousarray(inp["ffn2_w_gate"][0]), "wu2": np.ascontiguousarray(inp["ffn2_w_up"][0]),
        "wd2": np.ascontiguousarray(inp["ffn2_w_down"][0]),
        "w_in": np.ascontiguousarray(inp["w_in"][0]),
        "wsT": np.ascontiguousarray(inp["sgu_w"][0].transpose(2, 0, 1)).reshape(128, 512),
        "sgu_b": np.ascontiguousarray(inp["sgu_b"][0]).reshape(1, 512),
        "sgu_norm": np.ascontiguousarray(inp["sgu_norm"][0]).reshape(1, 512),
        "w_out": np.ascontiguousarray(inp["w_out"][0]),
    }
    return sh


def make_in_maps(inp):
    inp = {k: np.asarray(v) for k, v in inp.items()}
    sh = _shared_inputs(inp)
    maps = []
    for core in range(N_CORES):
        x, cT, valid = _core_inputs(core, inp)
        m = dict(sh)
        m["x"] = x
        m["cT"] = cT
        m["valid"] = valid
        maps.append(m)
    return maps


def kernel(**inputs):
    maps = make_in_maps(inputs)
    nc = build_program()
    res = run_bass_kernel_spmd(nc, maps, core_ids=list(range(N_CORES)))
    ys = np.empty((16, S, D), np.float32)
    yp = np.empty((2, 8192, D), np.float32)
    for core in range(N_CORES):
        y = res.results[core]["y"]
        ys[2 * core] = y[0:S]
        ys[2 * core + 1] = y[S:2 * S]
        pb, qd = core // 4, core % 4
        yp[pb, qd * 2048:(qd + 1) * 2048] = y[2 * S:]
    return (yp, ys)
```

```python
import numpy as np
from contextlib import ExitStack
import concourse.bass as bass
import concourse.mybir as mybir
from concourse.bass_utils import run_bass_kernel_spmd

F32 = mybir.dt.float32
BF16 = mybir.dt.bfloat16
AF = mybir.ActivationFunctionType
ALU = mybir.AluOpType

D = 1024
KC = 8
DFF = 2816
FC = 22
S = 4096
NSEQ = 3
NTOK = NSEQ * S
T = 512
TB = 4
NQTOK = 2 * S + 2048
EPS = 1e-6
HEADS = 8
PATTERNS = ((128, 1), (512, 4), (2048, 16))
N_CORES = 8
ARENA_F32 = 53200

ENGINES = ("tensor", "vector", "scalar", "gpsimd", "sync")


class Sem:
    def __init__(self, h):
        self.h = h
        self.n = 0


class Ctx:
    def __init__(self, nc, es):
        self.nc = nc
        self.es = es
        self.q = {e: [] for e in ENGINES}
        self.nsem = 0
        self.arena = es.enter_context(nc.sbuf_tensor("arena", [128, ARENA_F32], F32))
        self.top = 0
        self.psum = [es.enter_context(nc.psum_tensor("ps%d" % i, [128, 512], F32)) for i in range(8)]

    def sem(self, name):
        h = self.es.enter_context(self.nc.semaphore(name))
        self.nsem += 1
        return Sem(h)

    def alloc(self, cols, dtype=F32):
        if dtype == BF16:
            ncol32 = (cols + 1) // 2
        else:
            ncol32 = cols
        a = self.top
        self.top += ncol32
        assert self.top <= ARENA_F32, ("SBUF arena overflow", self.top)
        ap = self.arena[:, a:a + ncol32]
        if dtype == BF16:
            ap = ap.bitcast(BF16)
        return ap

    def op(self, eng, fn, sig=None, k=None):
        if sig is None:
            self.q[eng].append(fn)
            return None
        if k is None:
            k = 16 if eng in ("sync", "gpsimd_dma") else 1
        sig.n += k
        h = sig.h
        self.q[eng].append(lambda e, fn=fn, h=h, k=k: fn(e).then_inc(h, k))
        return sig.n

    def dma(self, eng, out, in_, sig):
        sig.n += 16
        h = sig.h
        self.q[eng].append(lambda e, out=out, in_=in_, h=h: e.dma_start(out=out, in_=in_).then_inc(h, 16))
        return sig.n

    def wait(self, eng, sem, val):
        if val is None or val <= 0:
            return
        h = sem.h
        self.q[eng].append(lambda e, h=h, val=val: e.wait_ge(h, val))

    def emit_all(self):
        nc = self.nc
        with nc.Block() as blk:
            for name in ENGINES:
                ops = self.q[name]

                def body(e, ops=ops):
                    for f in ops:
                        f(e)
                getattr(blk, name)(body)


def v3(ap, a):
    return ap.rearrange("p (a b) -> p a b", a=a)


def build_program(debug=None):
    nc = bass.Bass("TRN2", target_bir_lowering=False)
    dt = nc.dram_tensor

    def din(name, shape, dtype=F32):
        return dt(name, list(shape), dtype, kind="ExternalInput").ap()

    x_d = din("x", [NTOK, D])
    cT_d = din("cT", [128, KC * NSEQ])
    pvec_d = din("pvec", [128, 104])
    valid_d = din("valid", [128, NTOK // 128])
    ident_d = din("ident", [128, 128])
    absrel_d = din("absrel", [128, 256])
    band_d = din("band", [128, 256])
    adaw_d = din("ada_w", [D, 9 * D])
    wg_d = [din("wg1", [D, DFF]), din("wg2", [D, DFF])]
    wu_d = [din("wu1", [D, DFF]), din("wu2", [D, DFF])]
    wd_d = [din("wd1", [DFF, D]), din("wd2", [DFF, D])]
    win_d = din("w_in", [D, 2560])
    wsT_d = din("wsT", [128, 512])
    sgub_d = din("sgu_b", [1, 512])
    sgun_d = din("sgu_norm", [1, 512])
    wout_d = din("w_out", [D, D])
    y_d = dt("y", [NQTOK, D], F32, kind="ExternalOutput").ap()

    skind = "ExternalOutput" if (debug and not debug.get("internal")) else "Internal"
    x1T_d = dt("x1T", [KC, 128, NTOK], F32, kind=skind).ap()
    h2T_d = dt("h2T", [KC, 128, NTOK], BF16, kind=skind).ap()
    qT_d = dt("qT", [4, 128, NTOK], BF16, kind=skind).ap()
    kT_d = dt("kT", [4, 128, NTOK], BF16, kind=skind).ap()
    V_d = dt("V", [4, NTOK, 192], BF16, kind=skind).ap()
    gT_d = dt("gT", [4, 128, NTOK], BF16, kind=skind).ap()
    aT_d = dt("aT", [4, 128, NTOK], BF16, kind=skind).ap()
    x2T_d = dt("x2T", [KC, 128, NTOK], F32, kind=skind).ap()
    h3T_d = dt("h3T", [KC, 128, NTOK], BF16, kind=skind).ap()

    stop_after = debug.get("stop_after", 99) if debug else 99

    with ExitStack() as es:
        cx = Ctx(nc, es)
        PS = cx.psum

        ident = cx.alloc(128)
        ones_bf = cx.alloc(128, BF16)
        pvec = cx.alloc(104)
        valid = cx.alloc(NTOK // 128)
        modT = cx.alloc(72 * 3)
        Amod = cx.alloc(3 * 8 * 3)
        Gmod = cx.alloc(3 * 8 * 3)
        Emask = cx.alloc(24 * 256, BF16)
        persist_top = cx.top

        modT3 = v3(modT, 72)
        Amod4 = Amod.rearrange("p (l k s) -> p l k s", l=3, k=8)
        Gmod4 = Gmod.rearrange("p (l k s) -> p l k s", l=3, k=8)
        Emask3 = v3(Emask, 24)

        def A_(l, kc, s):
            return Amod4[:, l, kc, s:s + 1]

        def B_(l, kc, s):
            return modT3[:, l * 24 + kc, s:s + 1]

        def G_(l, kc, s):
            return Gmod4[:, l, kc, s:s + 1]

        s_out = cx.sem("s_out")

        def barrier():
            for e in ENGINES:
                cx.wait(e, s_out, s_out.n)

        s_ld = cx.sem("p0_ld")
        s_c = cx.sem("p0_c")
        s_aw = cx.sem("p0_aw")
        s_awf = cx.sem("p0_awf")
        s_v = cx.sem("p0_v")
        s_a = cx.sem("p0_a")

        cx.top = ARENA_F32 - 9400
        cT = cx.alloc(24)
        scT = cx.alloc(24)
        absrel = cx.alloc(256)
        band = cx.alloc(256)
        etmp = [cx.alloc(256), cx.alloc(256)]
        tmp24 = cx.alloc(24)
        awbuf = [cx.alloc(8 * 512), cx.alloc(8 * 512)]

        for dst, src in ((ident, ident_d), (pvec, pvec_d), (valid, valid_d), (cT, cT_d),
                         (absrel, absrel_d), (band, band_d)):
            cx.dma("sync", dst, src, s_ld)
        n_ld = s_ld.n
        cx.wait("scalar", s_ld, n_ld)
        cx.wait("vector", s_ld, n_ld)
        cx.op("vector", lambda e: e.memset(ones_bf, 1.0))
        v_sc = cx.op("scalar", lambda e: e.activation(out=scT, in_=cT, func=AF.Silu), s_c)
        cx.wait("tensor", s_c, v_sc)
        i = 0
        for h in range(HEADS):
            slope = 2.0 ** (-8.0 * (h + 1) / HEADS)
            for pi, (_, dil) in enumerate(PATTERNS):
                b = i % 2
                if i >= 2:
                    cx.wait("scalar", s_v, vmask[b])
                va = cx.op("scalar", lambda e, b=b, sc=-slope * dil: e.activation(
                    out=etmp[b], in_=absrel, func=AF.Exp, scale=sc), s_a)
                cx.wait("vector", s_a, va)
                vv = cx.op("vector", lambda e, b=b, idx=h * 3 + pi: e.tensor_tensor(
                    Emask3[:, idx, :], etmp[b], band, ALU.mult), s_v)
                if i == 0:
                    vmask = [0, 0]
                vmask[b] = vv
                i += 1
        adaw_v = adaw_d.rearrange("(kc p) f -> p kc f", p=128)
        aw_sems = [s_aw, cx.sem("p0_aw2")]
        for ch in range(18):
            b = ch % 2
            qn = "sync" if b == 0 else "scalar"
            if ch >= 2:
                cx.wait(qn, s_awf, ch - 1)
            vld = cx.dma(qn, v3(awbuf[b], 8), adaw_v[:, :, ch * 512:(ch + 1) * 512], aw_sems[b])
            cx.wait("tensor", aw_sems[b], vld)
            for fl in range(4):
                col = (ch * 4 + fl) * 3
                for kc in range(KC):
                    fn = (lambda e, b=b, fl=fl, kc=kc, col=col: e.matmul(
                        PS[0][:, col:col + 3], v3(awbuf[b], 8)[:, kc, fl * 128:(fl + 1) * 128],
                        scT[:, kc * 3:(kc + 1) * 3], start=(kc == 0), stop=(kc == KC - 1)))
                    if fl == 3 and kc == KC - 1:
                        cx.op("tensor", fn, s_awf)
                    else:
                        cx.op("tensor", fn)
        cx.wait("vector", s_awf, 18)
        psm3 = PS[0][:, 0:216].rearrange("p (a b) -> p a b", b=3)
        for s in range(3):
            vm = cx.op("vector", lambda e, s=s: e.tensor_tensor(modT3[:, :, s], psm3[:, :, s], pvec[:, 0:72], ALU.add), s_v)
        cx.wait("vector", s_v, vm)
        for l in range(3):
            gl = pvec[:, 72 + 8 * l:80 + 8 * l]
            for s in range(3):
                v1 = cx.op("vector", lambda e, l=l, s=s: e.tensor_scalar(
                    tmp24[:, 0:8], modT3[:, l * 24 + 8:l * 24 + 16, s], 1.0, None, ALU.add), s_v)
                cx.wait("vector", s_v, v1)
                v2 = cx.op("vector", lambda e, l=l, s=s, gl=gl: e.tensor_tensor(
                    Amod4[:, l, :, s], tmp24[:, 0:8], gl, ALU.mult), s_v)
                cx.wait("vector", s_v, v2)
                rw = 1.0 if l == 1 else 0.5
                cx.op("vector", lambda e, l=l, s=s, rw=rw: e.tensor_scalar(
                    Gmod4[:, l, :, s], modT3[:, l * 24 + 16:l * 24 + 24, s], rw, None, ALU.mult), s_v)
        v_p0 = s_v.n
        for e in ENGINES:
            if e == "gpsimd":
                continue
            cx.wait(e, s_v, v_p0)
            cx.wait(e, s_a, s_a.n)

        TF = 256
        TBF = 2
        W_BASE = persist_top
        WCOLS = KC * DFF // 2
        FFN_TILE_BASE = W_BASE + 3 * WCOLS

        def ffn_weight_aps():
            a = W_BASE
            Wg = v3(cx.arena[:, a:a + WCOLS].bitcast(BF16), KC)
            Wu = v3(cx.arena[:, a + WCOLS:a + 2 * WCOLS].bitcast(BF16), KC)
            Wd = v3(cx.arena[:, a + 2 * WCOLS:a + 3 * WCOLS].bitcast(BF16), FC)
            return Wg, Wu, Wd

        ffn_wsem = {}

        def ffn_issue_weights(mode, parts):
            if mode not in ffn_wsem:
                ffn_wsem[mode] = (cx.sem("f%d_wgu" % mode), cx.sem("f%d_wd" % mode))
            s_wgu, s_wd = ffn_wsem[mode]
            Wg, Wu, Wd = ffn_weight_aps()
            if "gu" in parts:
                wgv = wg_d[mode].rearrange("(kc p) f -> p kc f", p=128)
                wuv = wu_d[mode].rearrange("(kc p) f -> p kc f", p=128)
                for kc in range(KC):
                    cx.dma("gpsimd", Wg[:, kc, :], wgv[:, kc, :], s_wgu)
                    cx.dma("gpsimd", Wu[:, kc, :], wuv[:, kc, :], s_wgu)
            if "d" in parts:
                wdv = wd_d[mode].rearrange("(j p) d -> p j d", p=128)
                for j0 in range(0, FC, 2):
                    cx.dma("gpsimd", Wd[:, j0:j0 + 2, :], wdv[:, j0:j0 + 2, :], s_wd)

        ffn_issue_weights(0, ("gu", "d"))

        def ffn_phase(mode, preloaded=()):
            cx.top = FFN_TILE_BASE
            pre = "f%d_" % mode
            for part in ("gu", "d"):
                if part not in preloaded:
                    ffn_issue_weights(mode, (part,))
            s_wgu, s_wd = ffn_wsem[mode]
            names = ["xin", "trp", "cv", "ca", "st1", "sq1", "rs1", "st2", "sq2", "rs2", "tA", "hT",
                     "g", "u", "sg", "hid", "dn", "res", "cb", "yt", "ye"]
            sm = {n: cx.sem(pre + n) for n in names}
            Wg, Wu, Wd = ffn_weight_aps()

            xT = [v3(cx.alloc(KC * TF), KC), v3(cx.alloc(KC * TF), KC)]
            hT = [v3(cx.alloc(KC * TF, BF16), KC), v3(cx.alloc(KC * TF, BF16), KC)]
            hid = v3(cx.alloc(FC * TF, BF16), FC)
            xsqA = v3(cx.alloc(KC * TF, BF16), KC)
            xsqB = v3(cx.alloc(KC * TF, BF16), KC)
            tok = cx.alloc(TBF * D).rearrange("p (a b) -> p a b", a=TBF)
            h2buf = xsqA if mode == 0 else None
            tmpA = [cx.alloc(TF), cx.alloc(TF)]
            sg = [cx.alloc(TF), cx.alloc(TF)]
            rstd1 = rstd2 = cx.alloc(TF)
            sqt1 = sqt2 = cx.alloc(TF)
            last_rs = [None]
            eps_t = cx.alloc(1)
            cx.op("vector", lambda e: e.memset(eps_t, EPS))

            l_in = 0 if mode == 0 else 2
            if mode == 0:
                tiles = [(s, s * S + k * TF) for s in range(NSEQ) for k in range(S // TF)]
            else:
                tiles = [(s, s * S + k * TF) for s in range(NSEQ) for k in range(S // TF) if s < 2 or 4 <= k < 12]
            if debug and "ntiles" in debug:
                tiles = tiles[:debug["ntiles"]]
            n = len(tiles)
            st = [dict() for _ in range(n)]
            ring = Ring([PS[0], PS[1]])
            gu_hist = {"sg": [], "hid": []}
            tA_hist = []

            def load_in(i):
                s, g0 = tiles[i]
                if mode == 0:
                    if i >= 1:
                        cx.wait("sync", sm["trp"], st[i - 1]["tr_done"])
                    st[i]["xin"] = cx.dma("sync", tok, x_d[g0:g0 + TF, :].rearrange("(tb p) d -> p tb d", p=128), sm["xin"])
                else:
                    if i >= 2:
                        cx.wait("sync", sm["yt"], st[i - 2]["ytr_done"])
                        cx.wait("sync", sm["u"], st[i - 2]["gu_done"])
                    cx.dma("sync", xT[i % 2], x2T_d[:, :, g0:g0 + TF].rearrange("k p n -> p k n"), sm["xin"])
                    st[i]["xin"] = cx.dma("sync", hT[i % 2], h3T_d[:, :, g0:g0 + TF].rearrange("k p n -> p k n"), sm["xin"])

            def transposes_in(i):
                xb = xT[i % 2]
                cx.wait("tensor", sm["xin"], st[i]["xin"])
                if i >= 2:
                    cx.wait("vector", s_out, st[i - 2]["x_store"])
                sq = []
                hs = [st[k]["h_store"] for k in range(i) if "h_store" in st[k]]
                if hs:
                    cx.wait("scalar", s_out, hs[-1])
                for kc in range(KC):
                    bank, bi = ring.acquire()
                    for tb in range(TBF):
                        fn = lambda e, bank=bank, tb=tb, kc=kc: e.transpose(
                            bank[:, tb * 128:(tb + 1) * 128], tok[:, tb, kc * 128:(kc + 1) * 128], ident)
                        if tb == TBF - 1:
                            vtr = cx.op("tensor", fn, sm["trp"])
                        else:
                            cx.op("tensor", fn)
                    cx.wait("vector", sm["trp"], vtr)
                    vcv = cx.op("vector", lambda e, bank=bank, kc=kc, xb=xb: e.tensor_copy(xb[:, kc, :], bank[:, 0:TF]), sm["cv"])
                    ring.release(bi, sm["cv"], vcv)
                    cx.wait("scalar", sm["cv"], vcv)
                    sq.append(cx.op("scalar", lambda e, kc=kc, xb=xb: e.activation(
                        out=xsqA[:, kc, :], in_=xb[:, kc, :], func=AF.Square), sm["ca"]))
                    if kc < KC - 1:
                        yield
                st[i]["tr_done"] = vtr
                st[i]["sqA"] = sq
                yield

            def stats_norm(i, which):
                s, g0 = tiles[i]
                xb = xT[i % 2]
                if which == 1:
                    xsq, bankst, s_st, s_sq, s_rs, rstd, sqt, sq_sem, sq_vals = xsqA, PS[2], sm["st1"], sm["sq1"], sm["rs1"], rstd1, sqt1, sm["ca"], st[i]["sqA"]
                else:
                    xsq, bankst, s_st, s_sq, s_rs, rstd, sqt, sq_sem, sq_vals = xsqB, PS[7], sm["st2"], sm["sq2"], sm["rs2"], rstd2, sqt2, sm["cb"], st[i]["sqB"]
                cx.wait("tensor", s_sq, s_sq.n)
                for kc in range(KC):
                    cx.wait("tensor", sq_sem, sq_vals[kc])
                    fn = lambda e, kc=kc, xsq=xsq, bankst=bankst: e.matmul(bankst[:, 0:TF], ones_bf, xsq[:, kc, :], start=(kc == 0), stop=(kc == KC - 1))
                    if kc == KC - 1:
                        vst = cx.op("tensor", fn, s_st)
                    else:
                        cx.op("tensor", fn)
                cx.wait("scalar", s_st, vst)
                if last_rs[0] is not None:
                    cx.wait("scalar", last_rs[0][0], last_rs[0][1])
                vsq = cx.op("scalar", lambda e, sqt=sqt, bankst=bankst: e.activation(
                    out=sqt, in_=bankst[:, 0:TF], func=AF.Sqrt, bias=eps_t[:, 0:1], scale=1.0 / D), s_sq)
                cx.wait("vector", s_sq, vsq)
                vrs = cx.op("vector", lambda e, rstd=rstd, sqt=sqt: e.reciprocal(rstd, sqt), s_rs)
                last_rs[0] = (s_rs, vrs)
                yield
                cx.wait("vector", s_rs, vrs)
                if which == 3:
                    vy = []
                    for kc in range(KC):
                        vy.append(cx.op("vector", lambda e, kc=kc, xb=xb, rstd=rstd: e.scalar_tensor_tensor(
                            xb[:, kc, :], xb[:, kc, :], pvec[:, 96 + kc:97 + kc], rstd, ALU.mult, ALU.mult), sm["hT"]))
                        if kc < KC - 1:
                            yield
                    st[i]["y_ready"] = vy
                    yield
                    return
                l = 0 if which == 1 else 1
                dstb = hT[i % 2] if which == 1 else h2buf
                if which == 2 and i >= 1:
                    cx.wait("scalar", s_out, st[i - 1]["h_store"])
                vh = []
                for kc in range(KC):
                    b = len(tA_hist) % 2
                    if len(tA_hist) >= 2:
                        cx.wait("vector", sm["hT"], tA_hist[-2])
                    vt = cx.op("vector", lambda e, kc=kc, b=b, xb=xb, rstd=rstd, l=l, s=s: e.scalar_tensor_tensor(
                        tmpA[b], xb[:, kc, :], A_(l, kc, s), rstd, ALU.mult, ALU.mult), sm["tA"])
                    cx.wait("scalar", sm["tA"], vt)
                    va = cx.op("scalar", lambda e, kc=kc, b=b, dstb=dstb, l=l, s=s: e.activation(
                        out=dstb[:, kc, :], in_=tmpA[b], func=AF.Identity, bias=B_(l, kc, s), scale=1.0), sm["hT"])
                    tA_hist.append(va)
                    vh.append(va)
                    if kc < KC - 1:
                        yield
                if which == 1:
                    st[i]["h_ready"] = vh
                else:
                    cx.wait("gpsimd", sm["hT"], vh[-1])
                    st[i]["h_store"] = cx.dma("gpsimd", h2T_d[:, :, g0:g0 + TF].rearrange("k p n -> p k n"), h2buf, s_out)
                yield

            def out_transposes(i):
                xb = xT[i % 2]
                if i >= 1:
                    cx.wait("vector", s_out, st[i - 1]["y_store"])
                for tb in range(TBF):
                    for half in range(2):
                        bank, bi = ring.acquire()
                        for k4 in range(4):
                            kc = half * 4 + k4
                            if tb == 0:
                                cx.wait("tensor", sm["hT"], st[i]["y_ready"][kc])
                            fn = lambda e, bank=bank, k4=k4, kc=kc, tb=tb, xb=xb: e.transpose(
                                bank[:, k4 * 128:(k4 + 1) * 128], xb[:, kc, tb * 128:(tb + 1) * 128], ident)
                            if k4 == 3:
                                vtr = cx.op("tensor", fn, sm["yt"])
                            else:
                                cx.op("tensor", fn)
                        cx.wait("vector", sm["yt"], vtr)
                        vye = cx.op("vector", lambda e, bank=bank, tb=tb, half=half: e.tensor_copy(
                            tok[:, tb, half * 512:(half + 1) * 512], bank[:, :]), sm["ye"])
                        ring.release(bi, sm["ye"], vye)
                        if not (tb == TBF - 1 and half == 1):
                            yield
                st[i]["ytr_done"] = vtr
                cx.wait("gpsimd", sm["ye"], vye)
                r0 = i * TF
                st[i]["y_store"] = cx.dma("gpsimd", y_d[r0:r0 + TF, :].rearrange("(tb p) d -> p tb d", p=128), tok, s_out)
                yield

            def gate_up(i, hooks):
                hb = hT[i % 2]
                if i == 0:
                    cx.wait("tensor", s_wgu, s_wgu.n)
                if mode == 1:
                    cx.wait("tensor", sm["xin"], st[i]["xin"])
                for j in range(FC):
                    gi = len(gu_hist["sg"])
                    pg = PS[3 + gi % 2]
                    pu = PS[5 + gi % 2]
                    if gi >= 2:
                        cx.wait("tensor", sm["sg"], gu_hist["sg"][gi - 2])
                    for kc in range(KC):
                        if j == 0 and mode == 0:
                            cx.wait("tensor", sm["hT"], st[i]["h_ready"][kc])
                        fn = lambda e, pg=pg, kc=kc, j=j, hb=hb: e.matmul(
                            pg[:, 0:TF], Wg[:, kc, j * 128:(j + 1) * 128], hb[:, kc, :], start=(kc == 0), stop=(kc == KC - 1))
                        if kc == KC - 1:
                            vg = cx.op("tensor", fn, sm["g"])
                        else:
                            cx.op("tensor", fn)
                    if gi >= 2:
                        cx.wait("tensor", sm["hid"], gu_hist["hid"][gi - 2])
                    for kc in range(KC):
                        fn = lambda e, pu=pu, kc=kc, j=j, hb=hb: e.matmul(
                            pu[:, 0:TF], Wu[:, kc, j * 128:(j + 1) * 128], hb[:, kc, :], start=(kc == 0), stop=(kc == KC - 1))
                        if kc == KC - 1:
                            vu = cx.op("tensor", fn, sm["u"])
                        else:
                            cx.op("tensor", fn)
                    b = gi % 2
                    cx.wait("scalar", sm["g"], vg)
                    if gi >= 2:
                        cx.wait("scalar", sm["hid"], gu_hist["hid"][gi - 2])
                    vsg = cx.op("scalar", lambda e, pg=pg, b=b: e.activation(out=sg[b], in_=pg[:, 0:TF], func=AF.Silu), sm["sg"])
                    gu_hist["sg"].append(vsg)
                    cx.wait("vector", sm["sg"], vsg)
                    cx.wait("vector", sm["u"], vu)
                    vhid = cx.op("vector", lambda e, pu=pu, b=b, j=j: e.tensor_tensor(hid[:, j, :], pu[:, 0:TF], sg[b], ALU.mult), sm["hid"])
                    gu_hist["hid"].append(vhid)
                    if hooks is not None:
                        hooks(j)
                st[i]["gu_done"] = vu

            def down(i, hooks=None):
                s, g0 = tiles[i]
                xb = xT[i % 2]
                if i == 0:
                    cx.wait("tensor", s_wd, s_wd.n)
                if mode == 1:
                    cx.wait("vector", sm["xin"], st[i]["xin"])
                sq = []
                for dc in range(KC):
                    bank, bi = ring.acquire()
                    for j in range(FC):
                        if dc == 0:
                            cx.wait("tensor", sm["hid"], gu_hist["hid"][len(gu_hist["hid"]) - FC + j])
                        fn = lambda e, bank=bank, j=j, dc=dc: e.matmul(
                            bank[:, 0:TF], Wd[:, j, dc * 128:(dc + 1) * 128], hid[:, j, :], start=(j == 0), stop=(j == FC - 1))
                        if j == FC - 1:
                            vdn = cx.op("tensor", fn, sm["dn"])
                        else:
                            cx.op("tensor", fn)
                    cx.wait("vector", sm["dn"], vdn)
                    vres = cx.op("vector", lambda e, bank=bank, dc=dc, s=s, xb=xb: e.scalar_tensor_tensor(
                        xb[:, dc, :], bank[:, 0:TF], G_(l_in, dc, s), xb[:, dc, :], ALU.mult, ALU.add), sm["res"])
                    ring.release(bi, sm["res"], vres)
                    cx.wait("scalar", sm["res"], vres)
                    sq.append(cx.op("scalar", lambda e, dc=dc, xb=xb: e.activation(
                        out=xsqB[:, dc, :], in_=xb[:, dc, :], func=AF.Square), sm["cb"]))
                    if hooks is not None:
                        hooks(dc)
                st[i]["sqB"] = sq
                if mode == 0:
                    cx.wait("gpsimd", sm["res"], vres)
                    st[i]["x_store"] = cx.dma("gpsimd", x1T_d[:, :, g0:g0 + TF].rearrange("k p n -> p k n"), xb, s_out)

            def drain(g):
                for _ in g:
                    pass

            def gen_load(i):
                load_in(i)
                yield

            class Stepper:
                def __init__(self, gens):
                    self.gens = list(gens)

                def step(self):
                    while self.gens:
                        try:
                            next(self.gens[0])
                            return True
                        except StopIteration:
                            self.gens.pop(0)
                    return False

            load_in(0)
            if mode == 0:
                drain(transposes_in(0))
                drain(stats_norm(0, 1))
                if n > 1:
                    load_in(1)
            for i in range(n):
                gens = []
                if i >= 1:
                    gens.append(stats_norm(i - 1, 2 if mode == 0 else 3))
                    if mode == 1:
                        gens.append(out_transposes(i - 1))
                if mode == 1 and i + 1 < n:
                    gens.append(gen_load(i + 1))
                if mode == 0 and i + 1 < n:
                    gens.append(transposes_in(i + 1))
                    if i + 2 < n:
                        gens.append(gen_load(i + 2))
                    gens.append(stats_norm(i + 1, 1))
                stp = Stepper(gens)
                gate_up(i, lambda j: (stp.step() if j >= 1 else None))
                down(i, lambda dc: stp.step())
                while stp.step():
                    pass
            drain(stats_norm(n - 1, 2 if mode == 0 else 3))
            if mode == 1:
                drain(out_transposes(n - 1))
            barrier()

        class Ring:
            def __init__(self, banks):
                self.banks = banks
                self.hist = []

            def acquire(self):
                i = len(self.hist)
                n = len(self.banks)
                if i >= n:
                    for sem, val in self.hist[i - n]:
                        cx.wait("tensor", sem, val)
                self.hist.append([])
                return self.banks[i % n], i

            def release(self, i, sem, val):
                self.hist[i].append((sem, val))

        def mix_in_phase():
            cx.top = persist_top
            s_w = cx.sem("m_w")
            s_in = cx.sem("m_in")
            s_pe = cx.sem("m_pe")
            s_a = cx.sem("m_a")
            s_v = cx.sem("m_v")
            Win = v3(cx.alloc(KC * 2560, BF16), KC)
            wsT = v3(cx.alloc(512, BF16), 4)
            bs_row = cx.alloc(512, BF16)
            row32 = cx.alloc(512)
            ones32 = cx.alloc(256)
            sgn_bc = cx.alloc(512)
            eps_t = cx.alloc(1)
            h2 = [v3(cx.alloc(KC * T, BF16), KC), v3(cx.alloc(KC * T, BF16), KC)]
            qk_sb = v3(cx.alloc(8 * T, BF16), 8)
            V_sb = cx.alloc(TB * 4 * 192, BF16).rearrange("p (t h c) -> p t h c", t=TB, h=4)
            vn = v3(cx.alloc(TB * 512, BF16), TB)
            gv = [cx.alloc(512), cx.alloc(512)]
            sqv = cx.alloc(512)
            uT = v3(cx.alloc(4 * T), 4)
            gT_sb = v3(cx.alloc(4 * T, BF16), 4)
            ss = cx.alloc(8)
            sq1 = cx.alloc(8)
            rs = cx.alloc(8)

            winv = win_d.rearrange("(kc p) f -> p kc f", p=128)
            for kc in range(KC):
                cx.dma("gpsimd", Win[:, kc, :], winv[:, kc, :], s_w)
            cx.dma("gpsimd", wsT, wsT_d.rearrange("p (g t) -> p g t", g=4), s_w)
            cx.dma("gpsimd", bs_row[0:1, :], sgub_d, s_w)
            cx.dma("gpsimd", row32[0:1, :], sgun_d, s_w)
            cx.wait("vector", s_w, s_w.n)
            cx.wait("tensor", s_w, s_w.n)
            cx.op("vector", lambda e: e.memset(ones32, 1.0))
            cx.op("vector", lambda e: e.memset(eps_t, EPS))
            vo = cx.op("vector", lambda e: e.memset(sqv, 0.0), s_v)
            cx.wait("tensor", s_v, vo)
            vb = cx.op("tensor", lambda e: e.matmul(PS[7][:, :], ones32[0:1, 0:128], row32[0:1, :], start=True, stop=True), s_pe)
            cx.wait("vector", s_pe, vb)
            vo = cx.op("vector", lambda e: e.tensor_copy(sgn_bc, PS[7][:, :]), s_v)

            ring = Ring([PS[i] for i in range(6)])
            tiles = [(s, ti) for s in range(NSEQ) for ti in range(S // T)]
            if debug and "ntiles" in debug:
                tiles = tiles[:debug["ntiles"]]
            pe_tile_end = []
            last_stores = 0
            for it, (s, ti) in enumerate(tiles):
                g0 = s * S + ti * T
                halo = (s == 2 and not (2 <= ti < 6))
                hb_ = h2[it % 2]
                if it >= 2:
                    cx.wait("sync", s_pe, pe_tile_end[it - 2])
                vin = cx.dma("sync", hb_, h2T_d[:, :, g0:g0 + T].rearrange("k p n -> p k n"), s_in)
                cx.wait("tensor", s_in, vin)
                cx.wait("scalar", s_out, last_stores)
                cx.wait("vector", s_out, last_stores)
                va_last = 0
                for fcn in (range(4, 8) if halo else range(8)):
                    bank, bi = ring.acquire()
                    for kc in range(KC):
                        fn = lambda e, bank=bank, kc=kc, fcn=fcn, hb_=hb_: e.matmul(
                            bank[:, :], Win[:, kc, fcn * 128:(fcn + 1) * 128], hb_[:, kc, :], start=(kc == 0), stop=(kc == KC - 1))
                        if kc == KC - 1:
                            vp = cx.op("tensor", fn, s_pe)
                        else:
                            cx.op("tensor", fn)
                    cx.wait("scalar", s_pe, vp)
                    va_last = cx.op("scalar", lambda e, bank=bank, fcn=fcn: e.activation(
                        out=qk_sb[:, fcn, :], in_=bank[:, :], func=AF.Copy, scale=(0.125 if fcn < 4 else 1.0)), s_a)
                    ring.release(bi, s_a, va_last)
                cx.wait("gpsimd", s_a, va_last)
                if not halo:
                    cx.dma("gpsimd", qT_d[:, :, g0:g0 + T].rearrange("k p n -> p k n"), qk_sb[:, 0:4, :], s_out)
                cx.dma("gpsimd", kT_d[:, :, g0:g0 + T].rearrange("k p n -> p k n"), qk_sb[:, 4:8, :], s_out)
                for tb in range(TB):
                    blk = g0 // 128 + tb
                    bank, bi = ring.acquire()
                    for kc in range(KC):
                        fn = lambda e, bank=bank, kc=kc, tb=tb, hb_=hb_: e.matmul(
                            bank[:, :], hb_[:, kc, tb * 128:(tb + 1) * 128], Win[:, kc, 1024:1536], start=(kc == 0), stop=(kc == KC - 1))
                        if kc == KC - 1:
                            vp = cx.op("tensor", fn, s_pe)
                        else:
                            cx.op("tensor", fn)
                    cx.wait("vector", s_pe, vp)
                    bv = bank[:, :].rearrange("p (h c) -> p h c", h=4)
                    cx.op("vector", lambda e, bv=bv, tb=tb, blk=blk: e.tensor_scalar(
                        V_sb[:, tb, :, 0:64], bv[:, :, 0:64], valid[:, blk:blk + 1], None, ALU.mult))
                    cx.op("vector", lambda e, bv=bv, tb=tb, blk=blk: e.tensor_scalar(
                        V_sb[:, tb, :, 128:192], bv[:, :, 64:128], valid[:, blk:blk + 1], None, ALU.mult))
                    vv = cx.op("vector", lambda e, tb=tb, blk=blk: e.tensor_scalar(
                        V_sb[:, tb, :, 64:128], ones32.rearrange("p (h c) -> p h c", h=4), valid[:, blk:blk + 1], None, ALU.mult), s_v)
                    ring.release(bi, s_v, vv)
                    cx.wait("gpsimd", s_v, vv)
                    cx.dma("gpsimd", V_d[:, g0 + tb * 128:g0 + (tb + 1) * 128, :].rearrange("h p c -> p h c"), V_sb[:, tb], s_out)
                if not halo:
                    vvn = []
                    for tb in range(TB):
                        b = tb % 2
                        bank, bi = ring.acquire()
                        for kc in range(KC):
                            fn = lambda e, bank=bank, kc=kc, tb=tb, hb_=hb_: e.matmul(
                                bank[:, :], hb_[:, kc, tb * 128:(tb + 1) * 128], Win[:, kc, 2048:2560], start=(kc == 0), stop=(kc == KC - 1))
                            if kc == KC - 1:
                                vp = cx.op("tensor", fn, s_pe)
                            else:
                                cx.op("tensor", fn)
                        cx.wait("scalar", s_pe, vp)
                        if tb >= 2:
                            cx.wait("scalar", s_v, vvn[tb - 2])
                        va = cx.op("scalar", lambda e, bank=bank, b=b: e.activation(out=gv[b], in_=bank[:, :], func=AF.Gelu_apprx_tanh), s_a)
                        ring.release(bi, s_a, va)
                        cx.wait("vector", s_a, va)
                        v1 = cx.op("vector", lambda e, b=b: e.tensor_tensor(sqv, gv[b], gv[b], ALU.mult), s_v)
                        cx.wait("vector", s_v, v1)
                        v2 = cx.op("vector", lambda e, tb=tb: e.reduce_sum(ss[:, tb:tb + 1], sqv, mybir.AxisListType.X), s_v)
                        cx.wait("scalar", s_v, v2)
                        va2 = cx.op("scalar", lambda e, tb=tb: e.activation(
                            out=sq1[:, tb:tb + 1], in_=ss[:, tb:tb + 1], func=AF.Sqrt, bias=eps_t[:, 0:1], scale=1.0 / 512), s_a)
                        cx.wait("vector", s_a, va2)
                        v3_ = cx.op("vector", lambda e, tb=tb: e.reciprocal(rs[:, tb:tb + 1], sq1[:, tb:tb + 1]), s_v)
                        cx.wait("vector", s_v, v3_)
                        v4 = cx.op("vector", lambda e, tb=tb, b=b: e.scalar_tensor_tensor(
                            vn[:, tb, :], gv[b], rs[:, tb:tb + 1], sgn_bc, ALU.mult, ALU.mult), s_v)
                        vvn.append(v4)
                    vu = []
                    for g in range(4):
                        bank, bi = ring.acquire()
                        for kc in range(KC):
                            fn = lambda e, bank=bank, kc=kc, g=g, hb_=hb_: e.matmul(
                                bank[:, :], Win[:, kc, 1536 + g * 128:1536 + (g + 1) * 128], hb_[:, kc, :], start=(kc == 0), stop=(kc == KC - 1))
                            if kc == KC - 1:
                                vp = cx.op("tensor", fn, s_pe)
                            else:
                                cx.op("tensor", fn)
                        cx.wait("scalar", s_pe, vp)
                        va = cx.op("scalar", lambda e, bank=bank, g=g: e.activation(out=uT[:, g, :], in_=bank[:, :], func=AF.Gelu_apprx_tanh), s_a)
                        ring.release(bi, s_a, va)
                        vu.append(va)
                    for g in range(4):
                        bank, bi = ring.acquire()
                        for tb in range(TB):
                            if g == 0:
                                cx.wait("tensor", s_v, vvn[tb])
                            cx.op("tensor", lambda e, bank=bank, g=g, tb=tb: e.matmul(
                                bank[:, tb * 128:(tb + 1) * 128], vn[:, tb, g * 128:(g + 1) * 128], wsT[:, g, :], start=True, stop=False))
                            fn = lambda e, bank=bank, g=g, tb=tb: e.matmul(
                                bank[:, tb * 128:(tb + 1) * 128], ones_bf[0:1, 0:128], bs_row[0:1, g * 128:(g + 1) * 128], start=False, stop=True)
                            if tb == TB - 1:
                                vp = cx.op("tensor", fn, s_pe)
                            else:
                                cx.op("tensor", fn)
                        cx.wait("vector", s_pe, vp)
                        cx.wait("vector", s_a, vu[g])
                        vg = cx.op("vector", lambda e, bank=bank, g=g: e.tensor_tensor(gT_sb[:, g, :], bank[:, :], uT[:, g, :], ALU.mult), s_v)
                        ring.release(bi, s_v, vg)
                    cx.wait("gpsimd", s_v, vg)
                    cx.dma("gpsimd", gT_d[:, :, g0:g0 + T].rearrange("k p n -> p k n"), gT_sb, s_out)
                pe_tile_end.append(s_pe.n)
                last_stores = s_out.n
            barrier()

        def segments(Q0, n, L):
            nb = L // 128
            units = list(range(Q0 // 64, (Q0 + n) // 64))
            segs = []
            i = 0
            while i < len(units):
                u = units[i]
                if u % 2 == 1 and i + 1 < len(units) and (u + 1) * 64 < L:
                    P, nq = u * 64, 128
                    i += 2
                else:
                    P, nq = u * 64, 64
                    i += 1
                kbs = []
                mc_ = None
                for kb in range(max(0, (P - 64) // 128), min(nb - 1, (P + nq + 63) // 128) + 1):
                    v = P - 64 - 128 * kb
                    if v not in (-128, -64, 0, 64):
                        continue
                    hh = 1 if v < 0 else 0
                    mc = v + 128 * hh
                    if nq == 128 and mc != 0:
                        continue
                    assert mc_ is None or mc_ == mc
                    mc_ = mc
                    kbs.append((kb, hh))
                segs.append((P, nq, kbs, mc_))
            return segs

        def attn_phase():
            cx.top = persist_top
            s_ld = cx.sem("a_ld")
            s_vl = cx.sem("a_vl")
            s_ps = cx.sem("a_ps")
            s_ex = cx.sem("a_ex")
            s_mk = cx.sem("a_mk")
            s_po = cx.sem("a_po")
            s_ac = cx.sem("a_ac")
            s_nm = cx.sem("a_nm")
            qT = v3(cx.alloc(4 * S, BF16), 4)
            kT = v3(cx.alloc(4 * S, BF16), 4)
            Vt = [v3(cx.alloc(48 * 192, BF16), 48), v3(cx.alloc(48 * 192, BF16), 48)]
            acc = [cx.alloc(2048), cx.alloc(2048)]
            rD = cx.alloc(2048)
            attnT = [cx.alloc(2048, BF16), cx.alloc(2048, BF16)]
            NPB = 8
            pT = [cx.alloc(512, BF16) for _ in range(NPB)]
            pTm = [cx.alloc(512, BF16) for _ in range(NPB)]
            ringS = Ring([PS[0], PS[1], PS[2], PS[3], PS[4]])
            ringO = Ring([PS[5], PS[6], PS[7]])
            LOOK = 4
            MASK_ENGS = ("gpsimd",)
            pending_store = [None]

            def flush_store():
                hp_, s_, q0_, ab_, v4_ = pending_store[0]
                cx.wait("sync", s_nm, v4_)
                at_store[ab_] = cx.dma("sync", aT_d[hp_, :, s_ * S + q0_:s_ * S + q0_ + 2048], attnT[ab_], s_out)
                pending_store[0] = None

            vt_ready = [0, 0]
            seg_i = 0
            ex_hist = []
            mk_hist = []
            po_hist = []
            vl_i = 0
            vt_free = [0, 0]
            at_i = 0
            at_store = [0, 0]
            pe_done_seq = 0
            sts = [(0, 0), (0, 2048), (1, 0), (1, 2048), (2, 1024)]
            if debug and "nst" in debug:
                sts = sts[:debug["nst"]]
            cur_seq = -1
            for (s, q0) in sts:
                if s != cur_seq:
                    cur_seq = s
                    cx.wait("sync", s_po, pe_done_seq)
                    cx.wait("sync", s_ps, s_ps.n)
                    cx.dma("sync", qT, qT_d[:, :, s * S:(s + 1) * S].rearrange("k p n -> p k n"), s_ld)
                    vq = cx.dma("sync", kT, kT_d[:, :, s * S:(s + 1) * S].rearrange("k p n -> p k n"), s_ld)
                    cx.wait("tensor", s_ld, vq)
                for hp in range(4):
                    items = []
                    for pi, (_, dil) in enumerate(PATTERNS):
                        L = S // dil
                        Q0, n = q0 // dil, 2048 // dil
                        kb_lo = max(0, (Q0 - 64) // 128)
                        kb_hi = min(L // 128 - 1, (Q0 + n + 63) // 128)
                        nkb = kb_hi - kb_lo + 1
                        vb = vl_i % 2
                        vl_i += 1
                        first = True
                        segs = segments(Q0, n, L)
                        fulls = [sg_ for sg_ in segs if sg_[1] == 128]
                        halves = [sg_ for sg_ in segs if sg_[1] == 64]
                        groups = []
                        for r in range(dil):
                            k = 0
                            while k < len(fulls):
                                if k + 1 < len(fulls) and fulls[k + 1][0] == fulls[k][0] + 128:
                                    groups.append(("row", [(r, fulls[k]), (r, fulls[k + 1])]))
                                    k += 2
                                else:
                                    groups.append(("row", [(r, fulls[k])]))
                                    k += 1
                        for hseg in halves:
                            nk_ = len(hseg[2])
                            m_ = min(dil, 4 if nk_ == 2 else 8)
                            for r0 in range(0, dil, m_):
                                groups.append(("col", [(r, hseg) for r in range(r0, min(r0 + m_, dil))]))
                        for hd in range(2):
                            for (kind, subs) in groups:
                                items.append(dict(pi=pi, dil=dil, Q0=Q0, kb_lo=kb_lo, nkb=nkb, vb=vb, hd=hd,
                                                  kind=kind, subs=subs, vload=first, s=s, hp=hp))
                                first = False

                    def emit_front(it):
                        nonlocal seg_i
                        dil, hd, subs = it["dil"], it["hd"], it["subs"]
                        if it["vload"]:
                            vb, nkb, kb_lo = it["vb"], it["nkb"], it["kb_lo"]
                            cx.wait("sync", s_po, vt_free[vb])
                            for rr in range(dil):
                                t0 = it["s"] * S + rr + dil * 128 * kb_lo
                                src_ = V_d[it["hp"], t0:t0 + dil * (128 * nkb - 1) + 1:dil, :].rearrange("(kb p) c -> p kb c", p=128)
                                vvl = cx.dma("sync", Vt[vb][:, rr * nkb:(rr + 1) * nkb, :], src_, s_vl)
                            vt_ready[vb] = vvl
                        it["vready"] = vt_ready[it["vb"]]
                        h = it["hp"] * 2 + hd
                        hb = 64 * hd
                        midx = h * 3 + it["pi"]
                        (_, (P_, nq, kbs0, mc)) = subs[0]
                        nk = len(kbs0)
                        m = len(subs)
                        it["seg"] = seg_i
                        pb = seg_i % NPB
                        it["pb"] = pb
                        bankS, bsi = ringS.acquire()
                        off = 0
                        for si, (r, (P, nq_, kbs, mc_)) in enumerate(subs):
                            assert nq_ == nq and len(kbs) == nk and mc_ == mc and [x[1] for x in kbs] == [x[1] for x in kbs0]
                            qcols = slice(r + dil * P, r + dil * (P + nq - 1) + 1, dil)
                            for i, (kb, hh) in enumerate(kbs):
                                kcols = slice(r + dil * 128 * kb, r + dil * (128 * kb + 127) + 1, dil)
                                fn = lambda e, bankS=bankS, off=off, nq=nq, kcols=kcols, qcols=qcols, hb=hb, hp=it["hp"]: e.matmul(
                                    bankS[:, off:off + nq], kT[hb:hb + 64, hp, kcols], qT[hb:hb + 64, hp, qcols], start=True, stop=True)
                                off += nq
                                if si == m - 1 and i == nk - 1:
                                    vps = cx.op("tensor", fn, s_ps)
                                else:
                                    cx.op("tensor", fn)
                        W = off
                        cx.wait("scalar", s_ps, vps)
                        if seg_i >= NPB:
                            cx.wait("scalar", s_mk, mk_hist[seg_i - NPB])
                        vex = cx.op("scalar", lambda e, bankS=bankS, pb=pb, W=W: e.activation(
                            out=pT[pb][:, 0:W], in_=bankS[:, 0:W], func=AF.Exp), s_ex)
                        ringS.release(bsi, s_ex, vex)
                        meng = MASK_ENGS[seg_i % len(MASK_ENGS)]
                        cx.wait(meng, s_ex, vex)
                        if seg_i >= NPB:
                            cx.wait(meng, s_po, po_hist[seg_i - NPB])
                        if nk == 2:
                            m_ap = Emask3[:, midx, :].rearrange("p (h q) -> p h q", h=2)[:, :, mc:mc + nq]
                            if m > 1:
                                m_ap = m_ap.unsqueeze(1).broadcast_to([128, m, 2, nq])
                                o_ap = pTm[pb][:, 0:W].rearrange("p (m h q) -> p m h q", m=m, h=2)
                                i_ap = pT[pb][:, 0:W].rearrange("p (m h q) -> p m h q", m=m, h=2)
                            else:
                                o_ap = pTm[pb][:, 0:W].rearrange("p (h q) -> p h q", h=2)
                                i_ap = pT[pb][:, 0:W].rearrange("p (h q) -> p h q", h=2)
                        else:
                            hh = kbs0[0][1]
                            m_ap = Emask3[:, midx, hh * 128 + mc:hh * 128 + mc + nq]
                            if m > 1:
                                m_ap = m_ap.unsqueeze(1).broadcast_to([128, m, nq])
                                o_ap = pTm[pb][:, 0:W].rearrange("p (m q) -> p m q", m=m)
                                i_ap = pT[pb][:, 0:W].rearrange("p (m q) -> p m q", m=m)
                            else:
                                o_ap = pTm[pb][:, 0:W]
                                i_ap = pT[pb][:, 0:W]
                        vmk = cx.op(meng, lambda e, o_ap=o_ap, i_ap=i_ap, m_ap=m_ap: e.tensor_tensor(o_ap, i_ap, m_ap, ALU.mult), s_mk, k=1)
                        mk_hist.append(vmk)
                        it["vmk"] = vmk
                        seg_i += 1

                    def emit_back(it):
                        dil, hd, vb, subs = it["dil"], it["hd"], it["vb"], it["subs"]
                        (_, (P0, nq, kbs0, mc)) = subs[0]
                        nk = len(kbs0)
                        m = len(subs)
                        pb = it["pb"]
                        vcols = slice(0, 128) if hd == 0 else slice(64, 192)
                        bankO, boi = ringO.acquire()
                        cx.wait("tensor", s_vl, it["vready"])
                        cx.wait("tensor", s_mk, it["vmk"])
                        for si, (r, (P, nq_, kbs, mc_)) in enumerate(subs):
                            for i, (kb, hh) in enumerate(kbs):
                                blk = r * it["nkb"] + (kb - it["kb_lo"])
                                fn = lambda e, bankO=bankO, si=si, i=i, nq=nq, blk=blk, vb=vb, vcols=vcols, pb=pb, nk=nk: e.matmul(
                                    bankO[:, si * nq:(si + 1) * nq], Vt[vb][:, blk, vcols],
                                    pTm[pb][:, (si * nk + i) * nq:(si * nk + i + 1) * nq], start=(i == 0), stop=(i == nk - 1))
                                if si == m - 1 and i == nk - 1:
                                    vpo = cx.op("tensor", fn, s_po)
                                else:
                                    cx.op("tensor", fn)
                        po_hist.append(vpo)
                        vt_free[vb] = vpo
                        if it["kind"] == "row":
                            r = subs[0][0]
                            c0 = r + dil * (P0 - it["Q0"])
                            dst = acc[hd][:, c0:c0 + dil * (m * nq - 1) + 1:dil]
                            srcO = bankO[:, 0:m * nq]
                        else:
                            r0 = subs[0][0]
                            cb = dil * (P0 - it["Q0"])
                            dst = acc[hd][:, cb:cb + dil * nq].rearrange("p (q r) -> p r q", r=dil)[:, r0:r0 + m, :]
                            srcO = bankO[:, 0:m * nq].rearrange("p (m q) -> p m q", m=m)
                        cx.wait("vector", s_po, vpo)
                        if it["pi"] == 0:
                            vac = cx.op("vector", lambda e, dst=dst, srcO=srcO: e.tensor_copy(dst, srcO), s_ac)
                        else:
                            vac = cx.op("vector", lambda e, dst=dst, srcO=srcO: e.tensor_tensor(dst, srcO, dst, ALU.add), s_ac)
                        ringO.release(boi, s_ac, vac)

                    for idx in range(len(items) + LOOK):
                        if idx < len(items):
                            emit_front(items[idx])
                        if idx - LOOK >= 0:
                            emit_back(items[idx - LOOK])
                    ab = at_i % 2
                    at_i += 1
                    cx.wait("vector", s_ac, s_ac.n)
                    cx.wait("vector", s_out, at_store[ab])
                    v1 = cx.op("vector", lambda e: e.tensor_copy(rD[0:64, :], acc[0][64:128, :]), s_nm)
                    v2 = cx.op("vector", lambda e: e.tensor_copy(rD[64:128, :], acc[1][0:64, :]), s_nm)
                    cx.wait("vector", s_nm, v2)
                    v2b = cx.op("vector", lambda e: e.reciprocal(rD, rD), s_nm)
                    cx.wait("vector", s_nm, v2b)
                    v3_ = cx.op("vector", lambda e, ab=ab: e.tensor_tensor(attnT[ab][0:64, :], acc[0][0:64, :], rD[0:64, :], ALU.mult), s_nm)
                    v4 = cx.op("vector", lambda e, ab=ab: e.tensor_tensor(attnT[ab][64:128, :], acc[1][64:128, :], rD[64:128, :], ALU.mult), s_nm)
                    if pending_store[0] is not None:
                        flush_store()
                    pending_store[0] = (hp, s, q0, ab, v4)
                if pending_store[0] is not None:
                    flush_store()
                pe_done_seq = s_po.n
            barrier()

        def mix_out_phase():
            cx.top = W_BASE + 2 * WCOLS
            s_w = cx.sem("o_w")
            s_in = cx.sem("o_in")
            s_pe = cx.sem("o_pe")
            s_res = cx.sem("o_res")
            s_ca = cx.sem("o_ca")
            s_st = cx.sem("o_st")
            s_sq = cx.sem("o_sq")
            s_rs = cx.sem("o_rs")
            s_tA = cx.sem("o_tA")
            s_hT = cx.sem("o_hT")
            Wout = v3(cx.alloc(KC * D, BF16), KC)
            eps_t = cx.alloc(1)
            xT = v3(cx.alloc(KC * T), KC)
            mixT = v3(cx.alloc(KC * T, BF16), KC)
            hT = v3(cx.alloc(KC * T, BF16), KC)
            xsq = v3(cx.alloc(KC * T, BF16), KC)
            rstd = cx.alloc(T)
            sqt = cx.alloc(T)
            tmpA = [cx.alloc(T), cx.alloc(T)]
            woutv = wout_d.rearrange("(kc p) f -> p kc f", p=128)
            for kc in range(KC):
                cx.dma("gpsimd", Wout[:, kc, :], woutv[:, kc, :], s_w)
            cx.wait("tensor", s_w, s_w.n)
            ffn_issue_weights(1, ("gu",))
            cx.op("vector", lambda e: e.memset(eps_t, EPS))
            ring = Ring([PS[0], PS[1], PS[2]])
            tiles = [(s, ti) for s in range(NSEQ) for ti in range(S // T) if s < 2 or 2 <= ti < 6]
            if debug and "ntiles" in debug:
                tiles = tiles[:debug["ntiles"]]
            hT_hist = []
            last_x = 0
            last_h = 0
            for (s, ti) in tiles:
                g0 = s * S + ti * T
                cx.wait("sync", s_out, max(last_x, last_h))
                cx.wait("sync", s_pe, s_pe.n)
                cx.wait("sync", s_hT, s_hT.n)
                cx.dma("sync", xT, x1T_d[:, :, g0:g0 + T].rearrange("k p n -> p k n"), s_in)
                cx.dma("sync", mixT[:, 0:4, :], aT_d[:, :, g0:g0 + T].rearrange("k p n -> p k n"), s_in)
                vin = cx.dma("sync", mixT[:, 4:8, :], gT_d[:, :, g0:g0 + T].rearrange("k p n -> p k n"), s_in)
                cx.wait("tensor", s_in, vin)
                cx.wait("vector", s_in, vin)
                res_vals = []
                for dc in range(KC):
                    bank, bi = ring.acquire()
                    for kc in range(KC):
                        fn = lambda e, bank=bank, kc=kc, dc=dc: e.matmul(
                            bank[:, :], Wout[:, kc, dc * 128:(dc + 1) * 128], mixT[:, kc, :], start=(kc == 0), stop=(kc == KC - 1))
                        if kc == KC - 1:
                            vp = cx.op("tensor", fn, s_pe)
                        else:
                            cx.op("tensor", fn)
                    cx.wait("vector", s_pe, vp)
                    vres = cx.op("vector", lambda e, bank=bank, dc=dc, s=s: e.scalar_tensor_tensor(
                        xT[:, dc, :], bank[:, :], G_(1, dc, s), xT[:, dc, :], ALU.mult, ALU.add), s_res)
                    ring.release(bi, s_res, vres)
                    res_vals.append(vres)
                ca_vals = []
                for kc in range(KC):
                    cx.wait("scalar", s_res, res_vals[kc])
                    ca_vals.append(cx.op("scalar", lambda e, kc=kc: e.activation(out=xsq[:, kc, :], in_=xT[:, kc, :], func=AF.Square), s_ca))
                cx.wait("scalar", s_res, res_vals[-1])
                last_x = cx.dma("scalar", x2T_d[:, :, g0:g0 + T].rearrange("k p n -> p k n"), xT, s_out)
                cx.wait("tensor", s_sq, s_sq.n)
                for kc in range(KC):
                    cx.wait("tensor", s_ca, ca_vals[kc])
                    fn = lambda e, kc=kc: e.matmul(PS[3][:, :], ones_bf, xsq[:, kc, :], start=(kc == 0), stop=(kc == KC - 1))
                    if kc == KC - 1:
                        vst = cx.op("tensor", fn, s_st)
                    else:
                        cx.op("tensor", fn)
                cx.wait("scalar", s_st, vst)
                vsq = cx.op("scalar", lambda e: e.activation(out=sqt, in_=PS[3][:, :], func=AF.Sqrt, bias=eps_t[:, 0:1], scale=1.0 / D), s_sq)
                cx.wait("vector", s_sq, vsq)
                vrs = cx.op("vector", lambda e: e.reciprocal(rstd, sqt), s_rs)
                cx.wait("vector", s_rs, vrs)
                cx.wait("scalar", s_out, last_h)
                for kc in range(KC):
                    b = kc % 2
                    if len(hT_hist) >= 2:
                        cx.wait("vector", s_hT, hT_hist[-2])
                    vt = cx.op("vector", lambda e, kc=kc, b=b, s=s: e.scalar_tensor_tensor(
                        tmpA[b], xT[:, kc, :], A_(2, kc, s), rstd, ALU.mult, ALU.mult), s_tA)
                    cx.wait("scalar", s_tA, vt)
                    va = cx.op("scalar", lambda e, kc=kc, b=b, s=s: e.activation(
                        out=hT[:, kc, :], in_=tmpA[b], func=AF.Identity, bias=B_(2, kc, s), scale=1.0), s_hT)
                    hT_hist.append(va)
                cx.wait("scalar", s_hT, va)
                last_h = cx.dma("scalar", h3T_d[:, :, g0:g0 + T].rearrange("k p n -> p k n"), hT, s_out)
            barrier()

        if stop_after >= 1:
            ffn_phase(0, preloaded=("gu", "d"))
        if stop_after >= 2:
            mix_in_phase()
        if stop_after >= 3:
            attn_phase()
        if stop_after >= 4:
            mix_out_phase()
        if stop_after >= 5:
            ffn_phase(1, preloaded=("gu",))

        barrier()
        cx.emit_all()
    return nc


def _core_inputs(core, inp):
    xs = inp["x_sample"]
    xp = inp["x_prompt"]
    x = np.zeros((NSEQ, S, D), np.float32)
    x[0] = xs[2 * core]
    x[1] = xs[2 * core + 1]
    pb, qd = core // 4, core % 4
    lo = qd * 2048 - 1024
    valid = np.ones((NSEQ, S), np.float32)
    a, b = max(lo, 0), min(lo + S, 8192)
    x[2, a - lo:b - lo] = xp[pb, a:b]
    valid[2, :] = 0.0
    valid[2, a - lo:b - lo] = 1.0
    c3 = np.stack([inp["c_sample"][2 * core], inp["c_sample"][2 * core + 1], inp["c_prompt"][pb]], 0)
    cT = np.ascontiguousarray(c3.reshape(3, KC, 128).transpose(2, 1, 0)).reshape(128, KC * 3)
    return x.reshape(NTOK, D), cT, np.ascontiguousarray(valid.reshape(NTOK // 128, 128).T)


def _shared_inputs(inp):
    def pcol(v):
        return np.ascontiguousarray(np.asarray(v, np.float32).reshape(-1, 128).T)
    pvec = np.concatenate([pcol(inp["ada_b"][0]), pcol(inp["ffn1_norm"][0]), pcol(inp["mix_norm"][0]),
                           pcol(inp["ffn2_norm"][0]), pcol(inp["final_norm"])], axis=1)
    kp = np.arange(128)[:, None]
    col = np.arange(256)[None, :]
    hh, q = col // 128, col % 128
    rel = 128 * hh + kp - q - 64
    sh = {
        "pvec": pvec.astype(np.float32),
        "ident": np.eye(128, dtype=np.float32),
        "absrel": np.abs(rel).astype(np.float32),
        "band": (np.abs(rel) <= 64).astype(np.float32),
        "ada_w": np.ascontiguousarray(inp["ada_w"][0]),
        "wg1": np.ascontiguousarray(inp["ffn1_w_gate"][0]), "wu1": np.ascontiguousarray(inp["ffn1_w_up"][0]),
        "wd1": np.ascontiguousarray(inp["ffn1_w_down"][0]),
        "wg2": np.ascontiguousarray(inp["ffn2_w_gate"][0]), "wu2": np.ascontiguousarray(inp["ffn2_w_up"][0]),
        "wd2": np.ascontiguousarray(inp["ffn2_w_down"][0]),
        "w_in": np.ascontiguousarray(inp["w_in"][0]),
        "wsT": np.ascontiguousarray(inp["sgu_w"][0].transpose(2, 0, 1)).reshape(128, 512),
        "sgu_b": np.ascontiguousarray(inp["sgu_b"][0]).reshape(1, 512),
        "sgu_norm": np.ascontiguousarray(inp["sgu_norm"][0]).reshape(1, 512),
        "w_out": np.ascontiguousarray(inp["w_out"][0]),
    }
    return sh


def make_in_maps(inp):
    inp = {k: np.asarray(v) for k, v in inp.items()}
    sh = _shared_inputs(inp)
    maps = []
    for core in range(N_CORES):
        x, cT, valid = _core_inputs(core, inp)
        m = dict(sh)
        m["x"] = x
        m["cT"] = cT
        m["valid"] = valid
        maps.append(m)
    return maps


def kernel(**inputs):
    maps = make_in_maps(inputs)
    nc = build_program()
    res = run_bass_kernel_spmd(nc, maps, core_ids=list(range(N_CORES)))
    ys = np.empty((16, S, D), np.float32)
    yp = np.empty((2, 8192, D), np.float32)
    for core in range(N_CORES):
        y = res.results[core]["y"]
        ys[2 * core] = y[0:S]
        ys[2 * core + 1] = y[S:2 * S]
        pb, qd = core // 4, core % 4
        yp[pb, qd * 2048:(qd + 1) * 2048] = y[2 * S:]
    return (yp, ys)
```
